# Optimizing a Trainium2 kernel written in Bass

```python
import jax
import jax.numpy as jnp
from jax import lax
import numpy as np

D_MODEL = 1024
BATCH = 8
SEQ = 4096
DEPTH = 4

GRID_W = 64
CTX_LEN = 256
N_MIXERS = 3
N_MOD = 9
D_FF = 2816
HEAD_DIM = 64
N_HEADS = D_MODEL // HEAD_DIM
N_KV_HEADS = 4
GROUP = N_HEADS // N_KV_HEADS
D_KV = N_KV_HEADS * HEAD_DIM
Q_BLOCK = 128
WINDOW = 128
ROPE_THETA = 10000.0
N_FREQ = HEAD_DIM // 4
ATTN_SCALE = HEAD_DIM ** -0.5
RWKV_HEAD = 64
RWKV_HEADS = D_MODEL // RWKV_HEAD
DECAY_LORA = 64
ICLR_LORA = 64
VALUE_LORA = 32
GATE_LORA = 160
NORM_EPS = 1e-6
GN_EPS = 64e-5
L2_EPS = 1e-12
NEG_INF = -1e30
N_RWKV = (DEPTH + 2) // N_MIXERS
N_GLOBAL = (DEPTH + 1) // N_MIXERS
N_LOCAL = DEPTH // N_MIXERS

kernel_name = 'hybrid_rwkv7_gqa_swa_macaron_dit'


def rms_norm(x, g):
    xf = x.astype(jnp.float32)
    y = xf * lax.rsqrt(jnp.mean(xf * xf, axis=-1, keepdims=True) + NORM_EPS)
    return (y * g.astype(jnp.float32)).astype(x.dtype)


def ffn_half(h, shift, scale, gate, g_pre, g_post, w1, w3, w2):
    n = rms_norm(h, g_pre) * (1 + scale) + shift
    y = (jax.nn.silu(n @ w1) * (n @ w3)) @ w2
    return h + 0.5 * gate * rms_norm(y, g_post)


def axial_rope(rows):
    row = jnp.broadcast_to(jnp.arange(rows)[:, None], (rows, GRID_W)).reshape(-1)
    col = jnp.broadcast_to(jnp.arange(GRID_W)[None, :], (rows, GRID_W)).reshape(-1)
    inv_freq = ROPE_THETA ** (-jnp.arange(N_FREQ, dtype=jnp.float32) / N_FREQ)
    ang = jnp.stack([row, col], axis=-1).astype(jnp.float32)[:, :, None] * inv_freq
    return jnp.cos(ang), jnp.sin(ang)


def apply_rope(t, cos, sin):
    shp = t.shape
    tf = t.astype(jnp.float32).reshape(shp[:-1] + (2, 2, N_FREQ))
    t1, t2 = tf[..., 0, :], tf[..., 1, :]
    bshape = (cos.shape[0],) + (1,) * (t.ndim - 3) + (2, N_FREQ)
    c, s = cos.reshape(bshape), sin.reshape(bshape)
    out = jnp.stack([t1 * c - t2 * s, t2 * c + t1 * s], axis=-2)
    return out.reshape(shp).astype(t.dtype)


def gqa_project(n, wq, wk, wv):
    bn, ln, _ = n.shape
    q = (n @ wq).reshape(bn, ln, N_KV_HEADS, GROUP, HEAD_DIM)
    k = (n @ wk).reshape(bn, ln, N_KV_HEADS, HEAD_DIM)
    v = (n @ wv).reshape(bn, ln, N_KV_HEADS, HEAD_DIM)
    return q, k, v


def softmax_attend(q, k, v):
    s = jnp.einsum('bqkgd,bskd->bkgqs', q, k, preferred_element_type=jnp.float32) * ATTN_SCALE
    p = jax.nn.softmax(s, axis=-1).astype(v.dtype)
    return jnp.einsum('bkgqs,bskd->bqkgd', p, v)


def sink_softmax(s, sink_h):
    col = jnp.broadcast_to(sink_h[None, :, :, None, None], s.shape[:-1] + (1,))
    return jax.nn.softmax(jnp.concatenate([s, col], axis=-1), axis=-1)[..., :-1]


def global_attention(n_lat, n_ctx, wq, wk, wv, wo, gq, gk, cos, sin, with_ctx_out):
    bn, ln, _ = n_lat.shape
    q_l, k_l, v_l = gqa_project(n_lat, wq, wk, wv)
    q_c, k_c, v_c = gqa_project(n_ctx, wq, wk, wv)
    q_l = apply_rope(rms_norm(q_l, gq), cos, sin)
    k_l = apply_rope(rms_norm(k_l, gk), cos, sin)
    q_c, k_c = rms_norm(q_c, gq), rms_norm(k_c, gk)
    k_all = jnp.concatenate([k_l, k_c], axis=1)
    v_all = jnp.concatenate([v_l, v_c], axis=1)
    nb = ln // Q_BLOCK
    qb = jnp.moveaxis(q_l.reshape(bn, nb, Q_BLOCK, N_KV_HEADS, GROUP, HEAD_DIM), 1, 0)
    o_l = lax.map(lambda qblk: softmax_attend(qblk, k_all, v_all), qb)
    y_lat = jnp.moveaxis(o_l, 0, 1).reshape(bn, ln, D_MODEL) @ wo
    y_ctx = None
    if with_ctx_out:
        y_ctx = softmax_attend(q_c, k_c, v_c).reshape(n_ctx.shape[0], n_ctx.shape[1], D_MODEL) @ wo
    return y_lat, y_ctx


def window_attention(n_lat, n_ctx, wq, wk, wv, wo, sink, cos, sin, with_ctx_out):
    bn, ln, _ = n_lat.shape
    q_l, k_l, v_l = gqa_project(n_lat, wq, wk, wv)
    q_c, k_c, v_c = gqa_project(n_ctx, wq, wk, wv)
    q_l, k_l = apply_rope(q_l, cos, sin), apply_rope(k_l, cos, sin)
    sink_h = sink.reshape(N_KV_HEADS, GROUP).astype(jnp.float32)
    nb = ln // Q_BLOCK

    def band(t):
        tp = jnp.pad(t.reshape(bn, nb, Q_BLOCK, N_KV_HEADS, HEAD_DIM), ((0, 0), (1, 1), (0, 0), (0, 0), (0, 0)))
        return jnp.moveaxis(jnp.concatenate([tp[:, :-2], tp[:, 1:-1], tp[:, 2:]], axis=2), 1, 0)

    qb = jnp.moveaxis(q_l.reshape(bn, nb, Q_BLOCK, N_KV_HEADS, GROUP, HEAD_DIM), 1, 0)
    kb, vb = band(k_l), band(v_l)
    blk = jnp.arange(nb)
    q_pos = blk[:, None] * Q_BLOCK + jnp.arange(Q_BLOCK)[None, :]
    k_pos = (blk[:, None] - 1) * Q_BLOCK + jnp.arange(3 * Q_BLOCK)[None, :]
    rel = k_pos[:, None, :] - q_pos[:, :, None]
    mask = (jnp.abs(rel) <= WINDOW) & (k_pos[:, None, :] >= 0) & (k_pos[:, None, :] < ln)

    def attend(args):
        qblk, kblk, vblk, m = args
        s_loc = jnp.einsum('bqkgd,bskd->bkgqs', qblk, kblk, preferred_element_type=jnp.float32) * ATTN_SCALE
        s_loc = jnp.where(m, s_loc, NEG_INF)
        s_ctx = jnp.einsum('bqkgd,bskd->bkgqs', qblk, k_c, preferred_element_type=jnp.float32) * ATTN_SCALE
        p = sink_softmax(jnp.concatenate([s_loc, s_ctx], axis=-1), sink_h).astype(vblk.dtype)
        n_loc = kblk.shape[1]
        return (jnp.einsum('bkgqs,bskd->bqkgd', p[..., :n_loc], vblk)
                + jnp.einsum('bkgqs,bskd->bqkgd', p[..., n_loc:], v_c))

    o_l = lax.map(attend, (qb, kb, vb, mask))
    y_lat = jnp.moveaxis(o_l, 0, 1).reshape(bn, ln, D_MODEL) @ wo
    y_ctx = None
    if with_ctx_out:
        s = jnp.einsum('bqkgd,bskd->bkgqs', q_c, k_c, preferred_element_type=jnp.float32) * ATTN_SCALE
        p = sink_softmax(s, sink_h).astype(v_c.dtype)
        o_c = jnp.einsum('bkgqs,bskd->bqkgd', p, v_c)
        y_ctx = o_c.reshape(n_ctx.shape[0], n_ctx.shape[1], D_MODEL) @ wo
    return y_lat, y_ctx


def to_heads(t):
    return t.reshape(t.shape[0], t.shape[1], RWKV_HEADS, RWKV_HEAD)


def centred_shift(x):
    xp = jnp.pad(x, ((0, 0), (1, 1), (0, 0)))
    return 0.5 * (xp[:, :-2] + xp[:, 2:]) - x


def rwkv_features(n, mu, wr, wk, wv, k_k, vres):
    xx = centred_shift(n)
    xr, xw, xk, xv, xa, xg = (n + xx * mu[m] for m in range(6))
    r = xr @ wr
    k = xk @ wk
    v = xv @ wv
    if vres is not None:
        v_first, v0, v1, v2 = vres
        v = v + (v_first - v) * jax.nn.sigmoid(v0 + (xv @ v1) @ v2)
    kk = to_heads(k * k_k).astype(jnp.float32)
    kk = kk / jnp.maximum(jnp.sqrt(jnp.sum(kk * kk, axis=-1, keepdims=True)), L2_EPS)
    return {'r': r, 'k': k, 'v': v, 'kk': kk, 'xw': xw, 'xa': xa, 'xg': xg}


def rwkv_direction(f, k_a, w0, w1, w2, a0, a1, a2, g1, g2):
    w = -jax.nn.softplus(-(w0 + jnp.tanh(f['xw'] @ w1) @ w2)) - 0.5
    a = jax.nn.sigmoid(a0 + (f['xa'] @ a1) @ a2)
    k = f['k'] * (1 + (a - 1) * k_a)
    g = jax.nn.sigmoid(f['xg'] @ g1) @ g2
    a_h = to_heads(a).astype(jnp.float32)
    return {'r': to_heads(f['r']).astype(jnp.float32),
            'decay': jnp.exp(-jnp.exp(to_heads(w).astype(jnp.float32))),
            'k': to_heads(k).astype(jnp.float32),
            'v': to_heads(f['v']).astype(jnp.float32),
            'a': -f['kk'], 'b': f['kk'] * a_h, 'g': g}


def rwkv_scan(state0, dd, reverse):
    def step(s, inp):
        r_t, w_t, k_t, v_t, a_t, b_t = inp
        sa = jnp.einsum('bhvk,bhk->bhv', s, a_t)
        s = s * w_t[:, :, None, :] + sa[..., None] * b_t[:, :, None, :] + v_t[..., None] * k_t[:, :, None, :]
        return s, jnp.einsum('bhvk,bhk->bhv', s, r_t)
    xs = tuple(jnp.moveaxis(dd[nm], 1, 0) for nm in ('r', 'decay', 'k', 'v', 'a', 'b'))
    s_fin, ys = lax.scan(step, state0, xs, reverse=reverse)
    return s_fin, jnp.moveaxis(ys, 0, 1)


def rwkv_readout(y, dd, r_k, ln_w, ln_b):
    mu = jnp.mean(y, axis=-1, keepdims=True)
    var = jnp.mean(jnp.square(y - mu), axis=-1, keepdims=True)
    flat = y.shape[:2] + (D_MODEL,)
    yn = ((y - mu) * lax.rsqrt(var + GN_EPS)).reshape(flat) * ln_w.astype(jnp.float32) + ln_b.astype(jnp.float32)
    bonus = (jnp.sum(dd['r'] * dd['k'] * r_k.astype(jnp.float32), axis=-1, keepdims=True) * dd['v']).reshape(flat)
    return (yn + bonus) * dd['g'].astype(jnp.float32)


def rwkv_mixer(n_lat, n_ctx, p, vres_lat, vres_ctx, with_ctx_out):
    f_lat = rwkv_features(n_lat, p['mu'], p['wr'], p['wk'], p['wv'], p['k_k'], vres_lat)
    f_ctx = rwkv_features(n_ctx, p['mu'], p['wr'], p['wk'], p['wv'], p['k_k'], vres_ctx)
    s0 = jnp.zeros((n_lat.shape[0], RWKV_HEADS, RWKV_HEAD, RWKV_HEAD), jnp.float32)
    o_lat, o_ctx = [], []
    for d in range(2):
        dp = [p[nm][d] for nm in ('w0', 'w1', 'w2', 'a0', 'a1', 'a2', 'g1', 'g2')]
        d_lat = rwkv_direction(f_lat, p['k_a'], *dp)
        d_ctx = rwkv_direction(f_ctx, p['k_a'], *dp)
        s_ctx, y_ctx = rwkv_scan(s0, d_ctx, d == 1)
        _, y_lat = rwkv_scan(s_ctx, d_lat, d == 1)
        o_lat.append(rwkv_readout(y_lat, d_lat, p['r_k'], p['ln_w'][d], p['ln_b'][d]))
        if with_ctx_out:
            o_ctx.append(rwkv_readout(y_ctx, d_ctx, p['r_k'], p['ln_w'][d], p['ln_b'][d]))
    y_lat = (o_lat[0] + o_lat[1]).astype(n_lat.dtype) @ p['wo']
    y_ctx = (o_ctx[0] + o_ctx[1]).astype(n_ctx.dtype) @ p['wo'] if with_ctx_out else None
    return y_lat, y_ctx, f_lat['v'], f_ctx['v']


def setup_inputs(seed: int = 0) -> dict:
    key = jax.random.key(seed)
    ks = iter(jax.random.split(key, 64))

    def nrm(shape, scale=1.0):
        return scale * jax.random.normal(next(ks), shape, jnp.float32)

    d, f = D_MODEL, D_FF
    decay_base = jnp.linspace(-6.5, -1.5, d, dtype=jnp.float32)
    return {
        'x': nrm((BATCH, SEQ, d)),
        'c': nrm((BATCH, d)),
        'ctx': nrm((BATCH, CTX_LEN, d)),
        'c_ctx': nrm((d,)),
        'mod_w': nrm((DEPTH, d, N_MOD * d), 0.5 * d ** -0.5),
        'mod_b': nrm((DEPTH, N_MOD * d), 0.02),
        'norm_g': 1.0 + nrm((DEPTH, 6, d), 0.05),
        'ffn_w1': nrm((DEPTH, 2, d, f), d ** -0.5),
        'ffn_w3': nrm((DEPTH, 2, d, f), d ** -0.5),
        'ffn_w2': nrm((DEPTH, 2, f, d), f ** -0.5),
        'rwkv_mu': jax.random.uniform(next(ks), (N_RWKV, 6, d), jnp.float32),
        'rwkv_wr': nrm((N_RWKV, d, d), d ** -0.5),
        'rwkv_wk': nrm((N_RWKV, d, d), d ** -0.5),
        'rwkv_wv': nrm((N_RWKV, d, d), d ** -0.5),
        'rwkv_wo': nrm((N_RWKV, d, d), d ** -0.5),
        'rwkv_k_k': 0.85 + nrm((N_RWKV, d), 0.05),
        'rwkv_k_a': 1.0 + nrm((N_RWKV, d), 0.05),
        'rwkv_r_k': nrm((N_RWKV, RWKV_HEADS, RWKV_HEAD), 0.1),
        'rwkv_w0': decay_base + nrm((N_RWKV, 2, d), 0.1),
        'rwkv_w1': nrm((N_RWKV, 2, d, DECAY_LORA), 0.1 * d ** -0.5),
        'rwkv_w2': nrm((N_RWKV, 2, DECAY_LORA, d), 0.1 * DECAY_LORA ** -0.5),
        'rwkv_a0': nrm((N_RWKV, 2, d), 0.1),
        'rwkv_a1': nrm((N_RWKV, 2, d, ICLR_LORA), 0.1 * d ** -0.5),
        'rwkv_a2': nrm((N_RWKV, 2, ICLR_LORA, d), 0.1 * ICLR_LORA ** -0.5),
        'rwkv_g1': nrm((N_RWKV, 2, d, GATE_LORA), d ** -0.5),
        'rwkv_g2': nrm((N_RWKV, 2, GATE_LORA, d), GATE_LORA ** -0.5),
        'rwkv_ln_w': 1.0 + nrm((N_RWKV, 2, d), 0.05),
        'rwkv_ln_b': nrm((N_RWKV, 2, d), 0.02),
        'rwkv_v0': 1.0 + nrm((N_RWKV - 1, d), 0.1),
        'rwkv_v1': nrm((N_RWKV - 1, d, VALUE_LORA), 0.1 * d ** -0.5),
        'rwkv_v2': nrm((N_RWKV - 1, VALUE_LORA, d), 0.1 * VALUE_LORA ** -0.5),
        'gattn_wq': nrm((N_GLOBAL, d, N_HEADS * HEAD_DIM), d ** -0.5),
        'gattn_wk': nrm((N_GLOBAL, d, D_KV), d ** -0.5),
        'gattn_wv': nrm((N_GLOBAL, d, D_KV), d ** -0.5),
        'gattn_wo': nrm((N_GLOBAL, N_HEADS * HEAD_DIM, d), (N_HEADS * HEAD_DIM) ** -0.5),
        'gattn_q_norm': 1.0 + nrm((N_GLOBAL, HEAD_DIM), 0.05),
        'gattn_k_norm': 1.0 + nrm((N_GLOBAL, HEAD_DIM), 0.05),
        'wattn_wq': nrm((N_LOCAL, d, N_HEADS * HEAD_DIM), d ** -0.5),
        'wattn_wk': nrm((N_LOCAL, d, D_KV), d ** -0.5),
        'wattn_wv': nrm((N_LOCAL, d, D_KV), d ** -0.5),
        'wattn_wo': nrm((N_LOCAL, N_HEADS * HEAD_DIM, d), (N_HEADS * HEAD_DIM) ** -0.5),
        'wattn_sink': nrm((N_LOCAL, N_HEADS), 0.5),
    }


def reference(x, c, ctx, c_ctx, mod_w, mod_b, norm_g, ffn_w1, ffn_w3, ffn_w2,
              rwkv_mu, rwkv_wr, rwkv_wk, rwkv_wv, rwkv_wo, rwkv_k_k, rwkv_k_a, rwkv_r_k,
              rwkv_w0, rwkv_w1, rwkv_w2, rwkv_a0, rwkv_a1, rwkv_a2, rwkv_g1, rwkv_g2,
              rwkv_ln_w, rwkv_ln_b, rwkv_v0, rwkv_v1, rwkv_v2,
              gattn_wq, gattn_wk, gattn_wv, gattn_wo, gattn_q_norm, gattn_k_norm,
              wattn_wq, wattn_wk, wattn_wv, wattn_wo, wattn_sink):
    rows = x.shape[1] // GRID_W
    cos, sin = axial_rope(rows)
    v_first_lat, v_first_ctx = None, None
    for i in range(DEPTH):
        last = i == DEPTH - 1
        kind, j = i % N_MIXERS, i // N_MIXERS
        g = norm_g[i]
        ml = (jax.nn.silu(c) @ mod_w[i] + mod_b[i]).reshape(-1, N_MOD, 1, D_MODEL)
        ml = [ml[:, m] for m in range(N_MOD)]
        mc = (jax.nn.silu(c_ctx) @ mod_w[i] + mod_b[i]).reshape(N_MOD, D_MODEL)
        x = ffn_half(x, ml[0], ml[1], ml[2], g[0], g[1], ffn_w1[i, 0], ffn_w3[i, 0], ffn_w2[i, 0])
        ctx = ffn_half(ctx, mc[0], mc[1], mc[2], g[0], g[1], ffn_w1[i, 0], ffn_w3[i, 0], ffn_w2[i, 0])
        n_lat = rms_norm(x, g[2]) * (1 + ml[4]) + ml[3]
        n_ctx = rms_norm(ctx, g[2]) * (1 + mc[4]) + mc[3]
        if kind == 0:
            p = {'mu': rwkv_mu[j], 'wr': rwkv_wr[j], 'wk': rwkv_wk[j], 'wv': rwkv_wv[j], 'wo': rwkv_wo[j],
                 'k_k': rwkv_k_k[j], 'k_a': rwkv_k_a[j], 'r_k': rwkv_r_k[j],
                 'w0': rwkv_w0[j], 'w1': rwkv_w1[j], 'w2': rwkv_w2[j],
                 'a0': rwkv_a0[j], 'a1': rwkv_a1[j], 'a2': rwkv_a2[j],
                 'g1': rwkv_g1[j], 'g2': rwkv_g2[j], 'ln_w': rwkv_ln_w[j], 'ln_b': rwkv_ln_b[j]}
            vres_lat, vres_ctx = None, None
            if j > 0:
                vres_lat = (v_first_lat, rwkv_v0[j - 1], rwkv_v1[j - 1], rwkv_v2[j - 1])
                vres_ctx = (v_first_ctx, rwkv_v0[j - 1], rwkv_v1[j - 1], rwkv_v2[j - 1])
            y_lat, y_ctx, v_lat, v_ctx = rwkv_mixer(n_lat, n_ctx, p, vres_lat, vres_ctx, not last)
            if j == 0:
                v_first_lat, v_first_ctx = v_lat, v_ctx
        elif kind == 1:
            y_lat, y_ctx = global_attention(n_lat, n_ctx, gattn_wq[j], gattn_wk[j], gattn_wv[j], gattn_wo[j],
                                            gattn_q_norm[j], gattn_k_norm[j], cos, sin, not last)
        else:
            y_lat, y_ctx = window_attention(n_lat, n_ctx, wattn_wq[j], wattn_wk[j], wattn_wv[j], wattn_wo[j],
                                            wattn_sink[j], cos, sin, not last)
        x = x + ml[5] * rms_norm(y_lat, g[3])
        x = ffn_half(x, ml[6], ml[7], ml[8], g[4], g[5], ffn_w1[i, 1], ffn_w3[i, 1], ffn_w2[i, 1])
        if not last:
            ctx = ctx + mc[5] * rms_norm(y_ctx, g[3])
            ctx = ffn_half(ctx, mc[6], mc[7], mc[8], g[4], g[5], ffn_w1[i, 1], ffn_w3[i, 1], ffn_w2[i, 1])
    return x
```

```python
import numpy as np
from contextlib import ExitStack
import concourse.bass as bass
import concourse.mybir as mybir
from concourse.bass_utils import run_bass_kernel_spmd

F32 = mybir.dt.float32
BF16 = mybir.dt.bfloat16
AF = mybir.ActivationFunctionType
ALU = mybir.AluOpType
AX = mybir.AxisListType

D = 1024
SEQ = 4096
CTXL = 256
NTOK = SEQ + CTXL
NT = NTOK // 128
DEPTH = 4
DFF = 2816
NFC = DFF // 128
NMOD = 9
EPS = 1e-6


class Sched:
    def __init__(self, nc):
        self.nc = nc
        self.stack = ExitStack()
        self.eng = {}
        for name, e in (("pe", nc.tensor), ("dve", nc.vector), ("act", nc.scalar),
                        ("pool", nc.gpsimd), ("sp", nc.sync)):
            sem = self.stack.enter_context(nc.semaphore("s_" + name))
            self.eng[name] = dict(e=e, sem=sem, count=0, waited={}, pending=False)
        self.lastw = {}
        self.readers = {}
        self.dsem = {}
        self.nwaits = 0
        self.nins = 0

    def _wait(self, E, need):
        for sid, (sem, val) in need.items():
            if E["waited"].get(sid, 0) < val:
                E["e"].wait_ge(sem, val)
                E["waited"][sid] = val
                self.nwaits += 1

    @staticmethod
    def _merge(need, toks, skip=None):
        for sid, (sem, val) in toks.items():
            if sid == skip:
                continue
            if sid not in need or need[sid][1] < val:
                need[sid] = (sem, val)

    def op(self, eng, fn, reads=(), writes=(), inc=True):
        E = self.eng[eng]
        need = {}
        skip = "pe" if eng == "pe" else None
        for k in reads:
            self._merge(need, self.lastw.get(k, {}), skip)
        for k in writes:
            self._merge(need, self.lastw.get(k, {}), skip)
            self._merge(need, self.readers.get(k, {}), skip)
        self._wait(E, need)
        ins = fn(E["e"])
        self.nins += 1
        if inc:
            E["count"] += 1
            ins.then_inc(E["sem"], 1)
            val = E["count"]
            E["pending"] = False
        else:
            val = E["count"] + 1
            E["pending"] = True
        tok = (E["sem"], val)
        for k in reads:
            self.readers.setdefault(k, {})[eng] = tok
        for k in writes:
            self.lastw[k] = {eng: tok}
            self.readers[k] = {}
        return ins

    def dma(self, q, out, in_, key, load, part=False, **kw):
        E = self.eng[q]
        sid = ("dma", key)
        need = {}
        self._merge(need, self.lastw.get(key, {}), sid if (load and part) else None)
        if load:
            self._merge(need, self.readers.get(key, {}), None)
        self._wait(E, need)
        if key not in self.dsem:
            sem = self.stack.enter_context(self.nc.semaphore("d_%d" % len(self.dsem)))
            self.dsem[key] = [sem, 0]
        ds = self.dsem[key]
        ins = E["e"].dma_start(out=out, in_=in_, **kw)
        self.nins += 1
        ds[1] += 16
        ins.then_inc(ds[0], 16)
        tok = (ds[0], ds[1])
        if load:
            self.lastw[key] = {sid: tok}
            self.readers[key] = {}
        else:
            self.readers.setdefault(key, {})[sid] = tok
        return ins

    def barrier(self):
        toks = {}
        for name, E in self.eng.items():
            assert not E["pending"], name
            if E["count"] > 0:
                toks[name] = (E["sem"], E["count"])
        for key, ds in self.dsem.items():
            if ds[1] > 0:
                toks[("dma", key)] = (ds[0], ds[1])
        for name, E in self.eng.items():
            self._wait(E, toks)
        self.lastw = {}
        self.readers = {}


HD = 64
NH = 16
NKV = 4
DKV = 256
ATTN_SCALE = HD ** -0.5
NPOS = NTOK
NSROWS = NTOK + 3
GN_EPS = 64e-5
MASKV = -30000.0


class A:
    __slots__ = ("ap", "k")

    def __init__(self, ap, k):
        self.ap = ap
        self.k = k


class T:
    def __init__(self, t, k):
        self.t = t
        self.k = k

    def __getitem__(self, idx):
        return A(self.t[idx], self.k)

    def cust(self, offset, dims):
        return A(bass.AP(tensor=self.t[:].tensor, offset=offset, ap=[list(d) for d in dims]), self.k)


def _ap(x):
    return x.ap if isinstance(x, A) else x


def _keys(*xs):
    return [x.k for x in xs if isinstance(x, A)]


def pos_of_tile(tt):
    return 256 + tt * 128 if tt < 32 else (tt - 32) * 128


def nsrow_of_tile(tt):
    return 1 + tt * 128 if tt < 32 else 4098 + (tt - 32) * 128


class KB:
    def __init__(self, nc, cfg):
        self.nc = nc
        self.cfg = cfg
        self.S = Sched(nc)
        self.gs = self.S.stack
        self.I = {}

    def dram_in(self, name, shape):
        self.I[name] = self.nc.dram_tensor(name, list(shape), F32, kind="ExternalInput").ap()
        return self.I[name]

    def scratch(self, name, shape):
        return self.nc.dram_tensor(name, list(shape), F32, kind="Internal").ap()

    def sb(self, st, name, shape, dt=F32):
        self.uid = getattr(self, "uid", 0) + 1
        return T(st.enter_context(self.nc.sbuf_tensor("%s_%d" % (name, self.uid), list(shape), dt)), name)

    def ps(self, st, name, shape, dt=F32):
        self.uid = getattr(self, "uid", 0) + 1
        return T(st.enter_context(self.nc.psum_tensor("%s_%d" % (name, self.uid), list(shape), dt)), name)

    def tt(self, eng, out, a, b, op):
        self.S.op(eng, lambda e: e.tensor_tensor(out=out.ap, in0=a.ap, in1=b.ap, op=op),
                  reads=_keys(a, b), writes=[out.k])

    def ts(self, eng, out, a, s1, s2, op0, op1=None):
        if op1 is None:
            f = lambda e: e.tensor_scalar(out=out.ap, in0=a.ap, scalar1=_ap(s1), scalar2=None, op0=op0)
        else:
            f = lambda e: e.tensor_scalar(out=out.ap, in0=a.ap, scalar1=_ap(s1), scalar2=_ap(s2), op0=op0, op1=op1)
        self.S.op(eng, f, reads=_keys(a, s1, s2), writes=[out.k])

    def stt(self, out, a, s, b, op0, op1):
        self.S.op("dve", lambda e: e.scalar_tensor_tensor(out=out.ap, in0=a.ap, scalar=_ap(s), in1=b.ap,
                                                          op0=op0, op1=op1),
                  reads=_keys(a, s, b), writes=[out.k])

    def act(self, out, a, func, scale=1.0, bias=None, accum=None):
        kw = {}
        if bias is not None:
            kw["bias"] = _ap(bias)
        if accum is not None:
            kw["accum_out"] = accum.ap
        w = [out.k] + ([accum.k] if accum is not None else [])
        self.S.op("act", lambda e: e.activation(out=out.ap, in_=a.ap, func=func, scale=_ap(scale), **kw),
                  reads=_keys(a, bias, scale), writes=w)

    def red(self, out, a, op=ALU.add):
        self.S.op("dve", lambda e: e.tensor_reduce(out=out.ap, in_=a.ap, axis=AX.X, op=op),
                  reads=[a.k], writes=[out.k])

    def recip(self, out, a):
        self.S.op("dve", lambda e: e.reciprocal(out=out.ap, in_=a.ap), reads=[a.k], writes=[out.k])

    def cp(self, eng, out, a):
        if eng == "act":
            f = lambda e: e.copy(out=out.ap, in_=a.ap)
        else:
            f = lambda e: e.tensor_copy(out=out.ap, in_=a.ap)
        self.S.op(eng, f, reads=[a.k], writes=[out.k])

    def memset(self, eng, out, val):
        self.S.op(eng, lambda e: e.memset(out.ap, val), writes=[out.k])

    def mm(self, out, lhsT, rhs, start, stop, inc=None):
        self.S.op("pe", lambda e: e.matmul(out.ap, lhsT=lhsT.ap, rhs=rhs.ap, start=start, stop=stop),
                  reads=[lhsT.k, rhs.k], writes=[out.k], inc=(stop if inc is None else inc))

    def tr(self, out, a, ident, inc=True):
        self.S.op("pe", lambda e: e.transpose(out.ap, a.ap, ident.ap), reads=[a.k, ident.k], writes=[out.k], inc=inc)

    def load(self, dst, src_ap, q="sp", part=False, **kw):
        if q == "pool":
            kw.setdefault("max_dma_last_dim", 4096)
        self.S.dma(q, dst.ap, src_ap, dst.k, True, part=part, **kw)

    def store(self, dst_ap, src, q="sp", **kw):
        self.S.dma(q, dst_ap, src.ap, src.k, False, **kw)

    def pow_(self, out, a, expo):
        n = a.ap.shape[-1] if len(a.ap.shape) == 2 else None
        e = self.chalf if expo == 0.5 else self.nhalf
        self.tt("pool", out, a, e[:, 0:n], ALU.pow)

    def setup_consts(self):
        self.ident_f = self.sb(self.gs, "ident_f", [128, 128], F32)
        self.ident_b = self.sb(self.gs, "ident_b", [128, 128], BF16)
        self.nhalf = self.sb(self.gs, "nhalf", [128, 16], F32)
        self.chalf = self.sb(self.gs, "chalf", [128, 16], F32)
        self.ones_f = self.sb(self.gs, "ones_f", [128, 128], F32)
        self.memset("pool", self.nhalf[:], -0.5)
        self.memset("pool", self.chalf[:], 0.5)
        self.memset("pool", self.ones_f[:], 1.0)
        self.load(self.ident_f[:], self.I["ident"][:, :])
        self.cp("dve", self.ident_b[:], self.ident_f[:])

    def mod_phase(self):
        S = self.S
        with ExitStack() as st:
            craw = self.sb(st, "craw", [128, 2, 8], F32)
            sc = self.sb(st, "sc", [128, 8, 2], F32)
            wt = [self.sb(st, "modw%d" % i, [128, 8, 512], F32) for i in range(2)]
            bias = self.sb(st, "modbias", [2, NMOD * D], F32)
            res = self.sb(st, "modres", [2, NMOD * D], F32)
            pss = [self.ps(st, "modps%d" % i, [2, 512], F32) for i in range(2)]
            self.load(craw[:, 0, :], self.I["c"].rearrange("(p k) -> p k", k=8), part=True)
            self.load(craw[:, 1, :], self.I["c_ctx"].rearrange("(p k) -> p k", k=8), part=True)
            for m in range(2):
                self.act(sc[:, :, m], craw[:, m, :], AF.Silu)
            for i in range(self.cfg.get("l0", 0), self.cfg.get("l1", DEPTH)):
                wv = self.I["mod_w"][i].rearrange("(p k) n -> p k n", k=8)
                self.load(bias[:], self.I["mod_b"][i].partition_broadcast(2))
                for cb in range(18):
                    w = wt[cb % 2]
                    p = pss[cb % 2]
                    self.load(w[:], wv[:, :, cb * 512:(cb + 1) * 512])
                    for kc in range(8):
                        self.mm(p[:], sc[:, kc, :], w[:, kc, :], kc == 0, kc == 7)
                    self.tt("dve", res[:, cb * 512:(cb + 1) * 512], p[:], bias[:, cb * 512:(cb + 1) * 512], ALU.add)
                self.store(self.MOD[i], res[:])
        S.barrier()

    def bload(self, dst, row_ap):
        self.load(dst[:], row_ap.partition_broadcast(128))

    def alloc_common(self, st, nx=2):
        self.tmpa = self.sb(st, "tmpa", [128, D], F32)
        self.tmpb = self.sb(st, "tmpb", [128, D], F32)
        self.mvA = self.sb(st, "mvA", [128, D], F32)
        self.mvS = self.sb(st, "mvS", [128, D], F32)
        self.mvG = self.sb(st, "mvG", [128, D], F32)
        self.xt = [self.sb(st, "xt%d" % j, [128, D], F32) for j in range(nx)]
        self.nb = self.sb(st, "nb", [128, D], BF16)
        self.stat = self.sb(st, "stat", [128, 8], F32)
        self.pT = self.ps(st, "pT", [128, 8, 128], BF16)
        self.py = self.ps(st, "py", [128, D], F32)

    def set_mod(self, i, m, mA, gA, mS, mG, gG, cG):
        def mod(idx):
            return self.MOD[i, m, idx * D:(idx + 1) * D]
        self.bload(self.tmpa, mod(mA))
        self.bload(self.tmpb, self.I["norm_g"][i, gA])
        self.stt(self.mvA[:], self.tmpa[:], 1.0, self.tmpb[:], ALU.add, ALU.mult)
        self.bload(self.mvS, mod(mS))
        self.bload(self.tmpa, mod(mG))
        self.bload(self.tmpb, self.I["norm_g"][i, gG])
        self.stt(self.mvG[:], self.tmpa[:], float(cG), self.tmpb[:], ALU.mult, ALU.mult)

    def rms_rstd(self, src, junk, ss, rstd, n=D):
        self.act(junk, src, AF.Square, accum=ss)
        self.ts("dve", rstd, ss, 1.0 / n, EPS, ALU.mult, ALU.add)
        self.tt("pool", rstd, rstd, self.nhalf[:, 0:1], ALU.pow)

    def norm_tile(self, x, dst):
        st = self.stat
        self.rms_rstd(x[:], self.nb[:], st[:, 0:1], st[:, 1:2])
        self.stt(self.tmpa[:], x[:], st[:, 1:2], self.mvA[:], ALU.mult, ALU.mult)
        self.tt("pool", dst, self.tmpa[:], self.mvS[:], ALU.add)

    def transpose_cols(self, dstT, col0, src, nchunk=8):
        for kc in range(nchunk):
            self.tr(self.pT[:, kc, :], src[:, kc * 128:(kc + 1) * 128], self.ident_b[:], inc=(kc == nchunk - 1))
        self.cp("act", dstT[:, 0:nchunk, col0:col0 + 128], self.pT[:, 0:nchunk, :])

    def resid_tile(self, x, dst_ap):
        st = self.stat
        self.rms_rstd(self.py[:], self.tmpb[:], st[:, 2:3], st[:, 3:4])
        self.stt(self.tmpb[:], self.py[:], st[:, 3:4], self.mvG[:], ALU.mult, ALU.mult)
        self.tt("pool", x[:], self.tmpb[:], x[:], ALU.add)
        self.store(dst_ap, x[:])

    def load_w_bf16(self, dst, src, nk):
        sv = src.rearrange("(k p) n -> p k n", p=128)
        for kc in range(nk):
            self.load(dst[:, kc, :], sv[:, kc, :], q="pool", part=True)

    def ffn_phase(self, i, h, src, dst, tiles):
        S = self.S
        mb = 0 if h == 0 else 6
        gb = 0 if h == 0 else 4
        with ExitStack() as st:
            W1 = self.sb(st, "W1", [128, 8, DFF], BF16)
            W3 = self.sb(st, "W3", [128, 8, DFF], BF16)
            W2 = self.sb(st, "W2", [128, NFC, D], BF16)
            self.load_w_bf16(W1, self.I["ffn_w1"][i, h], 8)
            self.load_w_bf16(W3, self.I["ffn_w3"][i, h], 8)
            self.load_w_bf16(W2, self.I["ffn_w2"][i, h], NFC)
            self.alloc_common(st, nx=4)
            nT = [self.sb(st, "nT%d" % j, [128, 8, 256], BF16) for j in range(2)]
            gT = self.sb(st, "gT", [128, NFC, 256], BF16)
            sl = [self.sb(st, "sl%d" % j, [128, 256], BF16) for j in range(2)]
            ph = [self.ps(st, "ph%d" % j, [128, 256], F32) for j in range(4)]
            groups = []
            lat = [t for t in tiles if t < 32]
            ctx = [t for t in tiles if t >= 32]
            for lst in (lat, ctx):
                groups += [lst[a:a + 2] for a in range(0, len(lst), 2)]
            cur_m = None
            for gi_, grp in enumerate(groups):
                m = 0 if grp[0] < 32 else 1
                if m != cur_m:
                    self.set_mod(i, m, mb + 1, gb + 0, mb + 0, mb + 2, gb + 1, 0.5)
                    cur_m = m
                ntk = 128 * len(grp)
                nTg = nT[gi_ % 2]
                for j, tt_ in enumerate(grp):
                    x = self.xt[(gi_ % 2) * 2 + j]
                    self.load(x[:], src(tt_))
                    self.norm_tile(x, self.nb[:])
                    self.transpose_cols(nTg, j * 128, self.nb)
                for fc in range(NFC):
                    p1 = ph[(fc % 2) * 2]
                    p3 = ph[(fc % 2) * 2 + 1]
                    for (W, p) in ((W1, p1), (W3, p3)):
                        for kc in range(8):
                            self.mm(p[:, :ntk], W[:, kc, fc * 128:(fc + 1) * 128], nTg[:, kc, :ntk], kc == 0, kc == 7)
                    s = sl[fc % 2]
                    self.act(s[:, :ntk], p1[:, :ntk], AF.Silu)
                    self.tt("dve", gT[:, fc, :ntk], s[:, :ntk], p3[:, :ntk], ALU.mult)
                for j, tt_ in enumerate(grp):
                    x = self.xt[(gi_ % 2) * 2 + j]
                    for hf in range(2):
                        for fc in range(NFC):
                            self.mm(self.py[:, hf * 512:(hf + 1) * 512], gT[:, fc, j * 128:(j + 1) * 128],
                                    W2[:, fc, hf * 512:(hf + 1) * 512], fc == 0, fc == NFC - 1)
                    self.resid_tile(x, dst(tt_))
        S.barrier()

    def rope(self, dst, src, H, tt_):
        cs, sn = self.rcos, self.rsin
        self.load(cs[:], self.I["rope_cos"][tt_ * 128:(tt_ + 1) * 128, :])
        self.load(sn[:], self.I["rope_sin"][tt_ * 128:(tt_ + 1) * 128, :])

        def hv(t, w, dt_cols=None):
            v = t.t[:, 0:H * 64].rearrange("p (h a w f) -> p h a w f", h=H, a=2, w=2)
            return A(v[:, :, :, w, :], t.k)

        def bc(t):
            v = t.t[:, :].rearrange("p (a f) -> p a f", a=2).unsqueeze(1).to_broadcast([128, H, 2, 16])
            return A(v, t.k)

        def tv(t):
            return A(t.t[:, 0:H * 32].rearrange("p (h a f) -> p h a f", h=H, a=2), t.k)
        t1, t2 = hv(src, 0), hv(src, 1)
        c, s = bc(cs), bc(sn)
        u1, u2, u3, u4 = (tv(u) for u in self.ru)
        self.tt("dve", u1, t1, c, ALU.mult)
        self.tt("pool", u2, t2, s, ALU.mult)
        self.tt("dve", hv(dst, 0), u1, u2, ALU.subtract)
        self.tt("pool", u3, t2, c, ALU.mult)
        self.tt("dve", u4, t1, s, ALU.mult)
        self.tt("pool", hv(dst, 1), u3, u4, ALU.add)

    def alloc_rope(self, st):
        self.rcos = self.sb(st, "rcos", [128, 32], F32)
        self.rsin = self.sb(st, "rsin", [128, 32], F32)
        self.ru = [self.sb(st, "ru%d" % j, [128, 512], F32) for j in range(4)]

    def head_rms(self, dst, src, H, gvec):
        sq = self.tmpb
        self.tt("dve", sq[:, 0:H * 64], src, src, ALU.mult)
        ss = self.hstat
        self.red(ss[:, 0:H], A(sq.t[:, 0:H * 64].rearrange("p (h k) -> p h k", h=H), sq.k))
        self.ts("dve", ss[:, 0:H], ss[:, 0:H], 1.0 / HD, EPS, ALU.mult, ALU.add)
        self.tt("pool", ss[:, 0:H], ss[:, 0:H], self.nhalf[:, 0:H], ALU.pow)
        s3 = A(src.ap.rearrange("p (h k) -> p h k", h=H), src.k)
        d3 = A(dst.ap.rearrange("p (h k) -> p h k", h=H), dst.k)
        rb = A(ss.t[:, 0:H].unsqueeze(2).to_broadcast([128, H, HD]), ss.k)
        gb = A(gvec.t[:, :].unsqueeze(1).to_broadcast([128, H, HD]), gvec.k)
        self.tt("dve", d3, s3, rb, ALU.mult)
        self.tt("pool", d3, d3, gb, ALU.mult)

    def gattn_phase(self, i, j, src, dst, with_ctx, qtiles=None):
        S = self.S
        I = self.I
        with ExitStack() as st:
            Wq = self.sb(st, "Wq", [128, 8, D], BF16)
            Wkv = self.sb(st, "Wkv", [128, 8, 2 * DKV], BF16)
            Wo = self.sb(st, "Wo", [HD, NH, D], BF16)
            self.load_w_bf16(Wq, I["gattn_wq"][j], 8)
            wk = I["gattn_wk"][j].rearrange("(k p) n -> p k n", p=128)
            wv = I["gattn_wv"][j].rearrange("(k p) n -> p k n", p=128)
            for kc in range(8):
                self.load(Wkv[:, kc, 0:DKV], wk[:, kc, :], q="pool", part=True)
                self.load(Wkv[:, kc, DKV:2 * DKV], wv[:, kc, :], q="pool", part=True)
            wo = I["gattn_wo"][j].rearrange("(h p) n -> p h n", p=HD)
            for h in range(NH):
                self.load(Wo[:, h, :], wo[:, h, :], q="pool", part=True)
            self.alloc_common(st, nx=2)
            self.alloc_rope(st)
            self.hstat = self.sb(st, "hstat", [128, 16], F32)
            gq = self.sb(st, "gq", [128, HD], F32)
            gk = self.sb(st, "gk", [128, HD], F32)
            negm = self.sb(st, "negm", [128, 4], F32)
            self.bload(gq, I["gattn_q_norm"][j])
            self.bload(gk, I["gattn_k_norm"][j])
            S.op("dve", lambda e: e.tensor_reduce(out=negm.t[:, 0:1], in_=gq.t[:, :], axis=AX.X, op=ALU.max,
                                                  apply_absolute_value=True), reads=["gq"], writes=["negm"])
            S.op("dve", lambda e: e.tensor_reduce(out=negm.t[:, 1:2], in_=gk.t[:, :], axis=AX.X, op=ALU.max,
                                                  apply_absolute_value=True), reads=["gk"], writes=["negm"])
            self.tt("dve", negm[:, 2:3], negm[:, 0:1], negm[:, 1:2], ALU.mult)
            self.ts("dve", negm[:, 3:4], negm[:, 2:3], -8.0, None, ALU.mult)
            KT = self.sb(st, "KT", [HD, NKV, NTOK], BF16)
            V = self.sb(st, "V", [128, NT, NKV, HD + 1], BF16)
            self.memset("pool", V[:], 1.0)
            nT = self.sb(st, "nT", [128, 8, 128], BF16)
            qf = self.sb(st, "qf", [128, D], F32)
            qb = self.sb(st, "qb", [128, D], BF16)
            QT = self.sb(st, "QT", [HD, NH, 512], BF16)
            OT = self.sb(st, "OT", [HD, NH, 512], BF16)
            pexp = [self.sb(st, "pexp%d" % a, [128, 512], BF16) for a in range(3)]
            rd = self.sb(st, "rd", [HD + 1, 512], F32)
            bcs = self.sb(st, "bcs", [HD, 512], F32)
            pss = [self.ps(st, "pss%d" % a, [128, 512], F32) for a in range(2)]
            po = [self.ps(st, "po%d" % a, [HD + 1, 512], F32) for a in range(2)]
            pbc = self.ps(st, "pbc", [HD, 512], F32)
            x = self.xt[0]

            def prologue(tt_):
                self.load(x[:], src(tt_))
                self.norm_tile(x, self.nb[:])
                self.transpose_cols(nT, 0, self.nb)

            cur_m = None
            for tt_ in range(NT):
                m = 0 if tt_ < 32 else 1
                if m != cur_m:
                    self.set_mod(i, m, 4, 2, 3, 5, 3, 1.0)
                    cur_m = m
                prologue(tt_)
                pkv = self.py
                for kc in range(8):
                    self.mm(pkv[:, 0:512], nT[:, kc, :], Wkv[:, kc, :], kc == 0, kc == 7)
                self.cp("act", qf[:, 0:DKV], pkv[:, 0:DKV])
                self.head_rms(qf[:, 0:DKV], qf[:, 0:DKV], NKV, gk)
                if m == 0:
                    self.rope(qb, qf, NKV, tt_)
                else:
                    self.cp("act", qb[:, 0:DKV], qf[:, 0:DKV])
                for h in range(NKV):
                    self.tr(self.pT[0:HD, h, :], qb[:, h * HD:(h + 1) * HD], self.ident_b[:], inc=(h == NKV - 1))
                self.cp("act", KT[:, :, tt_ * 128:(tt_ + 1) * 128], self.pT[0:HD, 0:NKV, :])
                self.cp("act", V[:, tt_, :, 0:HD],
                        A(pkv.t[:, DKV:2 * DKV].rearrange("p (h k) -> p h k", h=NKV), pkv.k))
            groups = [list(range(a, a + 4)) for a in range(0, 32, 4)]
            if with_ctx:
                groups.append([32, 33])
            if qtiles is not None:
                groups = qtiles
            cur_m = 1
            pi = 0
            for grp in groups:
                m = 0 if grp[0] < 32 else 1
                if m != cur_m:
                    self.set_mod(i, m, 4, 2, 3, 5, 3, 1.0)
                    cur_m = m
                nq = 128 * len(grp)
                for jj, tt_ in enumerate(grp):
                    prologue(tt_)
                    for hf in range(2):
                        for kc in range(8):
                            self.mm(self.py[:, hf * 512:(hf + 1) * 512], nT[:, kc, :], Wq[:, kc, hf * 512:(hf + 1) * 512],
                                    kc == 0, kc == 7)
                    self.cp("act", qf[:], self.py[:])
                    self.head_rms(qf[:], qf[:], NH, gq)
                    if m == 0:
                        self.rope(qb, qf, NH, tt_)
                    else:
                        self.cp("act", qb[:], qf[:])
                    for hb in range(2):
                        for h8 in range(8):
                            h = hb * 8 + h8
                            self.tr(self.pT[0:HD, h8, :], qb[:, h * HD:(h + 1) * HD], self.ident_b[:], inc=(h8 == 7))
                        self.cp("act", QT[:, hb * 8:(hb + 1) * 8, jj * 128:(jj + 1) * 128], self.pT[0:HD, :, :])
                kts = list(range(NT)) if m == 0 else [32, 33]
                for h in range(NH):
                    kvh = h // (NH // NKV)
                    pacc = po[h % 2]
                    for ki, kt in enumerate(kts):
                        ps_ = pss[pi % 2]
                        pe_ = pexp[pi % 3]
                        pi += 1
                        self.mm(ps_[:, :nq], KT[:, kvh, kt * 128:(kt + 1) * 128], QT[:, h, :nq], True, True)
                        self.act(pe_[:, :nq], ps_[:, :nq], AF.Exp, scale=ATTN_SCALE, bias=negm[:, 3:4])
                        self.mm(pacc[:, :nq], V[:, kt, kvh, :], pe_[:, :nq], ki == 0, ki == len(kts) - 1)
                    self.recip(rd[HD:HD + 1, :nq], pacc[HD:HD + 1, :nq])
                    self.mm(pbc[:, :nq], self.ones_f[HD:HD + 1, 0:HD], rd[HD:HD + 1, :nq], True, True)
                    self.cp("act", bcs[:, :nq], pbc[:, :nq])
                    self.tt("dve", OT[:, h, :nq], pacc[0:HD, :nq], bcs[:, :nq], ALU.mult)
                for jj, tt_ in enumerate(grp):
                    for hf in range(2):
                        for h in range(NH):
                            self.mm(self.py[:, hf * 512:(hf + 1) * 512], OT[:, h, jj * 128:(jj + 1) * 128],
                                    Wo[:, h, hf * 512:(hf + 1) * 512], h == 0, h == NH - 1)
                    x2 = self.xt[1]
                    self.load(x2[:], src(tt_))
                    self.resid_tile(x2, dst(tt_))
        S.barrier()

    def wattn_phase(self, i, j, src, dst, with_ctx, qtiles=None):
        S = self.S
        I = self.I
        with ExitStack() as st:
            Wq = self.sb(st, "Wq", [128, 8, D], BF16)
            Wkv = self.sb(st, "Wkv", [128, 8, 2 * DKV], BF16)
            Wo = self.sb(st, "Wo", [128, 8, D], BF16)
            self.load_w_bf16(Wq, I["wattn_wq"][j], 8)
            self.load_w_bf16(Wo, I["wattn_wo"][j], 8)
            wk = I["wattn_wk"][j].rearrange("(k p) n -> p k n", p=128)
            wv = I["wattn_wv"][j].rearrange("(k p) n -> p k n", p=128)
            for kc in range(8):
                self.load(Wkv[:, kc, 0:DKV], wk[:, kc, :], q="pool", part=True)
                self.load(Wkv[:, kc, DKV:2 * DKV], wv[:, kc, :], q="pool", part=True)
            self.alloc_common(st, nx=2)
            self.alloc_rope(st)
            sinkb = self.sb(st, "sinkb", [128, NH], F32)
            self.bload(sinkb, I["wattn_sink"][j])
            mask = self.sb(st, "mask", [128, 384], F32)
            self.load(mask[:], I["wmask"][:, :])
            KT = self.sb(st, "KT", [HD, NKV, NTOK], BF16)
            V = self.sb(st, "V", [128, NT, NKV, HD], BF16)
            nT = self.sb(st, "nT", [128, 8, 128], BF16)
            qf = self.sb(st, "qf", [128, D], F32)
            qb = self.sb(st, "qb", [128, D], BF16)
            QT = self.sb(st, "QT", [HD, NH, 128], BF16)
            sc = self.sb(st, "sc", [128, 640], F32)
            P = self.sb(st, "P", [128, 640], BF16)
            PT = self.sb(st, "PT", [128, 5, 128], BF16)
            O = self.sb(st, "O", [128, D], BF16)
            OT = self.sb(st, "OT", [128, 8, 128], BF16)
            sm = self.sb(st, "sm", [128, 8], F32)
            pl = self.ps(st, "pl", [128, 512], F32)
            pc = self.ps(st, "pc", [128, 512], F32)
            pPT = self.ps(st, "pPT", [128, 8, 128], BF16)
            pov = self.ps(st, "pov", [128, 512], F32)
            x = self.xt[0]

            def prologue(tt_):
                self.load(x[:], src(tt_))
                self.norm_tile(x, self.nb[:])
                self.transpose_cols(nT, 0, self.nb)

            cur_m = None
            for tt_ in range(NT):
                m = 0 if tt_ < 32 else 1
                if m != cur_m:
                    self.set_mod(i, m, 4, 2, 3, 5, 3, 1.0)
                    cur_m = m
                prologue(tt_)
                pkv = self.py
                for kc in range(8):
                    self.mm(pkv[:, 0:512], nT[:, kc, :], Wkv[:, kc, :], kc == 0, kc == 7)
                if m == 0:
                    self.cp("act", qf[:, 0:DKV], pkv[:, 0:DKV])
                    self.rope(qb, qf, NKV, tt_)
                else:
                    self.cp("act", qb[:, 0:DKV], pkv[:, 0:DKV])
                for h in range(NKV):
                    self.tr(self.pT[0:HD, h, :], qb[:, h * HD:(h + 1) * HD], self.ident_b[:], inc=(h == NKV - 1))
                self.cp("act", KT[:, :, tt_ * 128:(tt_ + 1) * 128], self.pT[0:HD, 0:NKV, :])
                self.cp("act", V[:, tt_, :, :],
                        A(pkv.t[:, DKV:2 * DKV].rearrange("p (h k) -> p h k", h=NKV), pkv.k))
            qt = list(range(32)) + ([32, 33] if with_ctx else [])
            if qtiles is not None:
                qt = qtiles
            cur_m = 1
            for tt_ in qt:
                m = 0 if tt_ < 32 else 1
                if m != cur_m:
                    self.set_mod(i, m, 4, 2, 3, 5, 3, 1.0)
                    cur_m = m
                prologue(tt_)
                for hf in range(2):
                    for kc in range(8):
                        self.mm(self.py[:, hf * 512:(hf + 1) * 512], nT[:, kc, :], Wq[:, kc, hf * 512:(hf + 1) * 512],
                                kc == 0, kc == 7)
                if m == 0:
                    self.cp("act", qf[:], self.py[:])
                    self.rope(qb, qf, NH, tt_)
                else:
                    self.cp("act", qb[:], self.py[:])
                for hb in range(2):
                    for h8 in range(8):
                        h = hb * 8 + h8
                        self.tr(self.pT[0:HD, h8, :], qb[:, h * HD:(h + 1) * HD], self.ident_b[:], inc=(h8 == 7))
                    self.cp("act", QT[:, hb * 8:(hb + 1) * 8, :], self.pT[0:HD, :, :])
                if m == 0:
                    b0 = max(tt_ - 1, 0)
                    b1 = min(tt_ + 1, 31)
                    nloc = (b1 - b0 + 1) * 128
                    moff = 0 if tt_ > 0 else 128
                    ktiles = list(range(b0, b1 + 1)) + [32, 33]
                else:
                    nloc = 0
                    ktiles = [32, 33]
                nk = nloc + 256
                for h in range(NH):
                    kvh = h // (NH // NKV)
                    if nloc:
                        self.mm(pl[:, :nloc], QT[:, h, :], KT[:, kvh, b0 * 128:(b1 + 1) * 128], True, True)
                        self.stt(sc[:, :nloc], pl[:, :nloc], ATTN_SCALE, mask[:, moff:moff + nloc], ALU.mult, ALU.add)
                    self.mm(pc[:, 0:256], QT[:, h, :], KT[:, kvh, SEQ:NTOK], True, True)
                    self.act(sc[:, nloc:nk], pc[:, 0:256], AF.Copy, scale=ATTN_SCALE)
                    self.red(sm[:, 0:1], sc[:, :nk], ALU.max)
                    self.tt("dve", sm[:, 0:1], sm[:, 0:1], sinkb[:, h:h + 1], ALU.max)
                    self.ts("dve", sm[:, 1:2], sm[:, 0:1], -1.0, None, ALU.mult)
                    self.act(P[:, :nk], sc[:, :nk], AF.Exp, bias=sm[:, 1:2], accum=sm[:, 2:3])
                    self.act(sm[:, 3:4], sinkb[:, h:h + 1], AF.Exp, bias=sm[:, 1:2])
                    self.tt("dve", sm[:, 4:5], sm[:, 2:3], sm[:, 3:4], ALU.add)
                    self.recip(sm[:, 5:6], sm[:, 4:5])
                    nkt = nk // 128
                    for kt in range(nkt):
                        self.tr(pPT[:, kt, :], P[:, kt * 128:(kt + 1) * 128], self.ident_b[:], inc=(kt == nkt - 1))
                    self.cp("act", PT[:, 0:nkt, :], pPT[:, 0:nkt, :])
                    for kt in range(nkt):
                        self.mm(pov[:, 0:HD], PT[:, kt, :], V[:, ktiles[kt], kvh, :], kt == 0, kt == nkt - 1)
                    self.ts("dve", O[:, h * HD:(h + 1) * HD], pov[:, 0:HD], sm[:, 5:6], None, ALU.mult)
                self.transpose_cols(OT, 0, O)
                for hf in range(2):
                    for kc in range(8):
                        self.mm(self.py[:, hf * 512:(hf + 1) * 512], OT[:, kc, :], Wo[:, kc, hf * 512:(hf + 1) * 512],
                                kc == 0, kc == 7)
                x2 = self.xt[1]
                self.load(x2[:], src(tt_))
                self.resid_tile(x2, dst(tt_))
        S.barrier()

    def hm_tile(self, arr, pos0):
        return arr[:, pos0:pos0 + 128, :].rearrange("h t k -> t h k")

    @staticmethod
    def v3(a, H=NH):
        return A(a.ap.rearrange("p (h k) -> p h k", h=H), a.k)

    def rwkv_norm_pass(self, i, src):
        S = self.S
        with ExitStack() as st:
            self.alloc_common(st, nx=2)
            nf = [self.sb(st, "nf%d" % a, [128, D], F32) for a in range(2)]
            z = self.sb(st, "zrow", [1, D], F32)
            self.memset("pool", z[:], 0.0)
            for r in (0, SEQ + 1, NSROWS - 1):
                self.store(self.NS[r:r + 1, :], z[:])
            cur_m = None
            for tt_ in range(NT):
                m = 0 if tt_ < 32 else 1
                if m != cur_m:
                    self.set_mod(i, m, 4, 2, 3, 5, 3, 1.0)
                    cur_m = m
                x = self.xt[tt_ % 2]
                self.load(x[:], src(tt_))
                self.norm_tile(x, nf[tt_ % 2][:])
                r0 = nsrow_of_tile(tt_)
                self.store(self.NS[r0:r0 + 128, :], nf[tt_ % 2][:])
        S.barrier()

    def alloc_shift(self, st):
        self.ncur = self.sb(st, "ncur", [128, D], F32)
        self.nprev = self.sb(st, "nprev", [128, D], F32)
        self.nnext = self.sb(st, "nnext", [128, D], F32)
        self.nbb = self.sb(st, "nbb", [128, D], BF16)
        self.xxb = self.sb(st, "xxb", [128, D], BF16)
        self.nTx = self.sb(st, "nTx", [128, 16, 128], BF16)
        self.tmpa = self.sb(st, "tmpa", [128, D], F32)
        self.tmpb = self.sb(st, "tmpb", [128, D], F32)
        self.pT = self.ps(st, "pT", [128, 8, 128], BF16)
        self.hstat = self.sb(st, "hstat", [128, 16], F32)
        self.muT = self.sb(st, "muT", [128, 8, 6], F32)

    def load_mu(self, j):
        for m in range(6):
            for kc in range(8):
                self.load(self.muT[:, kc, m:m + 1],
                          self.I["rwkv_mu"][j, m, kc * 128:(kc + 1) * 128].rearrange("(p o) -> p o", o=1), part=True)

    def shift_tile(self, tt_):
        r0 = nsrow_of_tile(tt_)
        self.load(self.ncur[:], self.NS[r0:r0 + 128, :])
        self.load(self.nprev[:], self.NS[r0 - 1:r0 + 127, :])
        self.load(self.nnext[:], self.NS[r0 + 1:r0 + 129, :])
        self.tt("pool", self.tmpa[:], self.nprev[:], self.nnext[:], ALU.add)
        self.stt(self.xxb[:], self.tmpa[:], 0.5, self.ncur[:], ALU.mult, ALU.subtract)
        self.cp("act", self.nbb[:], self.ncur[:])
        for half, srcb in ((0, self.nbb), (1, self.xxb)):
            for kc in range(8):
                self.tr(self.pT[:, kc, :], srcb[:, kc * 128:(kc + 1) * 128], self.ident_b[:], inc=(kc == 7))
            self.cp("act", self.nTx[:, half * 8:(half + 1) * 8, :], self.pT[:, :, :])

    def load_mixed_w(self, dst, col0, ncol, src, m):
        sv = src.rearrange("(k p) n -> p k n", p=128)
        for kc in range(8):
            self.load(dst[:, kc, col0:col0 + ncol], sv[:, kc, :], q="pool", part=True)
            self.load(dst[:, 8 + kc, col0:col0 + ncol], sv[:, kc, :], q="pool", part=True)
        for kc in range(8):
            self.ts("dve" if kc % 2 else "pool", dst[:, 8 + kc, col0:col0 + ncol], dst[:, 8 + kc, col0:col0 + ncol],
                    self.muT[:, kc, m:m + 1], None, ALU.mult)

    def rwkv_feat1(self, j):
        S, I = self.S, self.I
        with ExitStack() as st:
            self.alloc_shift(st)
            self.load_mu(j)
            Wbig = self.sb(st, "Wbig", [128, 16, 3 * D], BF16)
            self.load_mixed_w(Wbig, 0, D, I["rwkv_wr"][j], 0)
            self.load_mixed_w(Wbig, D, D, I["rwkv_wk"][j], 2)
            self.load_mixed_w(Wbig, 2 * D, D, I["rwkv_wv"][j], 3)
            kk_b = self.sb(st, "kk_b", [128, D], F32)
            self.bload(kk_b, I["rwkv_k_k"][j])
            if j > 0:
                WLv = self.sb(st, "WLv", [128, 16, 32], BF16)
                self.load_mixed_w(WLv, 0, 32, I["rwkv_v1"][j - 1], 3)
                v2s = self.sb(st, "v2s", [32, D], BF16)
                self.load(v2s[:], I["rwkv_v2"][j - 1], q="pool", part=True)
                v0b = self.sb(st, "v0b", [128, D], F32)
                self.bload(v0b, I["rwkv_v0"][j - 1])
                hvb = self.sb(st, "hvb", [32, 128], BF16)
                ph = self.ps(st, "ph", [128, 512], F32)
            ob = [self.sb(st, "ob%d" % a, [128, D], F32) for a in range(4)]
            pys = [self.ps(st, "pya", [128, D], F32), self.ps(st, "pyb", [128, D], F32)]
            VAj = self.VA[j]
            for tt_ in range(NT):
                pos0 = pos_of_tile(tt_)
                self.shift_tile(tt_)
                for qi in range(3):
                    py = pys[qi % 2]
                    for hf in range(2):
                        for c in range(16):
                            self.mm(py[:, hf * 512:(hf + 1) * 512], self.nTx[:, c, :],
                                    Wbig[:, c, qi * D + hf * 512:qi * D + (hf + 1) * 512], c == 0, c == 15)
                    if qi == 0:
                        self.cp("act", ob[0][:], py[:])
                        self.store(self.hm_tile(self.RH, pos0), self.v3(ob[0][:]))
                    elif qi == 1:
                        kraw = ob[1]
                        self.cp("act", kraw[:], py[:])
                        self.store(self.KRAW[pos0:pos0 + 128, :], kraw[:])
                        self.tt("dve", self.tmpb[:], kraw[:], kk_b[:], ALU.mult)
                        self.tt("pool", self.tmpa[:], self.tmpb[:], self.tmpb[:], ALU.mult)
                        hs = self.hstat
                        self.red(hs[:, 0:16], self.v3(self.tmpa[:]))
                        self.tt("pool", hs[:, 0:16], hs[:, 0:16], self.chalf[:, 0:16], ALU.pow)
                        self.ts("dve", hs[:, 0:16], hs[:, 0:16], 1e-12, None, ALU.max)
                        self.recip(hs[:, 0:16], hs[:, 0:16])
                        self.ts("dve", hs[:, 0:16], hs[:, 0:16], -1.0, None, ALU.mult)
                        rb = A(hs.t[:, 0:16].unsqueeze(2).to_broadcast([128, NH, HD]), hs.k)
                        self.tt("dve", self.v3(ob[2][:]), self.v3(self.tmpb[:]), rb, ALU.mult)
                        self.store(self.hm_tile(self.AVH, pos0), self.v3(ob[2][:]))
                    else:
                        vf = ob[3]
                        self.cp("act", vf[:], py[:])
                        if j > 0:
                            for c in range(16):
                                self.mm(ph[0:32, 0:128], WLv[:, c, :], self.nTx[:, c, :], c == 0, c == 15)
                            self.cp("act", hvb[:], ph[0:32, 0:128])
                            py2 = pys[0]
                            for hf in range(2):
                                self.mm(py2[:, hf * 512:(hf + 1) * 512], hvb[:], v2s[:, hf * 512:(hf + 1) * 512], True, True)
                            self.tt("dve", self.tmpa[:], py2[:], v0b[:], ALU.add)
                            self.act(self.tmpa[:], self.tmpa[:], AF.Sigmoid)
                            self.load(self.tmpb[:], self.VA[0][pos0:pos0 + 128, :])
                            self.tt("pool", self.tmpb[:], self.tmpb[:], vf[:], ALU.subtract)
                            self.tt("dve", self.tmpb[:], self.tmpb[:], self.tmpa[:], ALU.mult)
                            self.tt("pool", vf[:], vf[:], self.tmpb[:], ALU.add)
                        self.store(VAj[pos0:pos0 + 128, :], vf[:])
        S.barrier()

    def rwkv_feat2(self, j):
        S, I = self.S, self.I
        with ExitStack() as st:
            self.alloc_shift(st)
            self.load_mu(j)
            WL1 = self.sb(st, "WL1", [128, 16, 576], BF16)
            for d in range(2):
                self.load_mixed_w(WL1, d * 64, 64, I["rwkv_w1"][j, d], 1)
                self.load_mixed_w(WL1, 128 + d * 64, 64, I["rwkv_a1"][j, d], 4)
                self.load_mixed_w(WL1, 256 + d * 160, 160, I["rwkv_g1"][j, d], 5)
            w2s = self.sb(st, "w2s", [128, D], BF16)
            a2s = self.sb(st, "a2s", [128, D], BF16)
            g2s = self.sb(st, "g2s", [128, 2, 2, D], BF16)
            self.load(w2s[:], I["rwkv_w2"][j].rearrange("d r n -> (d r) n"), q="pool", part=True)
            self.load(a2s[:], I["rwkv_a2"][j].rearrange("d r n -> (d r) n"), q="pool", part=True)
            for d in range(2):
                self.load(g2s[:, d, 0, :], I["rwkv_g2"][j, d, 0:128, :], q="pool", part=True)
                self.load(g2s[0:32, d, 1, :], I["rwkv_g2"][j, d, 128:160, :], q="pool", part=True)
            w0b = self.sb(st, "w0b", [128, 2, D], F32)
            a0b = self.sb(st, "a0b", [128, 2, D], F32)
            kab = self.sb(st, "kab", [128, D], F32)
            for d in range(2):
                self.load(w0b[:, d, :], I["rwkv_w0"][j, d].partition_broadcast(128), part=True)
                self.load(a0b[:, d, :], I["rwkv_a0"][j, d].partition_broadcast(128), part=True)
            self.bload(kab, I["rwkv_k_a"][j])
            hwT = self.sb(st, "hwT", [128, 128], BF16)
            haT = self.sb(st, "haT", [128, 128], BF16)
            hgT = [self.sb(st, "hgT%d" % d, [128, 2, 128], BF16) for d in range(2)]
            kraw = self.sb(st, "kraw", [128, D], F32)
            av = self.sb(st, "av", [128, D], F32)
            ob = [self.sb(st, "ob%d" % a, [128, D], F32) for a in range(4)]
            pys = [self.ps(st, "pya", [128, D], F32), self.ps(st, "pyb", [128, D], F32)]
            phs = [self.ps(st, "ph%d" % a, [128, 512], F32) for a in range(2)]
            for tt_ in range(NT):
                pos0 = pos_of_tile(tt_)
                self.shift_tile(tt_)
                self.load(kraw[:], self.KRAW[pos0:pos0 + 128, :])
                self.load(self.v3(av[:]), self.hm_tile(self.AVH, pos0))
                for c in range(16):
                    self.mm(phs[0][:, 0:128], WL1[:, c, 0:128], self.nTx[:, c, :], c == 0, c == 15)
                self.act(hwT[:], phs[0][:, 0:128], AF.Tanh)
                for c in range(16):
                    self.mm(phs[1][:, 0:128], WL1[:, c, 128:256], self.nTx[:, c, :], c == 0, c == 15)
                self.cp("act", haT[:], phs[1][:, 0:128])
                for d in range(2):
                    g0 = 256 + d * 160
                    for c in range(16):
                        self.mm(phs[0][:, 0:128], WL1[:, c, g0:g0 + 128], self.nTx[:, c, :], c == 0, c == 15)
                    self.act(hgT[d][:, 0, :], phs[0][:, 0:128], AF.Sigmoid)
                    for c in range(16):
                        self.mm(phs[1][0:32, 0:128], WL1[:, c, g0 + 128:g0 + 160], self.nTx[:, c, :], c == 0, c == 15)
                    self.act(hgT[d][0:32, 1, :], phs[1][0:32, 0:128], AF.Sigmoid)
                for d in range(2):
                    ps_ = slice(d * 64, (d + 1) * 64)
                    py = pys[0]
                    for hf in range(2):
                        self.mm(py[:, hf * 512:(hf + 1) * 512], hwT[ps_, :], w2s[ps_, hf * 512:(hf + 1) * 512], True, True)
                    self.tt("dve", self.tmpa[:], py[:], w0b[:, d, :], ALU.add)
                    self.act(self.tmpa[:], self.tmpa[:], AF.Sigmoid)
                    self.act(ob[0][:], self.tmpa[:], AF.Exp, scale=-0.6065306597126334)
                    self.store(self.hm_tile(self.WH[d], pos0), self.v3(ob[0][:]))
                    py = pys[1]
                    for hf in range(2):
                        self.mm(py[:, hf * 512:(hf + 1) * 512], haT[ps_, :], a2s[ps_, hf * 512:(hf + 1) * 512], True, True)
                    self.tt("dve", self.tmpb[:], py[:], a0b[:, d, :], ALU.add)
                    self.act(self.tmpb[:], self.tmpb[:], AF.Sigmoid)
                    self.stt(ob[1][:], av[:], -1.0, self.tmpb[:], ALU.mult, ALU.mult)
                    self.store(self.hm_tile(self.BH[d], pos0), self.v3(ob[1][:]))
                    self.stt(self.tmpb[:], self.tmpb[:], -1.0, kab[:], ALU.add, ALU.mult)
                    self.tt("pool", self.tmpb[:], self.tmpb[:], kraw[:], ALU.mult)
                    self.tt("dve", ob[2][:], self.tmpb[:], kraw[:], ALU.add)
                    self.store(self.hm_tile(self.KH[d], pos0), self.v3(ob[2][:]))
                    py = pys[0]
                    for hf in range(2):
                        cs = slice(hf * 512, (hf + 1) * 512)
                        self.mm(py[:, cs], hgT[d][:, 0, :], g2s[:, d, 0, cs], True, False)
                        self.mm(py[:, cs], hgT[d][0:32, 1, :], g2s[0:32, d, 1, cs], False, True)
                    self.cp("act", ob[3][:], py[:])
                    self.store(self.GT[d][pos0:pos0 + 128, :], ob[3][:])
        S.barrier()

    def rwkv_scan(self, j, nblocks=None):
        S = self.S
        TB = 16
        with ExitStack() as st:
            def per_d(name, shape):
                return [self.sb(st, "%s%d" % (name, d), shape, F32) for d in range(2)]
            St = per_d("sc_St", [128, 8, 64])
            t1 = per_d("sc_t1", [128, 8, 64])
            t2 = per_d("sc_t2", [128, 8, 64])
            Sw = per_d("sc_sw", [128, 8, 64])
            kv = [per_d("sc_kv%d_" % a, [128, 8, 64]) for a in range(2)]
            sa = per_d("sc_sa", [128, 8])
            qs = ("w", "b", "k", "a", "r")
            bufs = [{q: per_d("sc_%s%d_" % (q, bi), [128, TB, 64]) for q in qs} for bi in range(2)]
            vbuf = [per_d("sc_v%d_" % bi, [128, TB, 8]) for bi in range(2)]
            ybuf = [per_d("sc_y%d_" % bi, [128, TB, 8]) for bi in range(2)]
            for d in range(2):
                self.memset("dve", St[d][:], 0.0)
            VAj = self.VA[j]
            NB = NPOS // TB if nblocks is None else nblocks

            def lo_of(b):
                return (TB * b, (240 - TB * b) if b < 16 else (4592 - TB * b))

            def loads(b):
                bi = b % 2
                for d, lo in enumerate(lo_of(b)):
                    arrs = {"w": self.WH[d], "b": self.BH[d], "k": self.KH[d], "a": self.AVH, "r": self.RH}
                    for q in qs:
                        arr = arrs[q]
                        src = bass.AP(tensor=arr.tensor, offset=arr.offset + lo * 64,
                                      ap=[[NPOS * 64, 16], [0, 8], [1, TB * 64]])
                        dstb = bufs[bi][q][d]
                        self.load(A(dstb.t[:, :, :].rearrange("p t k -> p (t k)"), dstb.k), src)
                    self.load(vbuf[bi][d][:, :, :], VAj[lo:lo + TB, :].rearrange("t (p l) -> p t l", l=8))

            loads(0)
            for b in range(NB):
                bi = b % 2
                if b + 1 < NB:
                    loads(b + 1)
                for s in range(TB):
                    cc = (s, TB - 1 - s)

                    def opnd(q, d):
                        tb_ = bufs[bi][q][d]
                        return A(tb_.t[:, cc[d], :].unsqueeze(1).to_broadcast([128, 8, 64]), tb_.k)

                    def vb(d):
                        tb_ = vbuf[bi][d]
                        return A(tb_.t[:, cc[d], :].unsqueeze(2).to_broadcast([128, 8, 64]), tb_.k)

                    def sab(d):
                        return A(sa[d].t[:, :].unsqueeze(2).to_broadcast([128, 8, 64]), sa[d].k)
                    kvt = kv[s % 2]
                    R2 = range(2)
                    for d in R2:
                        self.tt("pool", kvt[d][:], vb(d), opnd("k", d), ALU.mult)
                    for d in R2:
                        self.tt("dve", t1[d][:], St[d][:], opnd("a", d), ALU.mult)
                    for d in R2:
                        self.red(sa[d][:], t1[d][:])
                    for d in R2:
                        self.tt("pool", Sw[d][:], St[d][:], opnd("w", d), ALU.mult)
                    for d in R2:
                        self.tt("dve", t2[d][:], sab(d), opnd("b", d), ALU.mult)
                    for d in R2:
                        self.tt("pool", Sw[d][:], Sw[d][:], kvt[d][:], ALU.add)
                    for d in R2:
                        self.tt("dve", St[d][:], Sw[d][:], t2[d][:], ALU.add)
                    for d in R2:
                        self.tt("dve", t1[d][:], St[d][:], opnd("r", d), ALU.mult)
                    for d in R2:
                        self.red(ybuf[bi][d][:, cc[d], :], t1[d][:])
                for d, lo in enumerate(lo_of(b)):
                    self.store(self.YT[d][lo:lo + TB, :].rearrange("t (p l) -> p t l", l=8), ybuf[bi][d][:, :, :])
        S.barrier()

    def rwkv_readout(self, i, j, src, dst, tiles):
        S, I = self.S, self.I
        with ExitStack() as st:
            self.alloc_common(st, nx=2)
            self.hstat = self.sb(st, "hstat", [128, 32], F32)
            Wo = self.sb(st, "Wo", [128, 8, D], BF16)
            self.load_w_bf16(Wo, I["rwkv_wo"][j], 8)
            lnw = self.sb(st, "lnw", [128, 2, D], F32)
            lnb = self.sb(st, "lnb", [128, 2, D], F32)
            rkb = self.sb(st, "rkb", [128, D], F32)
            for d in range(2):
                self.load(lnw[:, d, :], I["rwkv_ln_w"][j, d].partition_broadcast(128), part=True)
                self.load(lnb[:, d, :], I["rwkv_ln_b"][j, d].partition_broadcast(128), part=True)
            self.bload(rkb, I["rwkv_r_k"][j].rearrange("h k -> (h k)"))
            rt = self.sb(st, "rt", [128, D], F32)
            vt = self.sb(st, "vt", [128, D], F32)
            yt = self.sb(st, "yt", [128, D], F32)
            kt_ = self.sb(st, "kt", [128, D], F32)
            gt = self.sb(st, "gt", [128, D], F32)
            oacc = self.sb(st, "oacc", [128, D], F32)
            obf = self.sb(st, "obf", [128, D], BF16)
            OT = self.sb(st, "OT", [128, 8, 128], BF16)
            hs = self.hstat
            VAj = self.VA[j]
            cur_m = None
            for tt_ in tiles:
                m = 0 if tt_ < 32 else 1
                if m != cur_m:
                    self.set_mod(i, m, 4, 2, 3, 5, 3, 1.0)
                    cur_m = m
                pos0 = pos_of_tile(tt_)
                self.load(self.v3(rt[:]), self.hm_tile(self.RH, pos0))
                self.load(vt[:], VAj[pos0:pos0 + 128, :])
                for d in range(2):
                    self.load(yt[:], self.YT[d][pos0:pos0 + 128, :])
                    self.load(self.v3(kt_[:]), self.hm_tile(self.KH[d], pos0))
                    self.load(gt[:], self.GT[d][pos0:pos0 + 128, :])
                    ta, tb = self.tmpa, self.tmpb
                    self.red(hs[:, 0:16], self.v3(yt[:]))
                    self.ts("dve", hs[:, 0:16], hs[:, 0:16], 1.0 / HD, None, ALU.mult)
                    mb_ = A(hs.t[:, 0:16].unsqueeze(2).to_broadcast([128, NH, HD]), hs.k)
                    self.tt("dve", self.v3(ta[:]), self.v3(yt[:]), mb_, ALU.subtract)
                    self.tt("pool", tb[:], ta[:], ta[:], ALU.mult)
                    self.red(hs[:, 16:32], self.v3(tb[:]))
                    self.ts("dve", hs[:, 16:32], hs[:, 16:32], 1.0 / HD, GN_EPS, ALU.mult, ALU.add)
                    self.tt("pool", hs[:, 16:32], hs[:, 16:32], self.nhalf[:, 0:16], ALU.pow)
                    rb_ = A(hs.t[:, 16:32].unsqueeze(2).to_broadcast([128, NH, HD]), hs.k)
                    self.tt("dve", self.v3(ta[:]), self.v3(ta[:]), rb_, ALU.mult)
                    self.tt("pool", ta[:], ta[:], lnw[:, d, :], ALU.mult)
                    self.tt("dve", ta[:], ta[:], lnb[:, d, :], ALU.add)
                    self.tt("pool", tb[:], rt[:], kt_[:], ALU.mult)
                    self.tt("dve", tb[:], tb[:], rkb[:], ALU.mult)
                    self.red(hs[:, 0:16], self.v3(tb[:]))
                    bb_ = A(hs.t[:, 0:16].unsqueeze(2).to_broadcast([128, NH, HD]), hs.k)
                    self.tt("dve", self.v3(tb[:]), self.v3(vt[:]), bb_, ALU.mult)
                    self.tt("pool", ta[:], ta[:], tb[:], ALU.add)
                    if d == 0:
                        self.tt("dve", oacc[:], ta[:], gt[:], ALU.mult)
                    else:
                        self.tt("dve", ta[:], ta[:], gt[:], ALU.mult)
                        self.tt("pool", obf[:], ta[:], oacc[:], ALU.add)
                self.transpose_cols(OT, 0, obf)
                for hf in range(2):
                    for kc in range(8):
                        self.mm(self.py[:, hf * 512:(hf + 1) * 512], OT[:, kc, :], Wo[:, kc, hf * 512:(hf + 1) * 512],
                                kc == 0, kc == 7)
                x = self.xt[0]
                self.load(x[:], src(tt_))
                self.resid_tile(x, dst(tt_))
        S.barrier()

    def build(self):
        nc = self.nc
        cfg = self.cfg
        shapes = dict(
            x=[SEQ, D], ctx=[CTXL, D], c=[D], c_ctx=[D], ident=[128, 128], wmask=[128, 384],
            rope_cos=[SEQ, 32], rope_sin=[SEQ, 32],
            mod_w=[DEPTH, D, NMOD * D], mod_b=[DEPTH, NMOD * D], norm_g=[DEPTH, 6, D],
            ffn_w1=[DEPTH, 2, D, DFF], ffn_w3=[DEPTH, 2, D, DFF], ffn_w2=[DEPTH, 2, DFF, D],
            rwkv_mu=[2, 6, D], rwkv_wr=[2, D, D], rwkv_wk=[2, D, D], rwkv_wv=[2, D, D], rwkv_wo=[2, D, D],
            rwkv_k_k=[2, D], rwkv_k_a=[2, D], rwkv_r_k=[2, 16, 64], rwkv_w0=[2, 2, D], rwkv_w1=[2, 2, D, 64],
            rwkv_w2=[2, 2, 64, D], rwkv_a0=[2, 2, D], rwkv_a1=[2, 2, D, 64], rwkv_a2=[2, 2, 64, D],
            rwkv_g1=[2, 2, D, 160], rwkv_g2=[2, 2, 160, D], rwkv_ln_w=[2, 2, D], rwkv_ln_b=[2, 2, D],
            rwkv_v0=[1, D], rwkv_v1=[1, D, 32], rwkv_v2=[1, 32, D],
            gattn_wq=[1, D, D], gattn_wk=[1, D, DKV], gattn_wv=[1, D, DKV], gattn_wo=[1, D, D],
            gattn_q_norm=[1, HD], gattn_k_norm=[1, HD],
            wattn_wq=[1, D, D], wattn_wk=[1, D, DKV], wattn_wv=[1, D, DKV], wattn_wo=[1, D, D], wattn_sink=[1, NH],
        )
        for k, shp in shapes.items():
            self.dram_in(k, shp)
        self.Y = nc.dram_tensor("y", [SEQ, D], F32, kind="ExternalOutput").ap()
        self.XS = self.scratch("xs", [NTOK, D])
        self.MOD = self.scratch("modv", [DEPTH, 2, NMOD * D])
        self.NS = self.scratch("ns", [NSROWS, D])
        self.RH = self.scratch("rh", [NH, NPOS, HD])
        self.AVH = self.scratch("avh", [NH, NPOS, HD])
        self.WH = [self.scratch("wh%d" % d, [NH, NPOS, HD]) for d in range(2)]
        self.BH = [self.scratch("bh%d" % d, [NH, NPOS, HD]) for d in range(2)]
        self.KH = [self.scratch("kh%d" % d, [NH, NPOS, HD]) for d in range(2)]
        self.KRAW = self.scratch("kraw_d", [NPOS, D])
        self.VA = [self.scratch("va%d" % a, [NPOS, D]) for a in range(2)]
        self.GT = [self.scratch("gt%d" % d, [NPOS, D]) for d in range(2)]
        self.YT = [self.scratch("yt%d" % d, [NPOS, D]) for d in range(2)]
        dbg = cfg.get("dbg", [])
        self.DBG = {n: nc.dram_tensor("dbg_" + n, [NTOK, D], F32, kind="ExternalOutput").ap() for n in dbg}

        self.setup_consts()
        self.mod_phase()

        def src0(tt_):
            if tt_ < 32:
                return self.I["x"][tt_ * 128:(tt_ + 1) * 128, :]
            return self.I["ctx"][(tt_ - 32) * 128:(tt_ - 31) * 128, :]

        def xs(tt_):
            return self.XS[tt_ * 128:(tt_ + 1) * 128, :]

        def mk(name):
            d = self.DBG[name]
            return lambda tt_: d[tt_ * 128:(tt_ + 1) * 128, :]

        def yout(tt_):
            return self.Y[tt_ * 128:(tt_ + 1) * 128, :]

        l0, l1 = cfg.get("l0", 0), cfg.get("l1", DEPTH)
        ft = cfg.get("ffn_tiles")
        for i in range(l0, l1):
            last = i == DEPTH - 1
            kind, j = i % 3, i // 3
            alltiles = list(range(NT))
            s_in = src0 if i == l0 else xs
            if not cfg.get("skip_f1"):
                self.ffn_phase(i, 0, s_in, xs, alltiles if ft is None else ft)
                s_in = xs
            if cfg.get("stop") == "f1":
                break
            mt = list(range(32)) if last else alltiles
            if kind == 0:
                self.rwkv_norm_pass(i, s_in)
                self.rwkv_feat1(j)
                self.rwkv_feat2(j)
                self.rwkv_scan(j, cfg.get("nblocks"))
                self.rwkv_readout(i, j, s_in, xs, mt if cfg.get("mix_tiles") is None else cfg["mix_tiles"])
            elif kind == 1:
                self.gattn_phase(i, j, s_in, xs, not last, cfg.get("qgroups"))
            else:
                self.wattn_phase(i, j, s_in, xs, not last, cfg.get("qtiles"))
            if cfg.get("stop") == "mix":
                break
            self.ffn_phase(i, 1, xs, yout if last else xs, mt if ft is None else ft)
        if dbg:
            with ExitStack() as st:
                buf = self.sb(st, "dbgbuf", [128, D], F32)
                for n in dbg:
                    srcarr = {"xs": self.XS}.get(n)
                    if srcarr is None:
                        continue
                    for tt_ in cfg.get("dbg_tiles", range(NT)):
                        self.load(buf[:], srcarr[tt_ * 128:(tt_ + 1) * 128, :])
                        self.store(self.DBG[n][tt_ * 128:(tt_ + 1) * 128, :], buf[:])
            self.S.barrier()
        self.S.barrier()
        self.gs.close()
        return nc


def build_nc(cfg):
    nc = bass.Bass("TRN2", target_bir_lowering=False)
    kb = KB(nc, cfg)
    kb.build()
    print("instructions:", kb.S.nins, "waits:", kb.S.nwaits, "dma sems:", len(kb.S.dsem))
    return nc, list(kb.I.keys())


def host_consts():
    ident = np.eye(128, dtype=np.float32)
    pos = np.arange(SEQ)
    row = (pos // 64).astype(np.float32)
    col = (pos % 64).astype(np.float32)
    inv_freq = (np.float32(10000.0) ** (-np.arange(16, dtype=np.float32) / np.float32(16))).astype(np.float32)
    ang = np.stack([row, col], axis=-1)[:, :, None] * inv_freq
    rc = np.cos(ang).astype(np.float32).reshape(SEQ, 32)
    rs = np.sin(ang).astype(np.float32).reshape(SEQ, 32)
    qq = np.arange(128)[:, None]
    kk = np.arange(128)[None, :]
    maskL = np.where(kk >= qq, 0.0, MASKV).astype(np.float32)
    maskR = np.where(kk <= qq, 0.0, MASKV).astype(np.float32)
    wmask = np.concatenate([maskL, np.zeros((128, 128), np.float32), maskR], axis=1)
    return dict(ident=ident, rope_cos=rc, rope_sin=rs, wmask=wmask)


def make_in_maps(inputs, names, ncores=8):
    consts = host_consts()
    maps = []
    for b in range(ncores):
        m = {}
        for k, v in inputs.items():
            if k not in names:
                continue
            v = np.ascontiguousarray(v, dtype=np.float32)
            if k in ("x", "c", "ctx"):
                m[k] = np.ascontiguousarray(v[b])
            else:
                m[k] = v
        for k, v in consts.items():
            if k in names:
                m[k] = v
        maps.append(m)
    return maps


def kernel(**inputs):
    nc, names = build_nc({})
    maps = make_in_maps(inputs, names, 8)
    res = run_bass_kernel_spmd(nc, maps, core_ids=list(range(8)))
    return np.stack([r["y"] for r in res.results], axis=0).astype(np.float32)
```

```python
import numpy as np
from contextlib import ExitStack
import concourse.bass as bass
import concourse.mybir as mybir
from concourse.bass_utils import run_bass_kernel_spmd

F32 = mybir.dt.float32
BF16 = mybir.dt.bfloat16
AF = mybir.ActivationFunctionType
ALU = mybir.AluOpType
AX = mybir.AxisListType

D = 1024
SEQ = 4096
CTXL = 256
NTOK = SEQ + CTXL
NT = NTOK // 128
DEPTH = 4
DFF = 2816
NFC = DFF // 128
NMOD = 9
EPS = 1e-6


class Sched:
    def __init__(self, nc):
        self.nc = nc
        self.stack = ExitStack()
        self.eng = {}
        for name, e in (("pe", nc.tensor), ("dve", nc.vector), ("act", nc.scalar),
                        ("pool", nc.gpsimd), ("sp", nc.sync)):
            sem = self.stack.enter_context(nc.semaphore("s_" + name))
            self.eng[name] = dict(e=e, sem=sem, count=0, waited={}, pending=False)
        self.lastw = {}
        self.readers = {}
        self.dsem = {}
        self.nwaits = 0
        self.nins = 0

    def _wait(self, E, need):
        for sid, (sem, val) in need.items():
            if E["waited"].get(sid, 0) < val:
                E["e"].wait_ge(sem, val)
                E["waited"][sid] = val
                self.nwaits += 1

    @staticmethod
    def _merge(need, toks, skip=None):
        for sid, (sem, val) in toks.items():
            if sid == skip:
                continue
            if sid not in need or need[sid][1] < val:
                need[sid] = (sem, val)

    def op(self, eng, fn, reads=(), writes=(), inc=True):
        E = self.eng[eng]
        need = {}
        skip = "pe" if eng == "pe" else None
        for k in reads:
            self._merge(need, self.lastw.get(k, {}), skip)
        for k in writes:
            self._merge(need, self.lastw.get(k, {}), skip)
            self._merge(need, self.readers.get(k, {}), skip)
        self._wait(E, need)
        ins = fn(E["e"])
        self.nins += 1
        if inc:
            E["count"] += 1
            ins.then_inc(E["sem"], 1)
            val = E["count"]
            E["pending"] = False
        else:
            val = E["count"] + 1
            E["pending"] = True
        tok = (E["sem"], val)
        for k in reads:
            self.readers.setdefault(k, {})[eng] = tok
        for k in writes:
            self.lastw[k] = {eng: tok}
            self.readers[k] = {}
        return ins

    def dma(self, q, out, in_, key, load, part=False, **kw):
        E = self.eng[q]
        sid = ("dma", key)
        need = {}
        self._merge(need, self.lastw.get(key, {}), sid if (load and part) else None)
        if load:
            self._merge(need, self.readers.get(key, {}), None)
        self._wait(E, need)
        if key not in self.dsem:
            sem = self.stack.enter_context(self.nc.semaphore("d_%d" % len(self.dsem)))
            self.dsem[key] = [sem, 0]
        ds = self.dsem[key]
        ins = E["e"].dma_start(out=out, in_=in_, **kw)
        self.nins += 1
        ds[1] += 16
        ins.then_inc(ds[0], 16)
        tok = (ds[0], ds[1])
        if load:
            self.lastw[key] = {sid: tok}
            self.readers[key] = {}
        else:
            self.readers.setdefault(key, {})[sid] = tok
        return ins

    def barrier(self):
        toks = {}
        for name, E in self.eng.items():
            assert not E["pending"], name
            if E["count"] > 0:
                toks[name] = (E["sem"], E["count"])
        for key, ds in self.dsem.items():
            if ds[1] > 0:
                toks[("dma", key)] = (ds[0], ds[1])
        for name, E in self.eng.items():
            self._wait(E, toks)
        self.lastw = {}
        self.readers = {}


HD = 64
NH = 16
NKV = 4
DKV = 256
ATTN_SCALE = HD ** -0.5
NPOS = NTOK
NSROWS = NTOK + 3
GN_EPS = 64e-5
MASKV = -30000.0


class A:
    __slots__ = ("ap", "k")

    def __init__(self, ap, k):
        self.ap = ap
        self.k = k


class T:
    def __init__(self, t, k):
        self.t = t
        self.k = k

    def __getitem__(self, idx):
        return A(self.t[idx], self.k)

    def cust(self, offset, dims):
        return A(bass.AP(tensor=self.t[:].tensor, offset=offset, ap=[list(d) for d in dims]), self.k)


def _ap(x):
    return x.ap if isinstance(x, A) else x


def _keys(*xs):
    return [x.k for x in xs if isinstance(x, A)]


def pos_of_tile(tt):
    return 256 + tt * 128 if tt < 32 else (tt - 32) * 128


def nsrow_of_tile(tt):
    return 1 + tt * 128 if tt < 32 else 4098 + (tt - 32) * 128


class KB:
    def __init__(self, nc, cfg):
        self.nc = nc
        self.cfg = cfg
        self.S = Sched(nc)
        self.gs = self.S.stack
        self.I = {}

    def dram_in(self, name, shape):
        self.I[name] = self.nc.dram_tensor(name, list(shape), F32, kind="ExternalInput").ap()
        return self.I[name]

    def scratch(self, name, shape):
        return self.nc.dram_tensor(name, list(shape), F32, kind="Internal").ap()

    def sb(self, st, name, shape, dt=F32):
        self.uid = getattr(self, "uid", 0) + 1
        return T(st.enter_context(self.nc.sbuf_tensor("%s_%d" % (name, self.uid), list(shape), dt)), name)

    def ps(self, st, name, shape, dt=F32):
        self.uid = getattr(self, "uid", 0) + 1
        return T(st.enter_context(self.nc.psum_tensor("%s_%d" % (name, self.uid), list(shape), dt)), name)

    def tt(self, eng, out, a, b, op):
        self.S.op(eng, lambda e: e.tensor_tensor(out=out.ap, in0=a.ap, in1=b.ap, op=op),
                  reads=_keys(a, b), writes=[out.k])

    def ts(self, eng, out, a, s1, s2, op0, op1=None):
        if op1 is None:
            f = lambda e: e.tensor_scalar(out=out.ap, in0=a.ap, scalar1=_ap(s1), scalar2=None, op0=op0)
        else:
            f = lambda e: e.tensor_scalar(out=out.ap, in0=a.ap, scalar1=_ap(s1), scalar2=_ap(s2), op0=op0, op1=op1)
        self.S.op(eng, f, reads=_keys(a, s1, s2), writes=[out.k])

    def stt(self, out, a, s, b, op0, op1):
        self.S.op("dve", lambda e: e.scalar_tensor_tensor(out=out.ap, in0=a.ap, scalar=_ap(s), in1=b.ap,
                                                          op0=op0, op1=op1),
                  reads=_keys(a, s, b), writes=[out.k])

    def act(self, out, a, func, scale=1.0, bias=None, accum=None):
        kw = {}
        if bias is not None:
            kw["bias"] = _ap(bias)
        if accum is not None:
            kw["accum_out"] = accum.ap
        w = [out.k] + ([accum.k] if accum is not None else [])
        self.S.op("act", lambda e: e.activation(out=out.ap, in_=a.ap, func=func, scale=_ap(scale), **kw),
                  reads=_keys(a, bias, scale), writes=w)

    def red(self, out, a, op=ALU.add):
        self.S.op("dve", lambda e: e.tensor_reduce(out=out.ap, in_=a.ap, axis=AX.X, op=op),
                  reads=[a.k], writes=[out.k])

    def recip(self, out, a):
        self.S.op("dve", lambda e: e.reciprocal(out=out.ap, in_=a.ap), reads=[a.k], writes=[out.k])

    def cp(self, eng, out, a):
        if eng == "act":
            f = lambda e: e.copy(out=out.ap, in_=a.ap)
        else:
            f = lambda e: e.tensor_copy(out=out.ap, in_=a.ap)
        self.S.op(eng, f, reads=[a.k], writes=[out.k])

    def memset(self, eng, out, val):
        self.S.op(eng, lambda e: e.memset(out.ap, val), writes=[out.k])

    def mm(self, out, lhsT, rhs, start, stop, inc=None):
        self.S.op("pe", lambda e: e.matmul(out.ap, lhsT=lhsT.ap, rhs=rhs.ap, start=start, stop=stop),
                  reads=[lhsT.k, rhs.k], writes=[out.k], inc=(stop if inc is None else inc))

    def tr(self, out, a, ident, inc=True):
        self.S.op("pe", lambda e: e.transpose(out.ap, a.ap, ident.ap), reads=[a.k, ident.k], writes=[out.k], inc=inc)

    def load(self, dst, src_ap, q="sp", part=False, **kw):
        if q == "pool":
            kw.setdefault("max_dma_last_dim", 4096)
        self.S.dma(q, dst.ap, src_ap, dst.k, True, part=part, **kw)

    def store(self, dst_ap, src, q="sp", **kw):
        self.S.dma(q, dst_ap, src.ap, src.k, False, **kw)

    def pow_(self, out, a, expo):
        n = a.ap.shape[-1] if len(a.ap.shape) == 2 else None
        e = self.chalf if expo == 0.5 else self.nhalf
        self.tt("pool", out, a, e[:, 0:n], ALU.pow)

    def setup_consts(self):
        self.ident_f = self.sb(self.gs, "ident_f", [128, 128], F32)
        self.ident_b = self.sb(self.gs, "ident_b", [128, 128], BF16)
        self.nhalf = self.sb(self.gs, "nhalf", [128, 16], F32)
        self.chalf = self.sb(self.gs, "chalf", [128, 16], F32)
        self.ones_f = self.sb(self.gs, "ones_f", [128, 128], F32)
        self.memset("pool", self.nhalf[:], -0.5)
        self.memset("pool", self.chalf[:], 0.5)
        self.memset("pool", self.ones_f[:], 1.0)
        self.load(self.ident_f[:], self.I["ident"][:, :])
        self.cp("dve", self.ident_b[:], self.ident_f[:])

    def mod_phase(self):
        S = self.S
        with ExitStack() as st:
            craw = self.sb(st, "craw", [128, 2, 8], F32)
            sc = self.sb(st, "sc", [128, 8, 2], F32)
            wt = [self.sb(st, "modw%d" % i, [128, 8, 512], F32) for i in range(2)]
            bias = self.sb(st, "modbias", [2, NMOD * D], F32)
            res = self.sb(st, "modres", [2, NMOD * D], F32)
            pss = [self.ps(st, "modps%d" % i, [2, 512], F32) for i in range(2)]
            self.load(craw[:, 0, :], self.I["c"].rearrange("(p k) -> p k", k=8), part=True)
            self.load(craw[:, 1, :], self.I["c_ctx"].rearrange("(p k) -> p k", k=8), part=True)
            for m in range(2):
                self.act(sc[:, :, m], craw[:, m, :], AF.Silu)
            for i in range(self.cfg.get("l0", 0), self.cfg.get("l1", DEPTH)):
                wv = self.I["mod_w"][i].rearrange("(p k) n -> p k n", k=8)
                self.load(bias[:], self.I["mod_b"][i].partition_broadcast(2))
                for cb in range(18):
                    w = wt[cb % 2]
                    p = pss[cb % 2]
                    self.load(w[:], wv[:, :, cb * 512:(cb + 1) * 512])
                    for kc in range(8):
                        self.mm(p[:], sc[:, kc, :], w[:, kc, :], kc == 0, kc == 7)
                    self.tt("dve", res[:, cb * 512:(cb + 1) * 512], p[:], bias[:, cb * 512:(cb + 1) * 512], ALU.add)
                self.store(self.MOD[i], res[:])
        S.barrier()

    def bload(self, dst, row_ap):
        self.load(dst[:], row_ap.partition_broadcast(128))

    def alloc_common(self, st, nx=2):
        self.tmpa = self.sb(st, "tmpa", [128, D], F32)
        self.tmpb = self.sb(st, "tmpb", [128, D], F32)
        self.mvA = self.sb(st, "mvA", [128, D], F32)
        self.mvS = self.sb(st, "mvS", [128, D], F32)
        self.mvG = self.sb(st, "mvG", [128, D], F32)
        self.xt = [self.sb(st, "xt%d" % j, [128, D], F32) for j in range(nx)]
        self.nb = self.sb(st, "nb", [128, D], BF16)
        self.stat = self.sb(st, "stat", [128, 8], F32)
        self.pT = self.ps(st, "pT", [128, 8, 128], BF16)
        self.py = self.ps(st, "py", [128, D], F32)

    def set_mod(self, i, m, mA, gA, mS, mG, gG, cG):
        def mod(idx):
            return self.MOD[i, m, idx * D:(idx + 1) * D]
        self.bload(self.tmpa, mod(mA))
        self.bload(self.tmpb, self.I["norm_g"][i, gA])
        self.stt(self.mvA[:], self.tmpa[:], 1.0, self.tmpb[:], ALU.add, ALU.mult)
        self.bload(self.mvS, mod(mS))
        self.bload(self.tmpa, mod(mG))
        self.bload(self.tmpb, self.I["norm_g"][i, gG])
        self.stt(self.mvG[:], self.tmpa[:], float(cG), self.tmpb[:], ALU.mult, ALU.mult)

    def rms_rstd(self, src, junk, ss, rstd, n=D):
        self.act(junk, src, AF.Square, accum=ss)
        self.ts("dve", rstd, ss, 1.0 / n, EPS, ALU.mult, ALU.add)
        self.tt("pool", rstd, rstd, self.nhalf[:, 0:1], ALU.pow)

    def norm_tile(self, x, dst):
        st = self.stat
        self.rms_rstd(x[:], self.nb[:], st[:, 0:1], st[:, 1:2])
        self.stt(self.tmpa[:], x[:], st[:, 1:2], self.mvA[:], ALU.mult, ALU.mult)
        self.tt("pool", dst, self.tmpa[:], self.mvS[:], ALU.add)

    def transpose_cols(self, dstT, col0, src, nchunk=8):
        for kc in range(nchunk):
            self.tr(self.pT[:, kc, :], src[:, kc * 128:(kc + 1) * 128], self.ident_b[:], inc=(kc == nchunk - 1))
        self.cp("act", dstT[:, 0:nchunk, col0:col0 + 128], self.pT[:, 0:nchunk, :])

    def resid_tile(self, x, dst_ap):
        st = self.stat
        self.rms_rstd(self.py[:], self.tmpb[:], st[:, 2:3], st[:, 3:4])
        self.stt(self.tmpb[:], self.py[:], st[:, 3:4], self.mvG[:], ALU.mult, ALU.mult)
        self.tt("pool", x[:], self.tmpb[:], x[:], ALU.add)
        self.store(dst_ap, x[:])

    def load_w_bf16(self, dst, src, nk):
        sv = src.rearrange("(k p) n -> p k n", p=128)
        for kc in range(nk):
            self.load(dst[:, kc, :], sv[:, kc, :], q="pool", part=True)

    def ffn_phase(self, i, h, src, dst, tiles):
        S = self.S
        mb = 0 if h == 0 else 6
        gb = 0 if h == 0 else 4
        with ExitStack() as st:
            W1 = self.sb(st, "W1", [128, 8, DFF], BF16)
            W3 = self.sb(st, "W3", [128, 8, DFF], BF16)
            W2 = self.sb(st, "W2", [128, NFC, D], BF16)
            self.load_w_bf16(W1, self.I["ffn_w1"][i, h], 8)
            self.load_w_bf16(W3, self.I["ffn_w3"][i, h], 8)
            self.load_w_bf16(W2, self.I["ffn_w2"][i, h], NFC)
            self.alloc_common(st, nx=4)
            nT = [self.sb(st, "nT%d" % j, [128, 8, 256], BF16) for j in range(2)]
            gT = self.sb(st, "gT", [128, NFC, 256], BF16)
            sl = [self.sb(st, "sl%d" % j, [128, 256], BF16) for j in range(2)]
            ph = [self.ps(st, "ph%d" % j, [128, 256], F32) for j in range(4)]
            groups = []
            lat = [t for t in tiles if t < 32]
            ctx = [t for t in tiles if t >= 32]
            for lst in (lat, ctx):
                groups += [lst[a:a + 2] for a in range(0, len(lst), 2)]
            cur_m = None
            for gi_, grp in enumerate(groups):
                m = 0 if grp[0] < 32 else 1
                if m != cur_m:
                    self.set_mod(i, m, mb + 1, gb + 0, mb + 0, mb + 2, gb + 1, 0.5)
                    cur_m = m
                ntk = 128 * len(grp)
                nTg = nT[gi_ % 2]
                for j, tt_ in enumerate(grp):
                    x = self.xt[(gi_ % 2) * 2 + j]
                    self.load(x[:], src(tt_))
                    self.norm_tile(x, self.nb[:])
                    self.transpose_cols(nTg, j * 128, self.nb)
                for fc in range(NFC):
                    p1 = ph[(fc % 2) * 2]
                    p3 = ph[(fc % 2) * 2 + 1]
                    for (W, p) in ((W1, p1), (W3, p3)):
                        for kc in range(8):
                            self.mm(p[:, :ntk], W[:, kc, fc * 128:(fc + 1) * 128], nTg[:, kc, :ntk], kc == 0, kc == 7)
                    s = sl[fc % 2]
                    self.act(s[:, :ntk], p1[:, :ntk], AF.Silu)
                    self.tt("dve", gT[:, fc, :ntk], s[:, :ntk], p3[:, :ntk], ALU.mult)
                for j, tt_ in enumerate(grp):
                    x = self.xt[(gi_ % 2) * 2 + j]
                    for hf in range(2):
                        for fc in range(NFC):
                            self.mm(self.py[:, hf * 512:(hf + 1) * 512], gT[:, fc, j * 128:(j + 1) * 128],
                                    W2[:, fc, hf * 512:(hf + 1) * 512], fc == 0, fc == NFC - 1)
                    self.resid_tile(x, dst(tt_))
        S.barrier()

    def rope(self, dst, src, H, tt_):
        cs, sn = self.rcos, self.rsin
        self.load(cs[:], self.I["rope_cos"][tt_ * 128:(tt_ + 1) * 128, :])
        self.load(sn[:], self.I["rope_sin"][tt_ * 128:(tt_ + 1) * 128, :])

        def hv(t, w, dt_cols=None):
            v = t.t[:, 0:H * 64].rearrange("p (h a w f) -> p h a w f", h=H, a=2, w=2)
            return A(v[:, :, :, w, :], t.k)

        def bc(t):
            v = t.t[:, :].rearrange("p (a f) -> p a f", a=2).unsqueeze(1).to_broadcast([128, H, 2, 16])
            return A(v, t.k)

        def tv(t):
            return A(t.t[:, 0:H * 32].rearrange("p (h a f) -> p h a f", h=H, a=2), t.k)
        t1, t2 = hv(src, 0), hv(src, 1)
        c, s = bc(cs), bc(sn)
        u1, u2, u3, u4 = (tv(u) for u in self.ru)
        self.tt("dve", u1, t1, c, ALU.mult)
        self.tt("pool", u2, t2, s, ALU.mult)
        self.tt("dve", hv(dst, 0), u1, u2, ALU.subtract)
        self.tt("pool", u3, t2, c, ALU.mult)
        self.tt("dve", u4, t1, s, ALU.mult)
        self.tt("pool", hv(dst, 1), u3, u4, ALU.add)

    def alloc_rope(self, st):
        self.rcos = self.sb(st, "rcos", [128, 32], F32)
        self.rsin = self.sb(st, "rsin", [128, 32], F32)
        self.ru = [self.sb(st, "ru%d" % j, [128, 512], F32) for j in range(4)]

    def head_rms(self, dst, src, H, gvec):
        sq = self.tmpb
        self.tt("dve", sq[:, 0:H * 64], src, src, ALU.mult)
        ss = self.hstat
        self.red(ss[:, 0:H], A(sq.t[:, 0:H * 64].rearrange("p (h k) -> p h k", h=H), sq.k))
        self.ts("dve", ss[:, 0:H], ss[:, 0:H], 1.0 / HD, EPS, ALU.mult, ALU.add)
        self.tt("pool", ss[:, 0:H], ss[:, 0:H], self.nhalf[:, 0:H], ALU.pow)
        s3 = A(src.ap.rearrange("p (h k) -> p h k", h=H), src.k)
        d3 = A(dst.ap.rearrange("p (h k) -> p h k", h=H), dst.k)
        rb = A(ss.t[:, 0:H].unsqueeze(2).to_broadcast([128, H, HD]), ss.k)
        gb = A(gvec.t[:, :].unsqueeze(1).to_broadcast([128, H, HD]), gvec.k)
        self.tt("dve", d3, s3, rb, ALU.mult)
        self.tt("pool", d3, d3, gb, ALU.mult)

    def gattn_phase(self, i, j, src, dst, with_ctx, qtiles=None):
        S = self.S
        I = self.I
        with ExitStack() as st:
            Wq = self.sb(st, "Wq", [128, 8, D], BF16)
            Wkv = self.sb(st, "Wkv", [128, 8, 2 * DKV], BF16)
            Wo = self.sb(st, "Wo", [HD, NH, D], BF16)
            self.load_w_bf16(Wq, I["gattn_wq"][j], 8)
            wk = I["gattn_wk"][j].rearrange("(k p) n -> p k n", p=128)
            wv = I["gattn_wv"][j].rearrange("(k p) n -> p k n", p=128)
            for kc in range(8):
                self.load(Wkv[:, kc, 0:DKV], wk[:, kc, :], q="pool", part=True)
                self.load(Wkv[:, kc, DKV:2 * DKV], wv[:, kc, :], q="pool", part=True)
            wo = I["gattn_wo"][j].rearrange("(h p) n -> p h n", p=HD)
            for h in range(NH):
                self.load(Wo[:, h, :], wo[:, h, :], q="pool", part=True)
            self.alloc_common(st, nx=2)
            self.alloc_rope(st)
            self.hstat = self.sb(st, "hstat", [128, 16], F32)
            gq = self.sb(st, "gq", [128, HD], F32)
            gk = self.sb(st, "gk", [128, HD], F32)
            negm = self.sb(st, "negm", [128, 4], F32)
            self.bload(gq, I["gattn_q_norm"][j])
            self.bload(gk, I["gattn_k_norm"][j])
            S.op("dve", lambda e: e.tensor_reduce(out=negm.t[:, 0:1], in_=gq.t[:, :], axis=AX.X, op=ALU.max,
                                                  apply_absolute_value=True), reads=["gq"], writes=["negm"])
            S.op("dve", lambda e: e.tensor_reduce(out=negm.t[:, 1:2], in_=gk.t[:, :], axis=AX.X, op=ALU.max,
                                                  apply_absolute_value=True), reads=["gk"], writes=["negm"])
            self.tt("dve", negm[:, 2:3], negm[:, 0:1], negm[:, 1:2], ALU.mult)
            self.ts("dve", negm[:, 3:4], negm[:, 2:3], -8.0, None, ALU.mult)
            KT = self.sb(st, "KT", [HD, NKV, NTOK], BF16)
            V = self.sb(st, "V", [128, NT, NKV, HD + 1], BF16)
            self.memset("pool", V[:], 1.0)
            nT = self.sb(st, "nT", [128, 8, 128], BF16)
            qf = self.sb(st, "qf", [128, D], F32)
            qb = self.sb(st, "qb", [128, D], BF16)
            QT = self.sb(st, "QT", [HD, NH, 512], BF16)
            OT = self.sb(st, "OT", [HD, NH, 512], BF16)
            pexp = [self.sb(st, "pexp%d" % a, [128, 512], BF16) for a in range(3)]
            rd = self.sb(st, "rd", [HD + 1, 512], F32)
            bcs = self.sb(st, "bcs", [HD, 512], F32)
            pss = [self.ps(st, "pss%d" % a, [128, 512], F32) for a in range(2)]
            po = [self.ps(st, "po%d" % a, [HD + 1, 512], F32) for a in range(2)]
            pbc = self.ps(st, "pbc", [HD, 512], F32)
            x = self.xt[0]

            def prologue(tt_):
                self.load(x[:], src(tt_))
                self.norm_tile(x, self.nb[:])
                self.transpose_cols(nT, 0, self.nb)

            cur_m = None
            for tt_ in range(NT):
                m = 0 if tt_ < 32 else 1
                if m != cur_m:
                    self.set_mod(i, m, 4, 2, 3, 5, 3, 1.0)
                    cur_m = m
                prologue(tt_)
                pkv = self.py
                for kc in range(8):
                    self.mm(pkv[:, 0:512], nT[:, kc, :], Wkv[:, kc, :], kc == 0, kc == 7)
                self.cp("act", qf[:, 0:DKV], pkv[:, 0:DKV])
                self.head_rms(qf[:, 0:DKV], qf[:, 0:DKV], NKV, gk)
                if m == 0:
                    self.rope(qb, qf, NKV, tt_)
                else:
                    self.cp("act", qb[:, 0:DKV], qf[:, 0:DKV])
                for h in range(NKV):
                    self.tr(self.pT[0:HD, h, :], qb[:, h * HD:(h + 1) * HD], self.ident_b[:], inc=(h == NKV - 1))
                self.cp("act", KT[:, :, tt_ * 128:(tt_ + 1) * 128], self.pT[0:HD, 0:NKV, :])
                self.cp("act", V[:, tt_, :, 0:HD],
                        A(pkv.t[:, DKV:2 * DKV].rearrange("p (h k) -> p h k", h=NKV), pkv.k))
            groups = [list(range(a, a + 4)) for a in range(0, 32, 4)]
            if with_ctx:
                groups.append([32, 33])
            if qtiles is not None:
                groups = qtiles
            cur_m = 1
            pi = 0
            for grp in groups:
                m = 0 if grp[0] < 32 else 1
                if m != cur_m:
                    self.set_mod(i, m, 4, 2, 3, 5, 3, 1.0)
                    cur_m = m
                nq = 128 * len(grp)
                for jj, tt_ in enumerate(grp):
                    prologue(tt_)
                    for hf in range(2):
                        for kc in range(8):
                            self.mm(self.py[:, hf * 512:(hf + 1) * 512], nT[:, kc, :], Wq[:, kc, hf * 512:(hf + 1) * 512],
                                    kc == 0, kc == 7)
                    self.cp("act", qf[:], self.py[:])
                    self.head_rms(qf[:], qf[:], NH, gq)
                    if m == 0:
                        self.rope(qb, qf, NH, tt_)
                    else:
                        self.cp("act", qb[:], qf[:])
                    for hb in range(2):
                        for h8 in range(8):
                            h = hb * 8 + h8
                            self.tr(self.pT[0:HD, h8, :], qb[:, h * HD:(h + 1) * HD], self.ident_b[:], inc=(h8 == 7))
                        self.cp("act", QT[:, hb * 8:(hb + 1) * 8, jj * 128:(jj + 1) * 128], self.pT[0:HD, :, :])
                kts = list(range(NT)) if m == 0 else [32, 33]
                for h in range(NH):
                    kvh = h // (NH // NKV)
                    pacc = po[h % 2]
                    nk_ = len(kts)
                    slots = []
                    for ki in range(nk_ + 1):
                        if ki < nk_:
                            kt = kts[ki]
                            ps_ = pss[pi % 2]
                            pe_ = pexp[pi % 3]
                            pi += 1
                            self.mm(ps_[:, :nq], KT[:, kvh, kt * 128:(kt + 1) * 128], QT[:, h, :nq], True, True)
                            self.act(pe_[:, :nq], ps_[:, :nq], AF.Exp, scale=ATTN_SCALE, bias=negm[:, 3:4])
                            slots.append((kt, pe_))
                        if ki >= 1:
                            kt0, pe0 = slots[ki - 1]
                            self.mm(pacc[:, :nq], V[:, kt0, kvh, :], pe0[:, :nq], ki - 1 == 0, ki - 1 == nk_ - 1)
                    self.recip(rd[HD:HD + 1, :nq], pacc[HD:HD + 1, :nq])
                    self.mm(pbc[:, :nq], self.ones_f[HD:HD + 1, 0:HD], rd[HD:HD + 1, :nq], True, True)
                    self.cp("act", bcs[:, :nq], pbc[:, :nq])
                    self.tt("dve", OT[:, h, :nq], pacc[0:HD, :nq], bcs[:, :nq], ALU.mult)
                for jj, tt_ in enumerate(grp):
                    for hf in range(2):
                        for h in range(NH):
                            self.mm(self.py[:, hf * 512:(hf + 1) * 512], OT[:, h, jj * 128:(jj + 1) * 128],
                                    Wo[:, h, hf * 512:(hf + 1) * 512], h == 0, h == NH - 1)
                    x2 = self.xt[1]
                    self.load(x2[:], src(tt_))
                    self.resid_tile(x2, dst(tt_))
        S.barrier()

    def wattn_phase(self, i, j, src, dst, with_ctx, qtiles=None):
        S = self.S
        I = self.I
        with ExitStack() as st:
            Wq = self.sb(st, "Wq", [128, 8, D], BF16)
            Wkv = self.sb(st, "Wkv", [128, 8, 2 * DKV], BF16)
            Wo = self.sb(st, "Wo", [128, 8, D], BF16)
            self.load_w_bf16(Wq, I["wattn_wq"][j], 8)
            self.load_w_bf16(Wo, I["wattn_wo"][j], 8)
            wk = I["wattn_wk"][j].rearrange("(k p) n -> p k n", p=128)
            wv = I["wattn_wv"][j].rearrange("(k p) n -> p k n", p=128)
            for kc in range(8):
                self.load(Wkv[:, kc, 0:DKV], wk[:, kc, :], q="pool", part=True)
                self.load(Wkv[:, kc, DKV:2 * DKV], wv[:, kc, :], q="pool", part=True)
            self.alloc_common(st, nx=2)
            self.alloc_rope(st)
            sinkb = self.sb(st, "sinkb", [128, NH], F32)
            self.bload(sinkb, I["wattn_sink"][j])
            mask = self.sb(st, "mask", [128, 384], F32)
            self.load(mask[:], I["wmask"][:, :])
            KT = self.sb(st, "KT", [HD, NKV, NTOK], BF16)
            V = self.sb(st, "V", [128, NT, NKV, HD], BF16)
            nT = self.sb(st, "nT", [128, 8, 128], BF16)
            qf = self.sb(st, "qf", [128, D], F32)
            qb = self.sb(st, "qb", [128, D], BF16)
            QT = self.sb(st, "QT", [HD, NH, 128], BF16)
            sc = self.sb(st, "sc", [128, 640], F32)
            P = self.sb(st, "P", [128, 640], BF16)
            PT = self.sb(st, "PT", [128, 5, 128], BF16)
            O = self.sb(st, "O", [128, D], BF16)
            OT = self.sb(st, "OT", [128, 8, 128], BF16)
            sm = self.sb(st, "sm", [128, 8], F32)
            pl = self.ps(st, "pl", [128, 512], F32)
            pc = self.ps(st, "pc", [128, 512], F32)
            pPT = self.ps(st, "pPT", [128, 8, 128], BF16)
            pov = self.ps(st, "pov", [128, 512], F32)
            x = self.xt[0]

            def prologue(tt_):
                self.load(x[:], src(tt_))
                self.norm_tile(x, self.nb[:])
                self.transpose_cols(nT, 0, self.nb)

            cur_m = None
            for tt_ in range(NT):
                m = 0 if tt_ < 32 else 1
                if m != cur_m:
                    self.set_mod(i, m, 4, 2, 3, 5, 3, 1.0)
                    cur_m = m
                prologue(tt_)
                pkv = self.py
                for kc in range(8):
                    self.mm(pkv[:, 0:512], nT[:, kc, :], Wkv[:, kc, :], kc == 0, kc == 7)
                if m == 0:
                    self.cp("act", qf[:, 0:DKV], pkv[:, 0:DKV])
                    self.rope(qb, qf, NKV, tt_)
                else:
                    self.cp("act", qb[:, 0:DKV], pkv[:, 0:DKV])
                for h in range(NKV):
                    self.tr(self.pT[0:HD, h, :], qb[:, h * HD:(h + 1) * HD], self.ident_b[:], inc=(h == NKV - 1))
                self.cp("act", KT[:, :, tt_ * 128:(tt_ + 1) * 128], self.pT[0:HD, 0:NKV, :])
                self.cp("act", V[:, tt_, :, :],
                        A(pkv.t[:, DKV:2 * DKV].rearrange("p (h k) -> p h k", h=NKV), pkv.k))
            qt = list(range(32)) + ([32, 33] if with_ctx else [])
            if qtiles is not None:
                qt = qtiles
            cur_m = 1
            for tt_ in qt:
                m = 0 if tt_ < 32 else 1
                if m != cur_m:
                    self.set_mod(i, m, 4, 2, 3, 5, 3, 1.0)
                    cur_m = m
                prologue(tt_)
                for hf in range(2):
                    for kc in range(8):
                        self.mm(self.py[:, hf * 512:(hf + 1) * 512], nT[:, kc, :], Wq[:, kc, hf * 512:(hf + 1) * 512],
                                kc == 0, kc == 7)
                if m == 0:
                    self.cp("act", qf[:], self.py[:])
                    self.rope(qb, qf, NH, tt_)
                else:
                    self.cp("act", qb[:], self.py[:])
                for hb in range(2):
                    for h8 in range(8):
                        h = hb * 8 + h8
                        self.tr(self.pT[0:HD, h8, :], qb[:, h * HD:(h + 1) * HD], self.ident_b[:], inc=(h8 == 7))
                    self.cp("act", QT[:, hb * 8:(hb + 1) * 8, :], self.pT[0:HD, :, :])
                if m == 0:
                    b0 = max(tt_ - 1, 0)
                    b1 = min(tt_ + 1, 31)
                    nloc = (b1 - b0 + 1) * 128
                    moff = 0 if tt_ > 0 else 128
                    ktiles = list(range(b0, b1 + 1)) + [32, 33]
                else:
                    nloc = 0
                    ktiles = [32, 33]
                nk = nloc + 256
                for h in range(NH):
                    kvh = h // (NH // NKV)
                    if nloc:
                        self.mm(pl[:, :nloc], QT[:, h, :], KT[:, kvh, b0 * 128:(b1 + 1) * 128], True, True)
                        self.stt(sc[:, :nloc], pl[:, :nloc], ATTN_SCALE, mask[:, moff:moff + nloc], ALU.mult, ALU.add)
                    self.mm(pc[:, 0:256], QT[:, h, :], KT[:, kvh, SEQ:NTOK], True, True)
                    self.act(sc[:, nloc:nk], pc[:, 0:256], AF.Copy, scale=ATTN_SCALE)
                    self.red(sm[:, 0:1], sc[:, :nk], ALU.max)
                    self.tt("dve", sm[:, 0:1], sm[:, 0:1], sinkb[:, h:h + 1], ALU.max)
                    self.ts("dve", sm[:, 1:2], sm[:, 0:1], -1.0, None, ALU.mult)
                    self.act(P[:, :nk], sc[:, :nk], AF.Exp, bias=sm[:, 1:2], accum=sm[:, 2:3])
                    self.act(sm[:, 3:4], sinkb[:, h:h + 1], AF.Exp, bias=sm[:, 1:2])
                    self.tt("dve", sm[:, 4:5], sm[:, 2:3], sm[:, 3:4], ALU.add)
                    self.recip(sm[:, 5:6], sm[:, 4:5])
                    nkt = nk // 128
                    for kt in range(nkt):
                        self.tr(pPT[:, kt, :], P[:, kt * 128:(kt + 1) * 128], self.ident_b[:], inc=(kt == nkt - 1))
                    self.cp("act", PT[:, 0:nkt, :], pPT[:, 0:nkt, :])
                    for kt in range(nkt):
                        self.mm(pov[:, 0:HD], PT[:, kt, :], V[:, ktiles[kt], kvh, :], kt == 0, kt == nkt - 1)
                    self.ts("dve", O[:, h * HD:(h + 1) * HD], pov[:, 0:HD], sm[:, 5:6], None, ALU.mult)
                self.transpose_cols(OT, 0, O)
                for hf in range(2):
                    for kc in range(8):
                        self.mm(self.py[:, hf * 512:(hf + 1) * 512], OT[:, kc, :], Wo[:, kc, hf * 512:(hf + 1) * 512],
                                kc == 0, kc == 7)
                x2 = self.xt[1]
                self.load(x2[:], src(tt_))
                self.resid_tile(x2, dst(tt_))
        S.barrier()

    def hm_tile(self, arr, pos0):
        return arr[:, pos0:pos0 + 128, :].rearrange("h t k -> t h k")

    @staticmethod
    def v3(a, H=NH):
        return A(a.ap.rearrange("p (h k) -> p h k", h=H), a.k)

    def rwkv_norm_pass(self, i, src):
        S = self.S
        with ExitStack() as st:
            self.alloc_common(st, nx=2)
            nf = [self.sb(st, "nf%d" % a, [128, D], F32) for a in range(2)]
            z = self.sb(st, "zrow", [1, D], F32)
            self.memset("pool", z[:], 0.0)
            for r in (0, SEQ + 1, NSROWS - 1):
                self.store(self.NS[r:r + 1, :], z[:])
            cur_m = None
            for tt_ in range(NT):
                m = 0 if tt_ < 32 else 1
                if m != cur_m:
                    self.set_mod(i, m, 4, 2, 3, 5, 3, 1.0)
                    cur_m = m
                x = self.xt[tt_ % 2]
                self.load(x[:], src(tt_))
                self.norm_tile(x, nf[tt_ % 2][:])
                r0 = nsrow_of_tile(tt_)
                self.store(self.NS[r0:r0 + 128, :], nf[tt_ % 2][:])
        S.barrier()

    def alloc_shift(self, st):
        self.ncur = self.sb(st, "ncur", [128, D], F32)
        self.nprev = self.sb(st, "nprev", [128, D], F32)
        self.nnext = self.sb(st, "nnext", [128, D], F32)
        self.nbb = self.sb(st, "nbb", [128, D], BF16)
        self.xxb = self.sb(st, "xxb", [128, D], BF16)
        self.nTx = self.sb(st, "nTx", [128, 16, 128], BF16)
        self.tmpa = self.sb(st, "tmpa", [128, D], F32)
        self.tmpb = self.sb(st, "tmpb", [128, D], F32)
        self.pT = self.ps(st, "pT", [128, 8, 128], BF16)
        self.hstat = self.sb(st, "hstat", [128, 16], F32)
        self.muT = self.sb(st, "muT", [128, 8, 6], F32)

    def load_mu(self, j):
        for m in range(6):
            for kc in range(8):
                self.load(self.muT[:, kc, m:m + 1],
                          self.I["rwkv_mu"][j, m, kc * 128:(kc + 1) * 128].rearrange("(p o) -> p o", o=1), part=True)

    def shift_tile(self, tt_):
        r0 = nsrow_of_tile(tt_)
        self.load(self.ncur[:], self.NS[r0:r0 + 128, :])
        self.load(self.nprev[:], self.NS[r0 - 1:r0 + 127, :])
        self.load(self.nnext[:], self.NS[r0 + 1:r0 + 129, :])
        self.tt("pool", self.tmpa[:], self.nprev[:], self.nnext[:], ALU.add)
        self.stt(self.xxb[:], self.tmpa[:], 0.5, self.ncur[:], ALU.mult, ALU.subtract)
        self.cp("act", self.nbb[:], self.ncur[:])
        for half, srcb in ((0, self.nbb), (1, self.xxb)):
            for kc in range(8):
                self.tr(self.pT[:, kc, :], srcb[:, kc * 128:(kc + 1) * 128], self.ident_b[:], inc=(kc == 7))
            self.cp("act", self.nTx[:, half * 8:(half + 1) * 8, :], self.pT[:, :, :])

    def load_mixed_w(self, dst, col0, ncol, src, m):
        sv = src.rearrange("(k p) n -> p k n", p=128)
        for kc in range(8):
            self.load(dst[:, kc, col0:col0 + ncol], sv[:, kc, :], q="pool", part=True)
            self.load(dst[:, 8 + kc, col0:col0 + ncol], sv[:, kc, :], q="pool", part=True)
        for kc in range(8):
            self.ts("dve" if kc % 2 else "pool", dst[:, 8 + kc, col0:col0 + ncol], dst[:, 8 + kc, col0:col0 + ncol],
                    self.muT[:, kc, m:m + 1], None, ALU.mult)

    def rwkv_feat1(self, j):
        S, I = self.S, self.I
        with ExitStack() as st:
            self.alloc_shift(st)
            self.load_mu(j)
            Wbig = self.sb(st, "Wbig", [128, 16, 3 * D], BF16)
            self.load_mixed_w(Wbig, 0, D, I["rwkv_wr"][j], 0)
            self.load_mixed_w(Wbig, D, D, I["rwkv_wk"][j], 2)
            self.load_mixed_w(Wbig, 2 * D, D, I["rwkv_wv"][j], 3)
            kk_b = self.sb(st, "kk_b", [128, D], F32)
            self.bload(kk_b, I["rwkv_k_k"][j])
            if j > 0:
                WLv = self.sb(st, "WLv", [128, 16, 32], BF16)
                self.load_mixed_w(WLv, 0, 32, I["rwkv_v1"][j - 1], 3)
                v2s = self.sb(st, "v2s", [32, D], BF16)
                self.load(v2s[:], I["rwkv_v2"][j - 1], q="pool", part=True)
                v0b = self.sb(st, "v0b", [128, D], F32)
                self.bload(v0b, I["rwkv_v0"][j - 1])
                hvb = self.sb(st, "hvb", [32, 128], BF16)
                ph = self.ps(st, "ph", [128, 512], F32)
            ob = [self.sb(st, "ob%d" % a, [128, D], F32) for a in range(4)]
            pys = [self.ps(st, "pya", [128, D], F32), self.ps(st, "pyb", [128, D], F32)]
            VAj = self.VA[j]
            for tt_ in range(NT):
                pos0 = pos_of_tile(tt_)
                self.shift_tile(tt_)
                for qi in range(3):
                    py = pys[qi % 2]
                    for hf in range(2):
                        for c in range(16):
                            self.mm(py[:, hf * 512:(hf + 1) * 512], self.nTx[:, c, :],
                                    Wbig[:, c, qi * D + hf * 512:qi * D + (hf + 1) * 512], c == 0, c == 15)
                    if qi == 0:
                        self.cp("act", ob[0][:], py[:])
                        self.store(self.hm_tile(self.RH, pos0), self.v3(ob[0][:]))
                    elif qi == 1:
                        kraw = ob[1]
                        self.cp("act", kraw[:], py[:])
                        self.store(self.KRAW[pos0:pos0 + 128, :], kraw[:])
                        self.tt("dve", self.tmpb[:], kraw[:], kk_b[:], ALU.mult)
                        self.tt("pool", self.tmpa[:], self.tmpb[:], self.tmpb[:], ALU.mult)
                        hs = self.hstat
                        self.red(hs[:, 0:16], self.v3(self.tmpa[:]))
                        self.tt("pool", hs[:, 0:16], hs[:, 0:16], self.chalf[:, 0:16], ALU.pow)
                        self.ts("dve", hs[:, 0:16], hs[:, 0:16], 1e-12, None, ALU.max)
                        self.recip(hs[:, 0:16], hs[:, 0:16])
                        self.ts("dve", hs[:, 0:16], hs[:, 0:16], -1.0, None, ALU.mult)
                        rb = A(hs.t[:, 0:16].unsqueeze(2).to_broadcast([128, NH, HD]), hs.k)
                        self.tt("dve", self.v3(ob[2][:]), self.v3(self.tmpb[:]), rb, ALU.mult)
                        self.store(self.hm_tile(self.AVH, pos0), self.v3(ob[2][:]))
                    else:
                        vf = ob[3]
                        self.cp("act", vf[:], py[:])
                        if j > 0:
                            for c in range(16):
                                self.mm(ph[0:32, 0:128], WLv[:, c, :], self.nTx[:, c, :], c == 0, c == 15)
                            self.cp("act", hvb[:], ph[0:32, 0:128])
                            py2 = pys[0]
                            for hf in range(2):
                                self.mm(py2[:, hf * 512:(hf + 1) * 512], hvb[:], v2s[:, hf * 512:(hf + 1) * 512], True, True)
                            self.tt("dve", self.tmpa[:], py2[:], v0b[:], ALU.add)
                            self.act(self.tmpa[:], self.tmpa[:], AF.Sigmoid)
                            self.load(self.tmpb[:], self.VA[0][pos0:pos0 + 128, :])
                            self.tt("pool", self.tmpb[:], self.tmpb[:], vf[:], ALU.subtract)
                            self.tt("dve", self.tmpb[:], self.tmpb[:], self.tmpa[:], ALU.mult)
                            self.tt("pool", vf[:], vf[:], self.tmpb[:], ALU.add)
                        self.store(VAj[pos0:pos0 + 128, :], vf[:])
        S.barrier()

    def rwkv_feat2(self, j):
        S, I = self.S, self.I
        with ExitStack() as st:
            self.alloc_shift(st)
            self.load_mu(j)
            WL1 = self.sb(st, "WL1", [128, 16, 576], BF16)
            for d in range(2):
                self.load_mixed_w(WL1, d * 64, 64, I["rwkv_w1"][j, d], 1)
                self.load_mixed_w(WL1, 128 + d * 64, 64, I["rwkv_a1"][j, d], 4)
                self.load_mixed_w(WL1, 256 + d * 160, 160, I["rwkv_g1"][j, d], 5)
            w2s = self.sb(st, "w2s", [128, D], BF16)
            a2s = self.sb(st, "a2s", [128, D], BF16)
            g2s = self.sb(st, "g2s", [128, 2, 2, D], BF16)
            self.load(w2s[:], I["rwkv_w2"][j].rearrange("d r n -> (d r) n"), q="pool", part=True)
            self.load(a2s[:], I["rwkv_a2"][j].rearrange("d r n -> (d r) n"), q="pool", part=True)
            for d in range(2):
                self.load(g2s[:, d, 0, :], I["rwkv_g2"][j, d, 0:128, :], q="pool", part=True)
                self.load(g2s[0:32, d, 1, :], I["rwkv_g2"][j, d, 128:160, :], q="pool", part=True)
            w0b = self.sb(st, "w0b", [128, 2, D], F32)
            a0b = self.sb(st, "a0b", [128, 2, D], F32)
            kab = self.sb(st, "kab", [128, D], F32)
            for d in range(2):
                self.load(w0b[:, d, :], I["rwkv_w0"][j, d].partition_broadcast(128), part=True)
                self.load(a0b[:, d, :], I["rwkv_a0"][j, d].partition_broadcast(128), part=True)
            self.bload(kab, I["rwkv_k_a"][j])
            hwT = self.sb(st, "hwT", [128, 128], BF16)
            haT = self.sb(st, "haT", [128, 128], BF16)
            hgT = [self.sb(st, "hgT%d" % d, [128, 2, 128], BF16) for d in range(2)]
            kraw = self.sb(st, "kraw", [128, D], F32)
            av = self.sb(st, "av", [128, D], F32)
            ob = [self.sb(st, "ob%d" % a, [128, D], F32) for a in range(4)]
            pys = [self.ps(st, "pya", [128, D], F32), self.ps(st, "pyb", [128, D], F32)]
            phs = [self.ps(st, "ph%d" % a, [128, 512], F32) for a in range(2)]
            for tt_ in range(NT):
                pos0 = pos_of_tile(tt_)
                self.shift_tile(tt_)
                self.load(kraw[:], self.KRAW[pos0:pos0 + 128, :])
                self.load(self.v3(av[:]), self.hm_tile(self.AVH, pos0))
                for c in range(16):
                    self.mm(phs[0][:, 0:128], WL1[:, c, 0:128], self.nTx[:, c, :], c == 0, c == 15)
                self.act(hwT[:], phs[0][:, 0:128], AF.Tanh)
                for c in range(16):
                    self.mm(phs[1][:, 0:128], WL1[:, c, 128:256], self.nTx[:, c, :], c == 0, c == 15)
                self.cp("act", haT[:], phs[1][:, 0:128])
                for d in range(2):
                    g0 = 256 + d * 160
                    for c in range(16):
                        self.mm(phs[0][:, 0:128], WL1[:, c, g0:g0 + 128], self.nTx[:, c, :], c == 0, c == 15)
                    self.act(hgT[d][:, 0, :], phs[0][:, 0:128], AF.Sigmoid)
                    for c in range(16):
                        self.mm(phs[1][0:32, 0:128], WL1[:, c, g0 + 128:g0 + 160], self.nTx[:, c, :], c == 0, c == 15)
                    self.act(hgT[d][0:32, 1, :], phs[1][0:32, 0:128], AF.Sigmoid)
                for d in range(2):
                    ps_ = slice(d * 64, (d + 1) * 64)
                    py = pys[0]
                    for hf in range(2):
                        self.mm(py[:, hf * 512:(hf + 1) * 512], hwT[ps_, :], w2s[ps_, hf * 512:(hf + 1) * 512], True, True)
                    self.tt("dve", self.tmpa[:], py[:], w0b[:, d, :], ALU.add)
                    self.act(self.tmpa[:], self.tmpa[:], AF.Sigmoid)
                    self.act(ob[0][:], self.tmpa[:], AF.Exp, scale=-0.6065306597126334)
                    self.store(self.hm_tile(self.WH[d], pos0), self.v3(ob[0][:]))
                    py = pys[1]
                    for hf in range(2):
                        self.mm(py[:, hf * 512:(hf + 1) * 512], haT[ps_, :], a2s[ps_, hf * 512:(hf + 1) * 512], True, True)
                    self.tt("dve", self.tmpb[:], py[:], a0b[:, d, :], ALU.add)
                    self.act(self.tmpb[:], self.tmpb[:], AF.Sigmoid)
                    self.stt(ob[1][:], av[:], -1.0, self.tmpb[:], ALU.mult, ALU.mult)
                    self.store(self.hm_tile(self.BH[d], pos0), self.v3(ob[1][:]))
                    self.stt(self.tmpb[:], self.tmpb[:], -1.0, kab[:], ALU.add, ALU.mult)
                    self.tt("pool", self.tmpb[:], self.tmpb[:], kraw[:], ALU.mult)
                    self.tt("dve", ob[2][:], self.tmpb[:], kraw[:], ALU.add)
                    self.store(self.hm_tile(self.KH[d], pos0), self.v3(ob[2][:]))
                    py = pys[0]
                    for hf in range(2):
                        cs = slice(hf * 512, (hf + 1) * 512)
                        self.mm(py[:, cs], hgT[d][:, 0, :], g2s[:, d, 0, cs], True, False)
                        self.mm(py[:, cs], hgT[d][0:32, 1, :], g2s[0:32, d, 1, cs], False, True)
                    self.cp("act", ob[3][:], py[:])
                    self.store(self.GT[d][pos0:pos0 + 128, :], ob[3][:])
        S.barrier()

    def rwkv_scan(self, j, nblocks=None):
        S = self.S
        TB = 16
        with ExitStack() as st:
            St = self.sb(st, "St", [128, 2, 8, 64], F32)
            t1 = self.sb(st, "sc_t1", [128, 2, 8, 64], F32)
            t2 = self.sb(st, "sc_t2", [128, 2, 8, 64], F32)
            kv = [self.sb(st, "sc_kv%d" % a, [128, 2, 8, 64], F32) for a in range(2)]
            sa = self.sb(st, "sc_sa", [128, 16], F32)
            qs = ("w", "b", "k", "a", "r")
            bufs = [{q: self.sb(st, "sc_%s%d" % (q, bi), [128, 2, TB, 64], F32) for q in qs} for bi in range(2)]
            vbuf = [self.sb(st, "sc_v%d" % bi, [128, 2, TB, 8], F32) for bi in range(2)]
            ybuf = [self.sb(st, "sc_y%d" % bi, [128, 2, TB, 8], F32) for bi in range(2)]
            self.memset("dve", St[:], 0.0)
            VAj = self.VA[j]
            NB = NPOS // TB if nblocks is None else nblocks

            def lo_of(b):
                return (TB * b, (240 - TB * b) if b < 16 else (4592 - TB * b))

            def loads(b):
                bi = b % 2
                for d, lo in enumerate(lo_of(b)):
                    arrs = {"w": self.WH[d], "b": self.BH[d], "k": self.KH[d], "a": self.AVH, "r": self.RH}
                    for q in qs:
                        arr = arrs[q]
                        src = bass.AP(tensor=arr.tensor, offset=arr.offset + lo * 64,
                                      ap=[[NPOS * 64, 16], [0, 8], [1, TB * 64]])
                        dstb = bufs[bi][q]
                        self.load(A(dstb.t[:, d, :, :].rearrange("p t k -> p (t k)"), dstb.k), src, part=True)
                    self.load(vbuf[bi][:, d, :, :], VAj[lo:lo + TB, :].rearrange("t (p l) -> p t l", l=8), part=True)

            loads(0)
            for b in range(NB):
                bi = b % 2
                if b + 1 < NB:
                    loads(b + 1)
                for s in range(TB):
                    c0, c1 = s, TB - 1 - s

                    def opnd(q):
                        return bufs[bi][q].cust(c0 * 64, [[2 * TB * 64, 128], [TB * 64 + (c1 - c0) * 64, 2], [0, 8], [1, 64]])
                    vb = vbuf[bi].cust(c0 * 8, [[2 * TB * 8, 128], [TB * 8 + (c1 - c0) * 8, 2], [1, 8], [0, 64]])
                    yo = ybuf[bi].cust(c0 * 8, [[2 * TB * 8, 128], [TB * 8 + (c1 - c0) * 8, 2], [1, 8]])
                    kvt = kv[s % 2]
                    self.tt("pool", kvt[:], vb, opnd("k"), ALU.mult)
                    self.tt("dve", t1[:], St[:], opnd("a"), ALU.mult)
                    self.red(A(sa.t[:, :].rearrange("p (d l) -> p d l", d=2), sa.k), t1[:])
                    self.tt("dve", St[:], St[:], opnd("w"), ALU.mult)
                    sab = A(sa.t[:, :].rearrange("p (d l) -> p d l", d=2).unsqueeze(3).to_broadcast([128, 2, 8, 64]), sa.k)
                    self.tt("dve", t2[:], sab, opnd("b"), ALU.mult)
                    self.tt("dve", St[:], St[:], t2[:], ALU.add)
                    self.tt("dve", St[:], St[:], kvt[:], ALU.add)
                    self.tt("dve", t1[:], St[:], opnd("r"), ALU.mult)
                    self.red(yo, t1[:])
                for d, lo in enumerate(lo_of(b)):
                    self.store(self.YT[d][lo:lo + TB, :].rearrange("t (p l) -> p t l", l=8), ybuf[bi][:, d, :, :])
        S.barrier()

    def rwkv_readout(self, i, j, src, dst, tiles):
        S, I = self.S, self.I
        with ExitStack() as st:
            self.alloc_common(st, nx=2)
            self.hstat = self.sb(st, "hstat", [128, 32], F32)
            Wo = self.sb(st, "Wo", [128, 8, D], BF16)
            self.load_w_bf16(Wo, I["rwkv_wo"][j], 8)
            lnw = self.sb(st, "lnw", [128, 2, D], F32)
            lnb = self.sb(st, "lnb", [128, 2, D], F32)
            rkb = self.sb(st, "rkb", [128, D], F32)
            for d in range(2):
                self.load(lnw[:, d, :], I["rwkv_ln_w"][j, d].partition_broadcast(128), part=True)
                self.load(lnb[:, d, :], I["rwkv_ln_b"][j, d].partition_broadcast(128), part=True)
            self.bload(rkb, I["rwkv_r_k"][j].rearrange("h k -> (h k)"))
            rt = self.sb(st, "rt", [128, D], F32)
            vt = self.sb(st, "vt", [128, D], F32)
            yt = self.sb(st, "yt", [128, D], F32)
            kt_ = self.sb(st, "kt", [128, D], F32)
            gt = self.sb(st, "gt", [128, D], F32)
            oacc = self.sb(st, "oacc", [128, D], F32)
            obf = self.sb(st, "obf", [128, D], BF16)
            OT = self.sb(st, "OT", [128, 8, 128], BF16)
            hs = self.hstat
            VAj = self.VA[j]
            cur_m = None
            for tt_ in tiles:
                m = 0 if tt_ < 32 else 1
                if m != cur_m:
                    self.set_mod(i, m, 4, 2, 3, 5, 3, 1.0)
                    cur_m = m
                pos0 = pos_of_tile(tt_)
                self.load(self.v3(rt[:]), self.hm_tile(self.RH, pos0))
                self.load(vt[:], VAj[pos0:pos0 + 128, :])
                for d in range(2):
                    self.load(yt[:], self.YT[d][pos0:pos0 + 128, :])
                    self.load(self.v3(kt_[:]), self.hm_tile(self.KH[d], pos0))
                    self.load(gt[:], self.GT[d][pos0:pos0 + 128, :])
                    ta, tb = self.tmpa, self.tmpb
                    self.red(hs[:, 0:16], self.v3(yt[:]))
                    self.ts("dve", hs[:, 0:16], hs[:, 0:16], 1.0 / HD, None, ALU.mult)
                    mb_ = A(hs.t[:, 0:16].unsqueeze(2).to_broadcast([128, NH, HD]), hs.k)
                    self.tt("dve", self.v3(ta[:]), self.v3(yt[:]), mb_, ALU.subtract)
                    self.tt("pool", tb[:], ta[:], ta[:], ALU.mult)
                    self.red(hs[:, 16:32], self.v3(tb[:]))
                    self.ts("dve", hs[:, 16:32], hs[:, 16:32], 1.0 / HD, GN_EPS, ALU.mult, ALU.add)
                    self.tt("pool", hs[:, 16:32], hs[:, 16:32], self.nhalf[:, 0:16], ALU.pow)
                    rb_ = A(hs.t[:, 16:32].unsqueeze(2).to_broadcast([128, NH, HD]), hs.k)
                    self.tt("dve", self.v3(ta[:]), self.v3(ta[:]), rb_, ALU.mult)
                    self.tt("pool", ta[:], ta[:], lnw[:, d, :], ALU.mult)
                    self.tt("dve", ta[:], ta[:], lnb[:, d, :], ALU.add)
                    self.tt("pool", tb[:], rt[:], kt_[:], ALU.mult)
                    self.tt("dve", tb[:], tb[:], rkb[:], ALU.mult)
                    self.red(hs[:, 0:16], self.v3(tb[:]))
                    bb_ = A(hs.t[:, 0:16].unsqueeze(2).to_broadcast([128, NH, HD]), hs.k)
                    self.tt("dve", self.v3(tb[:]), self.v3(vt[:]), bb_, ALU.mult)
                    self.tt("pool", ta[:], ta[:], tb[:], ALU.add)
                    if d == 0:
                        self.tt("dve", oacc[:], ta[:], gt[:], ALU.mult)
                    else:
                        self.tt("dve", ta[:], ta[:], gt[:], ALU.mult)
                        self.tt("pool", obf[:], ta[:], oacc[:], ALU.add)
                self.transpose_cols(OT, 0, obf)
                for hf in range(2):
                    for kc in range(8):
                        self.mm(self.py[:, hf * 512:(hf + 1) * 512], OT[:, kc, :], Wo[:, kc, hf * 512:(hf + 1) * 512],
                                kc == 0, kc == 7)
                x = self.xt[0]
                self.load(x[:], src(tt_))
                self.resid_tile(x, dst(tt_))
        S.barrier()

    def build(self):
        nc = self.nc
        cfg = self.cfg
        shapes = dict(
            x=[SEQ, D], ctx=[CTXL, D], c=[D], c_ctx=[D], ident=[128, 128], wmask=[128, 384],
            rope_cos=[SEQ, 32], rope_sin=[SEQ, 32],
            mod_w=[DEPTH, D, NMOD * D], mod_b=[DEPTH, NMOD * D], norm_g=[DEPTH, 6, D],
            ffn_w1=[DEPTH, 2, D, DFF], ffn_w3=[DEPTH, 2, D, DFF], ffn_w2=[DEPTH, 2, DFF, D],
            rwkv_mu=[2, 6, D], rwkv_wr=[2, D, D], rwkv_wk=[2, D, D], rwkv_wv=[2, D, D], rwkv_wo=[2, D, D],
            rwkv_k_k=[2, D], rwkv_k_a=[2, D], rwkv_r_k=[2, 16, 64], rwkv_w0=[2, 2, D], rwkv_w1=[2, 2, D, 64],
            rwkv_w2=[2, 2, 64, D], rwkv_a0=[2, 2, D], rwkv_a1=[2, 2, D, 64], rwkv_a2=[2, 2, 64, D],
            rwkv_g1=[2, 2, D, 160], rwkv_g2=[2, 2, 160, D], rwkv_ln_w=[2, 2, D], rwkv_ln_b=[2, 2, D],
            rwkv_v0=[1, D], rwkv_v1=[1, D, 32], rwkv_v2=[1, 32, D],
            gattn_wq=[1, D, D], gattn_wk=[1, D, DKV], gattn_wv=[1, D, DKV], gattn_wo=[1, D, D],
            gattn_q_norm=[1, HD], gattn_k_norm=[1, HD],
            wattn_wq=[1, D, D], wattn_wk=[1, D, DKV], wattn_wv=[1, D, DKV], wattn_wo=[1, D, D], wattn_sink=[1, NH],
        )
        for k, shp in shapes.items():
            self.dram_in(k, shp)
        self.Y = nc.dram_tensor("y", [SEQ, D], F32, kind="ExternalOutput").ap()
        self.XS = self.scratch("xs", [NTOK, D])
        self.MOD = self.scratch("modv", [DEPTH, 2, NMOD * D])
        self.NS = self.scratch("ns", [NSROWS, D])
        self.RH = self.scratch("rh", [NH, NPOS, HD])
        self.AVH = self.scratch("avh", [NH, NPOS, HD])
        self.WH = [self.scratch("wh%d" % d, [NH, NPOS, HD]) for d in range(2)]
        self.BH = [self.scratch("bh%d" % d, [NH, NPOS, HD]) for d in range(2)]
        self.KH = [self.scratch("kh%d" % d, [NH, NPOS, HD]) for d in range(2)]
        self.KRAW = self.scratch("kraw_d", [NPOS, D])
        self.VA = [self.scratch("va%d" % a, [NPOS, D]) for a in range(2)]
        self.GT = [self.scratch("gt%d" % d, [NPOS, D]) for d in range(2)]
        self.YT = [self.scratch("yt%d" % d, [NPOS, D]) for d in range(2)]
        dbg = cfg.get("dbg", [])
        self.DBG = {n: nc.dram_tensor("dbg_" + n, [NTOK, D], F32, kind="ExternalOutput").ap() for n in dbg}

        self.setup_consts()
        self.mod_phase()

        def src0(tt_):
            if tt_ < 32:
                return self.I["x"][tt_ * 128:(tt_ + 1) * 128, :]
            return self.I["ctx"][(tt_ - 32) * 128:(tt_ - 31) * 128, :]

        def xs(tt_):
            return self.XS[tt_ * 128:(tt_ + 1) * 128, :]

        def mk(name):
            d = self.DBG[name]
            return lambda tt_: d[tt_ * 128:(tt_ + 1) * 128, :]

        def yout(tt_):
            return self.Y[tt_ * 128:(tt_ + 1) * 128, :]

        l0, l1 = cfg.get("l0", 0), cfg.get("l1", DEPTH)
        ft = cfg.get("ffn_tiles")
        for i in range(l0, l1):
            last = i == DEPTH - 1
            kind, j = i % 3, i // 3
            alltiles = list(range(NT))
            s_in = src0 if i == l0 else xs
            if not cfg.get("skip_f1"):
                self.ffn_phase(i, 0, s_in, xs, alltiles if ft is None else ft)
                s_in = xs
            if cfg.get("stop") == "f1":
                break
            mt = list(range(32)) if last else alltiles
            if kind == 0:
                self.rwkv_norm_pass(i, s_in)
                self.rwkv_feat1(j)
                self.rwkv_feat2(j)
                self.rwkv_scan(j, cfg.get("nblocks"))
                self.rwkv_readout(i, j, s_in, xs, mt if cfg.get("mix_tiles") is None else cfg["mix_tiles"])
            elif kind == 1:
                self.gattn_phase(i, j, s_in, xs, not last, cfg.get("qgroups"))
            else:
                self.wattn_phase(i, j, s_in, xs, not last, cfg.get("qtiles"))
            if cfg.get("stop") == "mix":
                break
            self.ffn_phase(i, 1, xs, yout if last else xs, mt if ft is None else ft)
        if dbg:
            with ExitStack() as st:
                buf = self.sb(st, "dbgbuf", [128, D], F32)
                for n in dbg:
                    srcarr = {"xs": self.XS}.get(n)
                    if srcarr is None:
                        continue
                    for tt_ in cfg.get("dbg_tiles", range(NT)):
                        self.load(buf[:], srcarr[tt_ * 128:(tt_ + 1) * 128, :])
                        self.store(self.DBG[n][tt_ * 128:(tt_ + 1) * 128, :], buf[:])
            self.S.barrier()
        self.S.barrier()
        self.gs.close()
        return nc


def build_nc(cfg):
    nc = bass.Bass("TRN2", target_bir_lowering=False)
    kb = KB(nc, cfg)
    kb.build()
    print("instructions:", kb.S.nins, "waits:", kb.S.nwaits, "dma sems:", len(kb.S.dsem))
    return nc, list(kb.I.keys())


def host_consts():
    ident = np.eye(128, dtype=np.float32)
    pos = np.arange(SEQ)
    row = (pos // 64).astype(np.float32)
    col = (pos % 64).astype(np.float32)
    inv_freq = (np.float32(10000.0) ** (-np.arange(16, dtype=np.float32) / np.float32(16))).astype(np.float32)
    ang = np.stack([row, col], axis=-1)[:, :, None] * inv_freq
    rc = np.cos(ang).astype(np.float32).reshape(SEQ, 32)
    rs = np.sin(ang).astype(np.float32).reshape(SEQ, 32)
    qq = np.arange(128)[:, None]
    kk = np.arange(128)[None, :]
    maskL = np.where(kk >= qq, 0.0, MASKV).astype(np.float32)
    maskR = np.where(kk <= qq, 0.0, MASKV).astype(np.float32)
    wmask = np.concatenate([maskL, np.zeros((128, 128), np.float32), maskR], axis=1)
    return dict(ident=ident, rope_cos=rc, rope_sin=rs, wmask=wmask)


def make_in_maps(inputs, names, ncores=8):
    consts = host_consts()
    maps = []
    for b in range(ncores):
        m = {}
        for k, v in inputs.items():
            if k not in names:
                continue
            v = np.ascontiguousarray(v, dtype=np.float32)
            if k in ("x", "c", "ctx"):
                m[k] = np.ascontiguousarray(v[b])
            else:
                m[k] = v
        for k, v in consts.items():
            if k in names:
                m[k] = v
        maps.append(m)
    return maps


def kernel(**inputs):
    nc, names = build_nc({})
    maps = make_in_maps(inputs, names, 8)
    res = run_bass_kernel_spmd(nc, maps, core_ids=list(range(8)))
    return np.stack([r["y"] for r in res.results], axis=0).astype(np.float32)
```

```python
import numpy as np
from contextlib import ExitStack
import concourse.bass as bass
import concourse.mybir as mybir
from concourse.bass_utils import run_bass_kernel_spmd

F32 = mybir.dt.float32
BF16 = mybir.dt.bfloat16
AF = mybir.ActivationFunctionType
ALU = mybir.AluOpType
AX = mybir.AxisListType

D = 1024
SEQ = 4096
CTXL = 256
NTOK = SEQ + CTXL
NT = NTOK // 128
DEPTH = 4
DFF = 2816
NFC = DFF // 128
NMOD = 9
EPS = 1e-6


class Sched:
    def __init__(self, nc):
        self.nc = nc
        self.stack = ExitStack()
        self.eng = {}
        for name, e in (("pe", nc.tensor), ("dve", nc.vector), ("act", nc.scalar),
                        ("pool", nc.gpsimd), ("sp", nc.sync)):
            sem = self.stack.enter_context(nc.semaphore("s_" + name))
            self.eng[name] = dict(e=e, sem=sem, count=0, waited={}, pending=False)
        self.lastw = {}
        self.readers = {}
        self.dsem = {}
        self.nwaits = 0
        self.nins = 0

    def _wait(self, E, need):
        for sid, (sem, val) in need.items():
            if E["waited"].get(sid, 0) < val:
                E["e"].wait_ge(sem, val)
                E["waited"][sid] = val
                self.nwaits += 1

    @staticmethod
    def _merge(need, toks, skip=None):
        for sid, (sem, val) in toks.items():
            if sid == skip:
                continue
            if sid not in need or need[sid][1] < val:
                need[sid] = (sem, val)

    def op(self, eng, fn, reads=(), writes=(), inc=True):
        E = self.eng[eng]
        need = {}
        skip = "pe" if eng == "pe" else None
        for k in reads:
            self._merge(need, self.lastw.get(k, {}), skip)
        for k in writes:
            self._merge(need, self.lastw.get(k, {}), skip)
            self._merge(need, self.readers.get(k, {}), skip)
        self._wait(E, need)
        ins = fn(E["e"])
        self.nins += 1
        if inc:
            E["count"] += 1
            ins.then_inc(E["sem"], 1)
            val = E["count"]
            E["pending"] = False
        else:
            val = E["count"] + 1
            E["pending"] = True
        tok = (E["sem"], val)
        for k in reads:
            self.readers.setdefault(k, {})[eng] = tok
        for k in writes:
            self.lastw[k] = {eng: tok}
            self.readers[k] = {}
        return ins

    def dma(self, q, out, in_, key, load, part=False, **kw):
        E = self.eng[q]
        sid = ("dma", key)
        need = {}
        self._merge(need, self.lastw.get(key, {}), sid if (load and part) else None)
        if load:
            self._merge(need, self.readers.get(key, {}), None)
        self._wait(E, need)
        if key not in self.dsem:
            sem = self.stack.enter_context(self.nc.semaphore("d_%d" % len(self.dsem)))
            self.dsem[key] = [sem, 0]
        ds = self.dsem[key]
        ins = E["e"].dma_start(out=out, in_=in_, **kw)
        self.nins += 1
        ds[1] += 16
        ins.then_inc(ds[0], 16)
        tok = (ds[0], ds[1])
        if load:
            self.lastw[key] = {sid: tok}
            self.readers[key] = {}
        else:
            self.readers.setdefault(key, {})[sid] = tok
        return ins

    def barrier(self):
        toks = {}
        for name, E in self.eng.items():
            assert not E["pending"], name
            if E["count"] > 0:
                toks[name] = (E["sem"], E["count"])
        for key, ds in self.dsem.items():
            if ds[1] > 0:
                toks[("dma", key)] = (ds[0], ds[1])
        for name, E in self.eng.items():
            self._wait(E, toks)
        self.lastw = {}
        self.readers = {}


HD = 64
NH = 16
NKV = 4
DKV = 256
ATTN_SCALE = HD ** -0.5
NPOS = NTOK
NSROWS = NTOK + 3
GN_EPS = 64e-5
MASKV = -30000.0


class A:
    __slots__ = ("ap", "k")

    def __init__(self, ap, k):
        self.ap = ap
        self.k = k


class T:
    def __init__(self, t, k):
        self.t = t
        self.k = k

    def __getitem__(self, idx):
        return A(self.t[idx], self.k)

    def cust(self, offset, dims):
        return A(bass.AP(tensor=self.t[:].tensor, offset=offset, ap=[list(d) for d in dims]), self.k)


def _ap(x):
    return x.ap if isinstance(x, A) else x


def _keys(*xs):
    return [x.k for x in xs if isinstance(x, A)]


def pos_of_tile(tt):
    return 256 + tt * 128 if tt < 32 else (tt - 32) * 128


def nsrow_of_tile(tt):
    return 1 + tt * 128 if tt < 32 else 4098 + (tt - 32) * 128


class KB:
    def __init__(self, nc, cfg):
        self.nc = nc
        self.cfg = cfg
        self.S = Sched(nc)
        self.gs = self.S.stack
        self.I = {}

    def dram_in(self, name, shape):
        self.I[name] = self.nc.dram_tensor(name, list(shape), F32, kind="ExternalInput").ap()
        return self.I[name]

    def scratch(self, name, shape):
        return self.nc.dram_tensor(name, list(shape), F32, kind="Internal").ap()

    def sb(self, st, name, shape, dt=F32):
        self.uid = getattr(self, "uid", 0) + 1
        return T(st.enter_context(self.nc.sbuf_tensor("%s_%d" % (name, self.uid), list(shape), dt)), name)

    def ps(self, st, name, shape, dt=F32):
        self.uid = getattr(self, "uid", 0) + 1
        return T(st.enter_context(self.nc.psum_tensor("%s_%d" % (name, self.uid), list(shape), dt)), name)

    def tt(self, eng, out, a, b, op):
        self.S.op(eng, lambda e: e.tensor_tensor(out=out.ap, in0=a.ap, in1=b.ap, op=op),
                  reads=_keys(a, b), writes=[out.k])

    def ts(self, eng, out, a, s1, s2, op0, op1=None):
        if op1 is None:
            f = lambda e: e.tensor_scalar(out=out.ap, in0=a.ap, scalar1=_ap(s1), scalar2=None, op0=op0)
        else:
            f = lambda e: e.tensor_scalar(out=out.ap, in0=a.ap, scalar1=_ap(s1), scalar2=_ap(s2), op0=op0, op1=op1)
        self.S.op(eng, f, reads=_keys(a, s1, s2), writes=[out.k])

    def stt(self, out, a, s, b, op0, op1):
        self.S.op("dve", lambda e: e.scalar_tensor_tensor(out=out.ap, in0=a.ap, scalar=_ap(s), in1=b.ap,
                                                          op0=op0, op1=op1),
                  reads=_keys(a, s, b), writes=[out.k])

    def act(self, out, a, func, scale=1.0, bias=None, accum=None):
        kw = {}
        if bias is not None:
            kw["bias"] = _ap(bias)
        if accum is not None:
            kw["accum_out"] = accum.ap
        w = [out.k] + ([accum.k] if accum is not None else [])
        self.S.op("act", lambda e: e.activation(out=out.ap, in_=a.ap, func=func, scale=_ap(scale), **kw),
                  reads=_keys(a, bias, scale), writes=w)

    def red(self, out, a, op=ALU.add):
        self.S.op("dve", lambda e: e.tensor_reduce(out=out.ap, in_=a.ap, axis=AX.X, op=op),
                  reads=[a.k], writes=[out.k])

    def recip(self, out, a):
        self.S.op("dve", lambda e: e.reciprocal(out=out.ap, in_=a.ap), reads=[a.k], writes=[out.k])

    def cp(self, eng, out, a):
        if eng == "act":
            f = lambda e: e.copy(out=out.ap, in_=a.ap)
        else:
            f = lambda e: e.tensor_copy(out=out.ap, in_=a.ap)
        self.S.op(eng, f, reads=[a.k], writes=[out.k])

    def memset(self, eng, out, val):
        self.S.op(eng, lambda e: e.memset(out.ap, val), writes=[out.k])

    def mm(self, out, lhsT, rhs, start, stop, inc=None):
        self.S.op("pe", lambda e: e.matmul(out.ap, lhsT=lhsT.ap, rhs=rhs.ap, start=start, stop=stop),
                  reads=[lhsT.k, rhs.k], writes=[out.k], inc=(stop if inc is None else inc))

    def tr(self, out, a, ident, inc=True):
        self.S.op("pe", lambda e: e.transpose(out.ap, a.ap, ident.ap), reads=[a.k, ident.k], writes=[out.k], inc=inc)

    def load(self, dst, src_ap, q="sp", part=False, **kw):
        if q == "pool":
            kw.setdefault("max_dma_last_dim", 4096)
        self.S.dma(q, dst.ap, src_ap, dst.k, True, part=part, **kw)

    def store(self, dst_ap, src, q="sp", **kw):
        self.S.dma(q, dst_ap, src.ap, src.k, False, **kw)

    def pow_(self, out, a, expo):
        n = a.ap.shape[-1] if len(a.ap.shape) == 2 else None
        e = self.chalf if expo == 0.5 else self.nhalf
        self.tt("pool", out, a, e[:, 0:n], ALU.pow)

    def setup_consts(self):
        self.ident_f = self.sb(self.gs, "ident_f", [128, 128], F32)
        self.ident_b = self.sb(self.gs, "ident_b", [128, 128], BF16)
        self.nhalf = self.sb(self.gs, "nhalf", [128, 16], F32)
        self.chalf = self.sb(self.gs, "chalf", [128, 16], F32)
        self.ones_f = self.sb(self.gs, "ones_f", [128, 128], F32)
        self.memset("pool", self.nhalf[:], -0.5)
        self.memset("pool", self.chalf[:], 0.5)
        self.memset("pool", self.ones_f[:], 1.0)
        self.load(self.ident_f[:], self.I["ident"][:, :])
        self.cp("dve", self.ident_b[:], self.ident_f[:])

    def mod_phase(self):
        S = self.S
        with ExitStack() as st:
            craw = self.sb(st, "craw", [128, 2, 8], F32)
            sc = self.sb(st, "sc", [128, 8, 2], F32)
            wt = [self.sb(st, "modw%d" % i, [128, 8, 512], F32) for i in range(2)]
            bias = self.sb(st, "modbias", [2, NMOD * D], F32)
            res = self.sb(st, "modres", [2, NMOD * D], F32)
            pss = [self.ps(st, "modps%d" % i, [2, 512], F32) for i in range(2)]
            self.load(craw[:, 0, :], self.I["c"].rearrange("(p k) -> p k", k=8), part=True)
            self.load(craw[:, 1, :], self.I["c_ctx"].rearrange("(p k) -> p k", k=8), part=True)
            for m in range(2):
                self.act(sc[:, :, m], craw[:, m, :], AF.Silu)
            for i in range(self.cfg.get("l0", 0), self.cfg.get("l1", DEPTH)):
                wv = self.I["mod_w"][i].rearrange("(p k) n -> p k n", k=8)
                self.load(bias[:], self.I["mod_b"][i].partition_broadcast(2))
                for cb in range(18):
                    w = wt[cb % 2]
                    p = pss[cb % 2]
                    self.load(w[:], wv[:, :, cb * 512:(cb + 1) * 512])
                    for kc in range(8):
                        self.mm(p[:], sc[:, kc, :], w[:, kc, :], kc == 0, kc == 7)
                    self.tt("dve", res[:, cb * 512:(cb + 1) * 512], p[:], bias[:, cb * 512:(cb + 1) * 512], ALU.add)
                self.store(self.MOD[i], res[:])
        S.barrier()

    def bload(self, dst, row_ap):
        self.load(dst[:], row_ap.partition_broadcast(128))

    def alloc_common(self, st, nx=2):
        self.tmpa = self.sb(st, "tmpa", [128, D], F32)
        self.tmpb = self.sb(st, "tmpb", [128, D], F32)
        self.mvA = self.sb(st, "mvA", [128, D], F32)
        self.mvS = self.sb(st, "mvS", [128, D], F32)
        self.mvG = self.sb(st, "mvG", [128, D], F32)
        self.xt = [self.sb(st, "xt%d" % j, [128, D], F32) for j in range(nx)]
        self.nb = self.sb(st, "nb", [128, D], BF16)
        self.stat = self.sb(st, "stat", [128, 8], F32)
        self.pT = self.ps(st, "pT", [128, 8, 128], BF16)
        self.py = self.ps(st, "py", [128, D], F32)

    def set_mod(self, i, m, mA, gA, mS, mG, gG, cG):
        def mod(idx):
            return self.MOD[i, m, idx * D:(idx + 1) * D]
        self.bload(self.tmpa, mod(mA))
        self.bload(self.tmpb, self.I["norm_g"][i, gA])
        self.stt(self.mvA[:], self.tmpa[:], 1.0, self.tmpb[:], ALU.add, ALU.mult)
        self.bload(self.mvS, mod(mS))
        self.bload(self.tmpa, mod(mG))
        self.bload(self.tmpb, self.I["norm_g"][i, gG])
        self.stt(self.mvG[:], self.tmpa[:], float(cG), self.tmpb[:], ALU.mult, ALU.mult)

    def rms_rstd(self, src, junk, ss, rstd, n=D):
        self.act(junk, src, AF.Square, accum=ss)
        self.ts("dve", rstd, ss, 1.0 / n, EPS, ALU.mult, ALU.add)
        self.tt("pool", rstd, rstd, self.nhalf[:, 0:1], ALU.pow)

    def norm_tile(self, x, dst):
        st = self.stat
        self.rms_rstd(x[:], self.nb[:], st[:, 0:1], st[:, 1:2])
        self.stt(self.tmpa[:], x[:], st[:, 1:2], self.mvA[:], ALU.mult, ALU.mult)
        self.tt("pool", dst, self.tmpa[:], self.mvS[:], ALU.add)

    def transpose_cols(self, dstT, col0, src, nchunk=8):
        for kc in range(nchunk):
            self.tr(self.pT[:, kc, :], src[:, kc * 128:(kc + 1) * 128], self.ident_b[:], inc=(kc == nchunk - 1))
        self.cp("act", dstT[:, 0:nchunk, col0:col0 + 128], self.pT[:, 0:nchunk, :])

    def resid_tile(self, x, dst_ap):
        st = self.stat
        self.rms_rstd(self.py[:], self.tmpb[:], st[:, 2:3], st[:, 3:4])
        self.stt(self.tmpb[:], self.py[:], st[:, 3:4], self.mvG[:], ALU.mult, ALU.mult)
        self.tt("pool", x[:], self.tmpb[:], x[:], ALU.add)
        self.store(dst_ap, x[:])

    def load_w_bf16(self, dst, src, nk):
        sv = src.rearrange("(k p) n -> p k n", p=128)
        for kc in range(nk):
            self.load(dst[:, kc, :], sv[:, kc, :], q="pool", part=True)

    def ffn_phase(self, i, h, src, dst, tiles):
        S = self.S
        mb = 0 if h == 0 else 6
        gb = 0 if h == 0 else 4
        with ExitStack() as st:
            W1 = self.sb(st, "W1", [128, 8, DFF], BF16)
            W3 = self.sb(st, "W3", [128, 8, DFF], BF16)
            W2 = self.sb(st, "W2", [128, NFC, D], BF16)
            self.load_w_bf16(W1, self.I["ffn_w1"][i, h], 8)
            self.load_w_bf16(W3, self.I["ffn_w3"][i, h], 8)
            self.load_w_bf16(W2, self.I["ffn_w2"][i, h], NFC)
            self.alloc_common(st, nx=4)
            nT = [self.sb(st, "nT%d" % j, [128, 8, 256], BF16) for j in range(2)]
            gT = self.sb(st, "gT", [128, NFC, 256], BF16)
            sl = [self.sb(st, "sl%d" % j, [128, 256], BF16) for j in range(2)]
            ph = [self.ps(st, "ph%d" % j, [128, 256], F32) for j in range(4)]
            groups = []
            lat = [t for t in tiles if t < 32]
            ctx = [t for t in tiles if t >= 32]
            for lst in (lat, ctx):
                groups += [lst[a:a + 2] for a in range(0, len(lst), 2)]
            cur_m = None
            for gi_, grp in enumerate(groups):
                m = 0 if grp[0] < 32 else 1
                if m != cur_m:
                    self.set_mod(i, m, mb + 1, gb + 0, mb + 0, mb + 2, gb + 1, 0.5)
                    cur_m = m
                ntk = 128 * len(grp)
                nTg = nT[gi_ % 2]
                for j, tt_ in enumerate(grp):
                    x = self.xt[(gi_ % 2) * 2 + j]
                    self.load(x[:], src(tt_))
                    self.norm_tile(x, self.nb[:])
                    self.transpose_cols(nTg, j * 128, self.nb)
                for fc in range(NFC):
                    p1 = ph[(fc % 2) * 2]
                    p3 = ph[(fc % 2) * 2 + 1]
                    for (W, p) in ((W1, p1), (W3, p3)):
                        for kc in range(8):
                            self.mm(p[:, :ntk], W[:, kc, fc * 128:(fc + 1) * 128], nTg[:, kc, :ntk], kc == 0, kc == 7)
                    s = sl[fc % 2]
                    self.act(s[:, :ntk], p1[:, :ntk], AF.Silu)
                    self.tt("dve", gT[:, fc, :ntk], s[:, :ntk], p3[:, :ntk], ALU.mult)
                for j, tt_ in enumerate(grp):
                    x = self.xt[(gi_ % 2) * 2 + j]
                    for hf in range(2):
                        for fc in range(NFC):
                            self.mm(self.py[:, hf * 512:(hf + 1) * 512], gT[:, fc, j * 128:(j + 1) * 128],
                                    W2[:, fc, hf * 512:(hf + 1) * 512], fc == 0, fc == NFC - 1)
                    self.resid_tile(x, dst(tt_))
        S.barrier()

    def rope(self, dst, src, H, tt_):
        cs, sn = self.rcos, self.rsin
        self.load(cs[:], self.I["rope_cos"][tt_ * 128:(tt_ + 1) * 128, :])
        self.load(sn[:], self.I["rope_sin"][tt_ * 128:(tt_ + 1) * 128, :])

        def hv(t, w, dt_cols=None):
            v = t.t[:, 0:H * 64].rearrange("p (h a w f) -> p h a w f", h=H, a=2, w=2)
            return A(v[:, :, :, w, :], t.k)

        def bc(t):
            v = t.t[:, :].rearrange("p (a f) -> p a f", a=2).unsqueeze(1).to_broadcast([128, H, 2, 16])
            return A(v, t.k)

        def tv(t):
            return A(t.t[:, 0:H * 32].rearrange("p (h a f) -> p h a f", h=H, a=2), t.k)
        t1, t2 = hv(src, 0), hv(src, 1)
        c, s = bc(cs), bc(sn)
        u1, u2, u3, u4 = (tv(u) for u in self.ru)
        self.tt("dve", u1, t1, c, ALU.mult)
        self.tt("pool", u2, t2, s, ALU.mult)
        self.tt("dve", hv(dst, 0), u1, u2, ALU.subtract)
        self.tt("pool", u3, t2, c, ALU.mult)
        self.tt("dve", u4, t1, s, ALU.mult)
        self.tt("pool", hv(dst, 1), u3, u4, ALU.add)

    def alloc_rope(self, st):
        self.rcos = self.sb(st, "rcos", [128, 32], F32)
        self.rsin = self.sb(st, "rsin", [128, 32], F32)
        self.ru = [self.sb(st, "ru%d" % j, [128, 512], F32) for j in range(4)]

    def head_rms(self, dst, src, H, gvec):
        sq = self.tmpb
        self.tt("dve", sq[:, 0:H * 64], src, src, ALU.mult)
        ss = self.hstat
        self.red(ss[:, 0:H], A(sq.t[:, 0:H * 64].rearrange("p (h k) -> p h k", h=H), sq.k))
        self.ts("dve", ss[:, 0:H], ss[:, 0:H], 1.0 / HD, EPS, ALU.mult, ALU.add)
        self.tt("pool", ss[:, 0:H], ss[:, 0:H], self.nhalf[:, 0:H], ALU.pow)
        s3 = A(src.ap.rearrange("p (h k) -> p h k", h=H), src.k)
        d3 = A(dst.ap.rearrange("p (h k) -> p h k", h=H), dst.k)
        rb = A(ss.t[:, 0:H].unsqueeze(2).to_broadcast([128, H, HD]), ss.k)
        gb = A(gvec.t[:, :].unsqueeze(1).to_broadcast([128, H, HD]), gvec.k)
        self.tt("dve", d3, s3, rb, ALU.mult)
        self.tt("pool", d3, d3, gb, ALU.mult)

    def gattn_phase(self, i, j, src, dst, with_ctx, qtiles=None):
        S = self.S
        I = self.I
        with ExitStack() as st:
            Wq = self.sb(st, "Wq", [128, 8, D], BF16)
            Wkv = self.sb(st, "Wkv", [128, 8, 2 * DKV], BF16)
            Wo = self.sb(st, "Wo", [HD, NH, D], BF16)
            self.load_w_bf16(Wq, I["gattn_wq"][j], 8)
            wk = I["gattn_wk"][j].rearrange("(k p) n -> p k n", p=128)
            wv = I["gattn_wv"][j].rearrange("(k p) n -> p k n", p=128)
            for kc in range(8):
                self.load(Wkv[:, kc, 0:DKV], wk[:, kc, :], q="pool", part=True)
                self.load(Wkv[:, kc, DKV:2 * DKV], wv[:, kc, :], q="pool", part=True)
            wo = I["gattn_wo"][j].rearrange("(h p) n -> p h n", p=HD)
            for h in range(NH):
                self.load(Wo[:, h, :], wo[:, h, :], q="pool", part=True)
            self.alloc_common(st, nx=2)
            self.alloc_rope(st)
            self.hstat = self.sb(st, "hstat", [128, 16], F32)
            gq = self.sb(st, "gq", [128, HD], F32)
            gk = self.sb(st, "gk", [128, HD], F32)
            negm = self.sb(st, "negm", [128, 4], F32)
            self.bload(gq, I["gattn_q_norm"][j])
            self.bload(gk, I["gattn_k_norm"][j])
            S.op("dve", lambda e: e.tensor_reduce(out=negm.t[:, 0:1], in_=gq.t[:, :], axis=AX.X, op=ALU.max,
                                                  apply_absolute_value=True), reads=["gq"], writes=["negm"])
            S.op("dve", lambda e: e.tensor_reduce(out=negm.t[:, 1:2], in_=gk.t[:, :], axis=AX.X, op=ALU.max,
                                                  apply_absolute_value=True), reads=["gk"], writes=["negm"])
            self.tt("dve", negm[:, 2:3], negm[:, 0:1], negm[:, 1:2], ALU.mult)
            self.ts("dve", negm[:, 3:4], negm[:, 2:3], -8.0, None, ALU.mult)
            KT = self.sb(st, "KT", [HD, NKV, NTOK], BF16)
            V = self.sb(st, "V", [128, NT, NKV, HD + 1], BF16)
            self.memset("pool", V[:], 1.0)
            nT = self.sb(st, "nT", [128, 8, 128], BF16)
            qf = self.sb(st, "qf", [128, D], F32)
            qb = self.sb(st, "qb", [128, D], BF16)
            QT = self.sb(st, "QT", [HD, NH, 512], BF16)
            OT = self.sb(st, "OT", [HD, NH, 512], BF16)
            pexp = [self.sb(st, "pexp%d" % a, [128, 512], BF16) for a in range(3)]
            rd = self.sb(st, "rd", [HD + 1, 512], F32)
            bcs = self.sb(st, "bcs", [HD, 512], F32)
            pss = [self.ps(st, "pss%d" % a, [128, 512], F32) for a in range(2)]
            po = [self.ps(st, "po%d" % a, [HD + 1, 512], F32) for a in range(2)]
            pbc = self.ps(st, "pbc", [HD, 512], F32)
            x = self.xt[0]

            def prologue(tt_):
                self.load(x[:], src(tt_))
                self.norm_tile(x, self.nb[:])
                self.transpose_cols(nT, 0, self.nb)

            cur_m = None
            for tt_ in range(NT):
                m = 0 if tt_ < 32 else 1
                if m != cur_m:
                    self.set_mod(i, m, 4, 2, 3, 5, 3, 1.0)
                    cur_m = m
                prologue(tt_)
                pkv = self.py
                for kc in range(8):
                    self.mm(pkv[:, 0:512], nT[:, kc, :], Wkv[:, kc, :], kc == 0, kc == 7)
                self.cp("act", qf[:, 0:DKV], pkv[:, 0:DKV])
                self.head_rms(qf[:, 0:DKV], qf[:, 0:DKV], NKV, gk)
                if m == 0:
                    self.rope(qb, qf, NKV, tt_)
                else:
                    self.cp("act", qb[:, 0:DKV], qf[:, 0:DKV])
                for h in range(NKV):
                    self.tr(self.pT[0:HD, h, :], qb[:, h * HD:(h + 1) * HD], self.ident_b[:], inc=(h == NKV - 1))
                self.cp("act", KT[:, :, tt_ * 128:(tt_ + 1) * 128], self.pT[0:HD, 0:NKV, :])
                self.cp("act", V[:, tt_, :, 0:HD],
                        A(pkv.t[:, DKV:2 * DKV].rearrange("p (h k) -> p h k", h=NKV), pkv.k))
            groups = [list(range(a, a + 4)) for a in range(0, 32, 4)]
            if with_ctx:
                groups.append([32, 33])
            if qtiles is not None:
                groups = qtiles
            cur_m = 1
            pi = 0
            for grp in groups:
                m = 0 if grp[0] < 32 else 1
                if m != cur_m:
                    self.set_mod(i, m, 4, 2, 3, 5, 3, 1.0)
                    cur_m = m
                nq = 128 * len(grp)
                for jj, tt_ in enumerate(grp):
                    prologue(tt_)
                    for hf in range(2):
                        for kc in range(8):
                            self.mm(self.py[:, hf * 512:(hf + 1) * 512], nT[:, kc, :], Wq[:, kc, hf * 512:(hf + 1) * 512],
                                    kc == 0, kc == 7)
                    self.cp("act", qf[:], self.py[:])
                    self.head_rms(qf[:], qf[:], NH, gq)
                    if m == 0:
                        self.rope(qb, qf, NH, tt_)
                    else:
                        self.cp("act", qb[:], qf[:])
                    for hb in range(2):
                        for h8 in range(8):
                            h = hb * 8 + h8
                            self.tr(self.pT[0:HD, h8, :], qb[:, h * HD:(h + 1) * HD], self.ident_b[:], inc=(h8 == 7))
                        self.cp("act", QT[:, hb * 8:(hb + 1) * 8, jj * 128:(jj + 1) * 128], self.pT[0:HD, :, :])
                kts = list(range(NT)) if m == 0 else [32, 33]
                for h in range(NH):
                    kvh = h // (NH // NKV)
                    pacc = po[h % 2]
                    nk_ = len(kts)
                    slots = []
                    for ki in range(nk_ + 1):
                        if ki < nk_:
                            kt = kts[ki]
                            ps_ = pss[pi % 2]
                            pe_ = pexp[pi % 3]
                            pi += 1
                            self.mm(ps_[:, :nq], KT[:, kvh, kt * 128:(kt + 1) * 128], QT[:, h, :nq], True, True)
                            self.act(pe_[:, :nq], ps_[:, :nq], AF.Exp, scale=ATTN_SCALE, bias=negm[:, 3:4])
                            slots.append((kt, pe_))
                        if ki >= 1:
                            kt0, pe0 = slots[ki - 1]
                            self.mm(pacc[:, :nq], V[:, kt0, kvh, :], pe0[:, :nq], ki - 1 == 0, ki - 1 == nk_ - 1)
                    self.recip(rd[HD:HD + 1, :nq], pacc[HD:HD + 1, :nq])
                    self.mm(pbc[:, :nq], self.ones_f[HD:HD + 1, 0:HD], rd[HD:HD + 1, :nq], True, True)
                    self.cp("act", bcs[:, :nq], pbc[:, :nq])
                    self.tt("dve", OT[:, h, :nq], pacc[0:HD, :nq], bcs[:, :nq], ALU.mult)
                for jj, tt_ in enumerate(grp):
                    for hf in range(2):
                        for h in range(NH):
                            self.mm(self.py[:, hf * 512:(hf + 1) * 512], OT[:, h, jj * 128:(jj + 1) * 128],
                                    Wo[:, h, hf * 512:(hf + 1) * 512], h == 0, h == NH - 1)
                    x2 = self.xt[1]
                    self.load(x2[:], src(tt_))
                    self.resid_tile(x2, dst(tt_))
        S.barrier()

    def wattn_phase(self, i, j, src, dst, with_ctx, qtiles=None):
        S = self.S
        I = self.I
        with ExitStack() as st:
            Wq = self.sb(st, "Wq", [128, 8, D], BF16)
            Wkv = self.sb(st, "Wkv", [128, 8, 2 * DKV], BF16)
            Wo = self.sb(st, "Wo", [128, 8, D], BF16)
            self.load_w_bf16(Wq, I["wattn_wq"][j], 8)
            self.load_w_bf16(Wo, I["wattn_wo"][j], 8)
            wk = I["wattn_wk"][j].rearrange("(k p) n -> p k n", p=128)
            wv = I["wattn_wv"][j].rearrange("(k p) n -> p k n", p=128)
            for kc in range(8):
                self.load(Wkv[:, kc, 0:DKV], wk[:, kc, :], q="pool", part=True)
                self.load(Wkv[:, kc, DKV:2 * DKV], wv[:, kc, :], q="pool", part=True)
            self.alloc_common(st, nx=2)
            self.alloc_rope(st)
            sinkb = self.sb(st, "sinkb", [128, NH], F32)
            self.bload(sinkb, I["wattn_sink"][j])
            mask = self.sb(st, "mask", [128, 384], F32)
            self.load(mask[:], I["wmask"][:, :])
            KT = self.sb(st, "KT", [HD, NKV, NTOK], BF16)
            V = self.sb(st, "V", [128, NT, NKV, HD], BF16)
            nT = self.sb(st, "nT", [128, 8, 128], BF16)
            qf = self.sb(st, "qf", [128, D], F32)
            qb = self.sb(st, "qb", [128, D], BF16)
            QT = self.sb(st, "QT", [HD, NH, 128], BF16)
            sc = self.sb(st, "sc", [128, 640], F32)
            P = self.sb(st, "P", [128, 640], BF16)
            PT = self.sb(st, "PT", [128, 5, 128], BF16)
            O = self.sb(st, "O", [128, D], BF16)
            OT = self.sb(st, "OT", [128, 8, 128], BF16)
            sm = self.sb(st, "sm", [128, 8], F32)
            pl = self.ps(st, "pl", [128, 512], F32)
            pc = self.ps(st, "pc", [128, 512], F32)
            pPT = self.ps(st, "pPT", [128, 8, 128], BF16)
            pov = self.ps(st, "pov", [128, 512], F32)
            x = self.xt[0]

            def prologue(tt_):
                self.load(x[:], src(tt_))
                self.norm_tile(x, self.nb[:])
                self.transpose_cols(nT, 0, self.nb)

            cur_m = None
            for tt_ in range(NT):
                m = 0 if tt_ < 32 else 1
                if m != cur_m:
                    self.set_mod(i, m, 4, 2, 3, 5, 3, 1.0)
                    cur_m = m
                prologue(tt_)
                pkv = self.py
                for kc in range(8):
                    self.mm(pkv[:, 0:512], nT[:, kc, :], Wkv[:, kc, :], kc == 0, kc == 7)
                if m == 0:
                    self.cp("act", qf[:, 0:DKV], pkv[:, 0:DKV])
                    self.rope(qb, qf, NKV, tt_)
                else:
                    self.cp("act", qb[:, 0:DKV], pkv[:, 0:DKV])
                for h in range(NKV):
                    self.tr(self.pT[0:HD, h, :], qb[:, h * HD:(h + 1) * HD], self.ident_b[:], inc=(h == NKV - 1))
                self.cp("act", KT[:, :, tt_ * 128:(tt_ + 1) * 128], self.pT[0:HD, 0:NKV, :])
                self.cp("act", V[:, tt_, :, :],
                        A(pkv.t[:, DKV:2 * DKV].rearrange("p (h k) -> p h k", h=NKV), pkv.k))
            qt = list(range(32)) + ([32, 33] if with_ctx else [])
            if qtiles is not None:
                qt = qtiles
            cur_m = 1
            for tt_ in qt:
                m = 0 if tt_ < 32 else 1
                if m != cur_m:
                    self.set_mod(i, m, 4, 2, 3, 5, 3, 1.0)
                    cur_m = m
                prologue(tt_)
                for hf in range(2):
                    for kc in range(8):
                        self.mm(self.py[:, hf * 512:(hf + 1) * 512], nT[:, kc, :], Wq[:, kc, hf * 512:(hf + 1) * 512],
                                kc == 0, kc == 7)
                if m == 0:
                    self.cp("act", qf[:], self.py[:])
                    self.rope(qb, qf, NH, tt_)
                else:
                    self.cp("act", qb[:], self.py[:])
                for hb in range(2):
                    for h8 in range(8):
                        h = hb * 8 + h8
                        self.tr(self.pT[0:HD, h8, :], qb[:, h * HD:(h + 1) * HD], self.ident_b[:], inc=(h8 == 7))
                    self.cp("act", QT[:, hb * 8:(hb + 1) * 8, :], self.pT[0:HD, :, :])
                if m == 0:
                    b0 = max(tt_ - 1, 0)
                    b1 = min(tt_ + 1, 31)
                    nloc = (b1 - b0 + 1) * 128
                    moff = 0 if tt_ > 0 else 128
                    ktiles = list(range(b0, b1 + 1)) + [32, 33]
                else:
                    nloc = 0
                    ktiles = [32, 33]
                nk = nloc + 256
                for h in range(NH):
                    kvh = h // (NH // NKV)
                    if nloc:
                        self.mm(pl[:, :nloc], QT[:, h, :], KT[:, kvh, b0 * 128:(b1 + 1) * 128], True, True)
                        self.stt(sc[:, :nloc], pl[:, :nloc], ATTN_SCALE, mask[:, moff:moff + nloc], ALU.mult, ALU.add)
                    self.mm(pc[:, 0:256], QT[:, h, :], KT[:, kvh, SEQ:NTOK], True, True)
                    self.act(sc[:, nloc:nk], pc[:, 0:256], AF.Copy, scale=ATTN_SCALE)
                    self.red(sm[:, 0:1], sc[:, :nk], ALU.max)
                    self.tt("dve", sm[:, 0:1], sm[:, 0:1], sinkb[:, h:h + 1], ALU.max)
                    self.ts("dve", sm[:, 1:2], sm[:, 0:1], -1.0, None, ALU.mult)
                    self.act(P[:, :nk], sc[:, :nk], AF.Exp, bias=sm[:, 1:2], accum=sm[:, 2:3])
                    self.act(sm[:, 3:4], sinkb[:, h:h + 1], AF.Exp, bias=sm[:, 1:2])
                    self.tt("dve", sm[:, 4:5], sm[:, 2:3], sm[:, 3:4], ALU.add)
                    self.recip(sm[:, 5:6], sm[:, 4:5])
                    nkt = nk // 128
                    for kt in range(nkt):
                        self.tr(pPT[:, kt, :], P[:, kt * 128:(kt + 1) * 128], self.ident_b[:], inc=(kt == nkt - 1))
                    self.cp("act", PT[:, 0:nkt, :], pPT[:, 0:nkt, :])
                    for kt in range(nkt):
                        self.mm(pov[:, 0:HD], PT[:, kt, :], V[:, ktiles[kt], kvh, :], kt == 0, kt == nkt - 1)
                    self.ts("dve", O[:, h * HD:(h + 1) * HD], pov[:, 0:HD], sm[:, 5:6], None, ALU.mult)
                self.transpose_cols(OT, 0, O)
                for hf in range(2):
                    for kc in range(8):
                        self.mm(self.py[:, hf * 512:(hf + 1) * 512], OT[:, kc, :], Wo[:, kc, hf * 512:(hf + 1) * 512],
                                kc == 0, kc == 7)
                x2 = self.xt[1]
                self.load(x2[:], src(tt_))
                self.resid_tile(x2, dst(tt_))
        S.barrier()

    def hm_tile(self, arr, pos0):
        return arr[:, pos0:pos0 + 128, :].rearrange("h t k -> t h k")

    @staticmethod
    def v3(a, H=NH):
        return A(a.ap.rearrange("p (h k) -> p h k", h=H), a.k)

    def rwkv_norm_pass(self, i, src):
        S = self.S
        with ExitStack() as st:
            self.alloc_common(st, nx=2)
            nf = [self.sb(st, "nf%d" % a, [128, D], F32) for a in range(2)]
            z = self.sb(st, "zrow", [1, D], F32)
            self.memset("pool", z[:], 0.0)
            for r in (0, SEQ + 1, NSROWS - 1):
                self.store(self.NS[r:r + 1, :], z[:])
            cur_m = None
            for tt_ in range(NT):
                m = 0 if tt_ < 32 else 1
                if m != cur_m:
                    self.set_mod(i, m, 4, 2, 3, 5, 3, 1.0)
                    cur_m = m
                x = self.xt[tt_ % 2]
                self.load(x[:], src(tt_))
                self.norm_tile(x, nf[tt_ % 2][:])
                r0 = nsrow_of_tile(tt_)
                self.store(self.NS[r0:r0 + 128, :], nf[tt_ % 2][:])
        S.barrier()

    def alloc_shift(self, st):
        self.ncur = self.sb(st, "ncur", [128, D], F32)
        self.nprev = self.sb(st, "nprev", [128, D], F32)
        self.nnext = self.sb(st, "nnext", [128, D], F32)
        self.nbb = self.sb(st, "nbb", [128, D], BF16)
        self.xxb = self.sb(st, "xxb", [128, D], BF16)
        self.nTx = self.sb(st, "nTx", [128, 16, 128], BF16)
        self.tmpa = self.sb(st, "tmpa", [128, D], F32)
        self.tmpb = self.sb(st, "tmpb", [128, D], F32)
        self.pT = self.ps(st, "pT", [128, 8, 128], BF16)
        self.hstat = self.sb(st, "hstat", [128, 16], F32)
        self.muT = self.sb(st, "muT", [128, 8, 6], F32)

    def load_mu(self, j):
        for m in range(6):
            for kc in range(8):
                self.load(self.muT[:, kc, m:m + 1],
                          self.I["rwkv_mu"][j, m, kc * 128:(kc + 1) * 128].rearrange("(p o) -> p o", o=1), part=True)

    def shift_tile(self, tt_):
        r0 = nsrow_of_tile(tt_)
        self.load(self.ncur[:], self.NS[r0:r0 + 128, :])
        self.load(self.nprev[:], self.NS[r0 - 1:r0 + 127, :])
        self.load(self.nnext[:], self.NS[r0 + 1:r0 + 129, :])
        self.tt("pool", self.tmpa[:], self.nprev[:], self.nnext[:], ALU.add)
        self.stt(self.xxb[:], self.tmpa[:], 0.5, self.ncur[:], ALU.mult, ALU.subtract)
        self.cp("act", self.nbb[:], self.ncur[:])
        for half, srcb in ((0, self.nbb), (1, self.xxb)):
            for kc in range(8):
                self.tr(self.pT[:, kc, :], srcb[:, kc * 128:(kc + 1) * 128], self.ident_b[:], inc=(kc == 7))
            self.cp("act", self.nTx[:, half * 8:(half + 1) * 8, :], self.pT[:, :, :])

    def load_mixed_w(self, dst, col0, ncol, src, m):
        sv = src.rearrange("(k p) n -> p k n", p=128)
        for kc in range(8):
            self.load(dst[:, kc, col0:col0 + ncol], sv[:, kc, :], q="pool", part=True)
            self.load(dst[:, 8 + kc, col0:col0 + ncol], sv[:, kc, :], q="pool", part=True)
        for kc in range(8):
            self.ts("dve" if kc % 2 else "pool", dst[:, 8 + kc, col0:col0 + ncol], dst[:, 8 + kc, col0:col0 + ncol],
                    self.muT[:, kc, m:m + 1], None, ALU.mult)

    def rwkv_feat1(self, j):
        S, I = self.S, self.I
        with ExitStack() as st:
            self.alloc_shift(st)
            self.load_mu(j)
            Wbig = self.sb(st, "Wbig", [128, 16, 3 * D], BF16)
            self.load_mixed_w(Wbig, 0, D, I["rwkv_wr"][j], 0)
            self.load_mixed_w(Wbig, D, D, I["rwkv_wk"][j], 2)
            self.load_mixed_w(Wbig, 2 * D, D, I["rwkv_wv"][j], 3)
            kk_b = self.sb(st, "kk_b", [128, D], F32)
            self.bload(kk_b, I["rwkv_k_k"][j])
            if j > 0:
                WLv = self.sb(st, "WLv", [128, 16, 32], BF16)
                self.load_mixed_w(WLv, 0, 32, I["rwkv_v1"][j - 1], 3)
                v2s = self.sb(st, "v2s", [32, D], BF16)
                self.load(v2s[:], I["rwkv_v2"][j - 1], q="pool", part=True)
                v0b = self.sb(st, "v0b", [128, D], F32)
                self.bload(v0b, I["rwkv_v0"][j - 1])
                hvb = self.sb(st, "hvb", [32, 128], BF16)
                ph = self.ps(st, "ph", [128, 512], F32)
            ob = [self.sb(st, "ob%d" % a, [128, D], F32) for a in range(4)]
            pys = [self.ps(st, "pya", [128, D], F32), self.ps(st, "pyb", [128, D], F32)]
            VAj = self.VA[j]
            for tt_ in range(NT):
                pos0 = pos_of_tile(tt_)
                self.shift_tile(tt_)
                for qi in range(3):
                    py = pys[qi % 2]
                    for hf in range(2):
                        for c in range(16):
                            self.mm(py[:, hf * 512:(hf + 1) * 512], self.nTx[:, c, :],
                                    Wbig[:, c, qi * D + hf * 512:qi * D + (hf + 1) * 512], c == 0, c == 15)
                    if qi == 0:
                        self.cp("act", ob[0][:], py[:])
                        self.store(self.hm_tile(self.RH, pos0), self.v3(ob[0][:]))
                    elif qi == 1:
                        kraw = ob[1]
                        self.cp("act", kraw[:], py[:])
                        self.store(self.KRAW[pos0:pos0 + 128, :], kraw[:])
                        self.tt("dve", self.tmpb[:], kraw[:], kk_b[:], ALU.mult)
                        self.tt("pool", self.tmpa[:], self.tmpb[:], self.tmpb[:], ALU.mult)
                        hs = self.hstat
                        self.red(hs[:, 0:16], self.v3(self.tmpa[:]))
                        self.tt("pool", hs[:, 0:16], hs[:, 0:16], self.chalf[:, 0:16], ALU.pow)
                        self.ts("dve", hs[:, 0:16], hs[:, 0:16], 1e-12, None, ALU.max)
                        self.recip(hs[:, 0:16], hs[:, 0:16])
                        self.ts("dve", hs[:, 0:16], hs[:, 0:16], -1.0, None, ALU.mult)
                        rb = A(hs.t[:, 0:16].unsqueeze(2).to_broadcast([128, NH, HD]), hs.k)
                        self.tt("dve", self.v3(ob[2][:]), self.v3(self.tmpb[:]), rb, ALU.mult)
                        self.store(self.hm_tile(self.AVH, pos0), self.v3(ob[2][:]))
                    else:
                        vf = ob[3]
                        self.cp("act", vf[:], py[:])
                        if j > 0:
                            for c in range(16):
                                self.mm(ph[0:32, 0:128], WLv[:, c, :], self.nTx[:, c, :], c == 0, c == 15)
                            self.cp("act", hvb[:], ph[0:32, 0:128])
                            py2 = pys[0]
                            for hf in range(2):
                                self.mm(py2[:, hf * 512:(hf + 1) * 512], hvb[:], v2s[:, hf * 512:(hf + 1) * 512], True, True)
                            self.tt("dve", self.tmpa[:], py2[:], v0b[:], ALU.add)
                            self.act(self.tmpa[:], self.tmpa[:], AF.Sigmoid)
                            self.load(self.tmpb[:], self.VA[0][pos0:pos0 + 128, :])
                            self.tt("pool", self.tmpb[:], self.tmpb[:], vf[:], ALU.subtract)
                            self.tt("dve", self.tmpb[:], self.tmpb[:], self.tmpa[:], ALU.mult)
                            self.tt("pool", vf[:], vf[:], self.tmpb[:], ALU.add)
                        self.store(VAj[pos0:pos0 + 128, :], vf[:])
        S.barrier()

    def rwkv_feat2(self, j):
        S, I = self.S, self.I
        with ExitStack() as st:
            self.alloc_shift(st)
            self.load_mu(j)
            WL1 = self.sb(st, "WL1", [128, 16, 576], BF16)
            for d in range(2):
                self.load_mixed_w(WL1, d * 64, 64, I["rwkv_w1"][j, d], 1)
                self.load_mixed_w(WL1, 128 + d * 64, 64, I["rwkv_a1"][j, d], 4)
                self.load_mixed_w(WL1, 256 + d * 160, 160, I["rwkv_g1"][j, d], 5)
            w2s = self.sb(st, "w2s", [128, D], BF16)
            a2s = self.sb(st, "a2s", [128, D], BF16)
            g2s = self.sb(st, "g2s", [128, 2, 2, D], BF16)
            self.load(w2s[:], I["rwkv_w2"][j].rearrange("d r n -> (d r) n"), q="pool", part=True)
            self.load(a2s[:], I["rwkv_a2"][j].rearrange("d r n -> (d r) n"), q="pool", part=True)
            for d in range(2):
                self.load(g2s[:, d, 0, :], I["rwkv_g2"][j, d, 0:128, :], q="pool", part=True)
                self.load(g2s[0:32, d, 1, :], I["rwkv_g2"][j, d, 128:160, :], q="pool", part=True)
            w0b = self.sb(st, "w0b", [128, 2, D], F32)
            a0b = self.sb(st, "a0b", [128, 2, D], F32)
            kab = self.sb(st, "kab", [128, D], F32)
            for d in range(2):
                self.load(w0b[:, d, :], I["rwkv_w0"][j, d].partition_broadcast(128), part=True)
                self.load(a0b[:, d, :], I["rwkv_a0"][j, d].partition_broadcast(128), part=True)
            self.bload(kab, I["rwkv_k_a"][j])
            hwT = self.sb(st, "hwT", [128, 128], BF16)
            haT = self.sb(st, "haT", [128, 128], BF16)
            hgT = [self.sb(st, "hgT%d" % d, [128, 2, 128], BF16) for d in range(2)]
            kraw = self.sb(st, "kraw", [128, D], F32)
            av = self.sb(st, "av", [128, D], F32)
            ob = [self.sb(st, "ob%d" % a, [128, D], F32) for a in range(6)]
            lwt = self.sb(st, "lwt", [128, D], F32)
            gG = self.sb(st, "gG", [128, D], F32)
            gI = self.sb(st, "gI", [128, D], F32)
            gX = self.sb(st, "gX", [128, D], F32)
            rt2 = self.sb(st, "rt2", [128, D], F32)
            tri = self.sb(st, "tri", [128, 2, 128], F32)
            self.load(tri[:], I["tri"].rearrange("d s t -> s d t"))
            pys = [self.ps(st, "pya", [128, D], F32), self.ps(st, "pyb", [128, D], F32)]
            phs = [self.ps(st, "ph%d" % a, [128, 512], F32) for a in range(2)]
            for tt_ in range(NT):
                pos0 = pos_of_tile(tt_)
                self.shift_tile(tt_)
                self.load(kraw[:], self.KRAW[pos0:pos0 + 128, :])
                self.load(self.v3(av[:]), self.hm_tile(self.AVH, pos0))
                self.load(self.v3(rt2[:]), self.hm_tile(self.RH, pos0))
                for c in range(16):
                    self.mm(phs[0][:, 0:128], WL1[:, c, 0:128], self.nTx[:, c, :], c == 0, c == 15)
                self.act(hwT[:], phs[0][:, 0:128], AF.Tanh)
                for c in range(16):
                    self.mm(phs[1][:, 0:128], WL1[:, c, 128:256], self.nTx[:, c, :], c == 0, c == 15)
                self.cp("act", haT[:], phs[1][:, 0:128])
                for d in range(2):
                    g0 = 256 + d * 160
                    for c in range(16):
                        self.mm(phs[0][:, 0:128], WL1[:, c, g0:g0 + 128], self.nTx[:, c, :], c == 0, c == 15)
                    self.act(hgT[d][:, 0, :], phs[0][:, 0:128], AF.Sigmoid)
                    for c in range(16):
                        self.mm(phs[1][0:32, 0:128], WL1[:, c, g0 + 128:g0 + 160], self.nTx[:, c, :], c == 0, c == 15)
                    self.act(hgT[d][0:32, 1, :], phs[1][0:32, 0:128], AF.Sigmoid)
                for d in range(2):
                    ps_ = slice(d * 64, (d + 1) * 64)
                    py = pys[0]
                    for hf in range(2):
                        self.mm(py[:, hf * 512:(hf + 1) * 512], hwT[ps_, :], w2s[ps_, hf * 512:(hf + 1) * 512], True, True)
                    self.tt("dve", self.tmpa[:], py[:], w0b[:, d, :], ALU.add)
                    self.act(self.tmpa[:], self.tmpa[:], AF.Sigmoid)
                    self.ts("dve", lwt[:], self.tmpa[:], -0.6065306597126334, None, ALU.mult)
                    for hf in range(2):
                        cs = slice(hf * 512, (hf + 1) * 512)
                        self.mm(py[:, cs], tri[:, d, :], lwt[:, cs], True, True)
                    self.act(gG[:], py[:], AF.Exp)
                    self.act(gI[:], py[:], AF.Exp, scale=-1.0)
                    self.tt("dve", self.tmpa[:], py[:], lwt[:], ALU.subtract)
                    self.act(gX[:], self.tmpa[:], AF.Exp)
                    self.store(self.hm_tile(self.WH[d], pos0), self.v3(gG[:]))
                    self.tt("pool", ob[0][:], av[:], gX[:], ALU.mult)
                    self.store(self.hm_tile(self.AH[d], pos0), self.v3(ob[0][:]))
                    self.tt("dve", ob[4][:], rt2[:], gG[:], ALU.mult)
                    self.store(self.hm_tile(self.RS[d], pos0), self.v3(ob[4][:]))
                    py = pys[1]
                    for hf in range(2):
                        self.mm(py[:, hf * 512:(hf + 1) * 512], haT[ps_, :], a2s[ps_, hf * 512:(hf + 1) * 512], True, True)
                    self.tt("dve", self.tmpb[:], py[:], a0b[:, d, :], ALU.add)
                    self.act(self.tmpb[:], self.tmpb[:], AF.Sigmoid)
                    self.stt(ob[1][:], av[:], -1.0, self.tmpb[:], ALU.mult, ALU.mult)
                    self.tt("pool", ob[1][:], ob[1][:], gI[:], ALU.mult)
                    self.store(self.hm_tile(self.BH[d], pos0), self.v3(ob[1][:]))
                    self.stt(self.tmpb[:], self.tmpb[:], -1.0, kab[:], ALU.add, ALU.mult)
                    self.tt("pool", self.tmpb[:], self.tmpb[:], kraw[:], ALU.mult)
                    self.tt("dve", ob[2][:], self.tmpb[:], kraw[:], ALU.add)
                    self.store(self.hm_tile(self.KH[d], pos0), self.v3(ob[2][:]))
                    self.tt("pool", ob[5][:], ob[2][:], gI[:], ALU.mult)
                    self.store(self.hm_tile(self.KS[d], pos0), self.v3(ob[5][:]))
                    py = pys[0]
                    for hf in range(2):
                        cs = slice(hf * 512, (hf + 1) * 512)
                        self.mm(py[:, cs], hgT[d][:, 0, :], g2s[:, d, 0, cs], True, False)
                        self.mm(py[:, cs], hgT[d][0:32, 1, :], g2s[0:32, d, 1, cs], False, True)
                    self.cp("act", ob[3][:], py[:])
                    self.store(self.GT[d][pos0:pos0 + 128, :], ob[3][:])
        S.barrier()

    def rwkv_scan(self, j, nblocks=None):
        S = self.S
        TB = 16
        with ExitStack() as st:
            St = self.sb(st, "St", [128, 2, 8, 64], F32)
            t1 = self.sb(st, "sc_t1", [128, 2, 8, 64], F32)
            t2 = self.sb(st, "sc_t2", [128, 2, 8, 64], F32)
            kv = [self.sb(st, "sc_kv%d" % a, [128, 2, 8, 64], F32) for a in range(2)]
            sa = self.sb(st, "sc_sa", [128, 16], F32)
            qs = ("b", "k", "a", "r")
            bufs = [{q: self.sb(st, "sc_%s%d" % (q, bi), [128, 2, TB, 64], F32) for q in qs} for bi in range(2)]
            gend = [self.sb(st, "sc_ge%d" % bi, [128, 2, 64], F32) for bi in range(2)]
            vbuf = [self.sb(st, "sc_v%d" % bi, [128, 2, TB, 8], F32) for bi in range(2)]
            ybuf = [self.sb(st, "sc_y%d" % bi, [128, 2, TB, 8], F32) for bi in range(2)]
            self.memset("dve", St[:], 0.0)
            VAj = self.VA[j]
            NB = NPOS // TB if nblocks is None else nblocks

            def lo_of(b):
                return (TB * b, (240 - TB * b) if b < 16 else (4592 - TB * b))

            def loads(b):
                bi = b % 2
                for d, lo in enumerate(lo_of(b)):
                    arrs = {"b": self.BH[d], "k": self.KS[d], "a": self.AH[d], "r": self.RS[d]}
                    pe_ = lo + TB - 1 if d == 0 else lo
                    gsrc = bass.AP(tensor=self.WH[d].tensor, offset=self.WH[d].offset + pe_ * 64,
                                   ap=[[NPOS * 64, 16], [0, 8], [1, 64]])
                    self.load(gend[bi][:, d, :], gsrc, part=True)
                    for q in qs:
                        arr = arrs[q]
                        src = bass.AP(tensor=arr.tensor, offset=arr.offset + lo * 64,
                                      ap=[[NPOS * 64, 16], [0, 8], [1, TB * 64]])
                        dstb = bufs[bi][q]
                        self.load(A(dstb.t[:, d, :, :].rearrange("p t k -> p (t k)"), dstb.k), src, part=True)
                    self.load(vbuf[bi][:, d, :, :], VAj[lo:lo + TB, :].rearrange("t (p l) -> p t l", l=8), part=True)

            loads(0)
            for b in range(NB):
                bi = b % 2
                if b + 1 < NB:
                    loads(b + 1)
                for s in range(TB):
                    c0, c1 = s, TB - 1 - s

                    def opnd(q):
                        return bufs[bi][q].cust(c0 * 64, [[2 * TB * 64, 128], [TB * 64 + (c1 - c0) * 64, 2], [0, 8], [1, 64]])
                    vb = vbuf[bi].cust(c0 * 8, [[2 * TB * 8, 128], [TB * 8 + (c1 - c0) * 8, 2], [1, 8], [0, 64]])
                    yo = ybuf[bi].cust(c0 * 8, [[2 * TB * 8, 128], [TB * 8 + (c1 - c0) * 8, 2], [1, 8]])
                    kvt = kv[s % 2]
                    self.tt("pool", kvt[:], vb, opnd("k"), ALU.mult)
                    self.tt("dve", t1[:], St[:], opnd("a"), ALU.mult)
                    self.red(A(sa.t[:, :].rearrange("p (d l) -> p d l", d=2), sa.k), t1[:])
                    sab = A(sa.t[:, :].rearrange("p (d l) -> p d l", d=2).unsqueeze(3).to_broadcast([128, 2, 8, 64]), sa.k)
                    self.tt("dve", t2[:], sab, opnd("b"), ALU.mult)
                    self.tt("dve", St[:], St[:], t2[:], ALU.add)
                    self.tt("dve", St[:], St[:], kvt[:], ALU.add)
                    self.tt("dve", t1[:], St[:], opnd("r"), ALU.mult)
                    self.red(yo, t1[:])
                geb = A(gend[bi].t[:, :, :].unsqueeze(2).to_broadcast([128, 2, 8, 64]), gend[bi].k)
                self.tt("dve", St[:], St[:], geb, ALU.mult)
                for d, lo in enumerate(lo_of(b)):
                    self.store(self.YT[d][lo:lo + TB, :].rearrange("t (p l) -> p t l", l=8), ybuf[bi][:, d, :, :])
        S.barrier()

    def rwkv_readout(self, i, j, src, dst, tiles):
        S, I = self.S, self.I
        with ExitStack() as st:
            self.alloc_common(st, nx=2)
            self.hstat = self.sb(st, "hstat", [128, 32], F32)
            Wo = self.sb(st, "Wo", [128, 8, D], BF16)
            self.load_w_bf16(Wo, I["rwkv_wo"][j], 8)
            lnw = self.sb(st, "lnw", [128, 2, D], F32)
            lnb = self.sb(st, "lnb", [128, 2, D], F32)
            rkb = self.sb(st, "rkb", [128, D], F32)
            for d in range(2):
                self.load(lnw[:, d, :], I["rwkv_ln_w"][j, d].partition_broadcast(128), part=True)
                self.load(lnb[:, d, :], I["rwkv_ln_b"][j, d].partition_broadcast(128), part=True)
            self.bload(rkb, I["rwkv_r_k"][j].rearrange("h k -> (h k)"))
            rt = self.sb(st, "rt", [128, D], F32)
            vt = self.sb(st, "vt", [128, D], F32)
            yt = self.sb(st, "yt", [128, D], F32)
            kt_ = self.sb(st, "kt", [128, D], F32)
            gt = self.sb(st, "gt", [128, D], F32)
            oacc = self.sb(st, "oacc", [128, D], F32)
            obf = self.sb(st, "obf", [128, D], BF16)
            OT = self.sb(st, "OT", [128, 8, 128], BF16)
            hs = self.hstat
            VAj = self.VA[j]
            cur_m = None
            for tt_ in tiles:
                m = 0 if tt_ < 32 else 1
                if m != cur_m:
                    self.set_mod(i, m, 4, 2, 3, 5, 3, 1.0)
                    cur_m = m
                pos0 = pos_of_tile(tt_)
                self.load(self.v3(rt[:]), self.hm_tile(self.RH, pos0))
                self.load(vt[:], VAj[pos0:pos0 + 128, :])
                for d in range(2):
                    self.load(yt[:], self.YT[d][pos0:pos0 + 128, :])
                    self.load(self.v3(kt_[:]), self.hm_tile(self.KH[d], pos0))
                    self.load(gt[:], self.GT[d][pos0:pos0 + 128, :])
                    ta, tb = self.tmpa, self.tmpb
                    self.red(hs[:, 0:16], self.v3(yt[:]))
                    self.ts("dve", hs[:, 0:16], hs[:, 0:16], 1.0 / HD, None, ALU.mult)
                    mb_ = A(hs.t[:, 0:16].unsqueeze(2).to_broadcast([128, NH, HD]), hs.k)
                    self.tt("dve", self.v3(ta[:]), self.v3(yt[:]), mb_, ALU.subtract)
                    self.tt("pool", tb[:], ta[:], ta[:], ALU.mult)
                    self.red(hs[:, 16:32], self.v3(tb[:]))
                    self.ts("dve", hs[:, 16:32], hs[:, 16:32], 1.0 / HD, GN_EPS, ALU.mult, ALU.add)
                    self.tt("pool", hs[:, 16:32], hs[:, 16:32], self.nhalf[:, 0:16], ALU.pow)
                    rb_ = A(hs.t[:, 16:32].unsqueeze(2).to_broadcast([128, NH, HD]), hs.k)
                    self.tt("dve", self.v3(ta[:]), self.v3(ta[:]), rb_, ALU.mult)
                    self.tt("pool", ta[:], ta[:], lnw[:, d, :], ALU.mult)
                    self.tt("dve", ta[:], ta[:], lnb[:, d, :], ALU.add)
                    self.tt("pool", tb[:], rt[:], kt_[:], ALU.mult)
                    self.tt("dve", tb[:], tb[:], rkb[:], ALU.mult)
                    self.red(hs[:, 0:16], self.v3(tb[:]))
                    bb_ = A(hs.t[:, 0:16].unsqueeze(2).to_broadcast([128, NH, HD]), hs.k)
                    self.tt("dve", self.v3(tb[:]), self.v3(vt[:]), bb_, ALU.mult)
                    self.tt("pool", ta[:], ta[:], tb[:], ALU.add)
                    if d == 0:
                        self.tt("dve", oacc[:], ta[:], gt[:], ALU.mult)
                    else:
                        self.tt("dve", ta[:], ta[:], gt[:], ALU.mult)
                        self.tt("pool", obf[:], ta[:], oacc[:], ALU.add)
                self.transpose_cols(OT, 0, obf)
                for hf in range(2):
                    for kc in range(8):
                        self.mm(self.py[:, hf * 512:(hf + 1) * 512], OT[:, kc, :], Wo[:, kc, hf * 512:(hf + 1) * 512],
                                kc == 0, kc == 7)
                x = self.xt[0]
                self.load(x[:], src(tt_))
                self.resid_tile(x, dst(tt_))
        S.barrier()

    def build(self):
        nc = self.nc
        cfg = self.cfg
        shapes = dict(
            x=[SEQ, D], ctx=[CTXL, D], c=[D], c_ctx=[D], ident=[128, 128], wmask=[128, 384], tri=[2, 128, 128],
            rope_cos=[SEQ, 32], rope_sin=[SEQ, 32],
            mod_w=[DEPTH, D, NMOD * D], mod_b=[DEPTH, NMOD * D], norm_g=[DEPTH, 6, D],
            ffn_w1=[DEPTH, 2, D, DFF], ffn_w3=[DEPTH, 2, D, DFF], ffn_w2=[DEPTH, 2, DFF, D],
            rwkv_mu=[2, 6, D], rwkv_wr=[2, D, D], rwkv_wk=[2, D, D], rwkv_wv=[2, D, D], rwkv_wo=[2, D, D],
            rwkv_k_k=[2, D], rwkv_k_a=[2, D], rwkv_r_k=[2, 16, 64], rwkv_w0=[2, 2, D], rwkv_w1=[2, 2, D, 64],
            rwkv_w2=[2, 2, 64, D], rwkv_a0=[2, 2, D], rwkv_a1=[2, 2, D, 64], rwkv_a2=[2, 2, 64, D],
            rwkv_g1=[2, 2, D, 160], rwkv_g2=[2, 2, 160, D], rwkv_ln_w=[2, 2, D], rwkv_ln_b=[2, 2, D],
            rwkv_v0=[1, D], rwkv_v1=[1, D, 32], rwkv_v2=[1, 32, D],
            gattn_wq=[1, D, D], gattn_wk=[1, D, DKV], gattn_wv=[1, D, DKV], gattn_wo=[1, D, D],
            gattn_q_norm=[1, HD], gattn_k_norm=[1, HD],
            wattn_wq=[1, D, D], wattn_wk=[1, D, DKV], wattn_wv=[1, D, DKV], wattn_wo=[1, D, D], wattn_sink=[1, NH],
        )
        for k, shp in shapes.items():
            self.dram_in(k, shp)
        self.Y = nc.dram_tensor("y", [SEQ, D], F32, kind="ExternalOutput").ap()
        self.XS = self.scratch("xs", [NTOK, D])
        self.MOD = self.scratch("modv", [DEPTH, 2, NMOD * D])
        self.NS = self.scratch("ns", [NSROWS, D])
        self.RH = self.scratch("rh", [NH, NPOS, HD])
        self.AVH = self.scratch("avh", [NH, NPOS, HD])
        self.WH = [self.scratch("wh%d" % d, [NH, NPOS, HD]) for d in range(2)]
        self.BH = [self.scratch("bh%d" % d, [NH, NPOS, HD]) for d in range(2)]
        self.KH = [self.scratch("kh%d" % d, [NH, NPOS, HD]) for d in range(2)]
        self.KS = [self.scratch("ks%d" % d, [NH, NPOS, HD]) for d in range(2)]
        self.AH = [self.scratch("ah%d" % d, [NH, NPOS, HD]) for d in range(2)]
        self.RS = [self.scratch("rs%d" % d, [NH, NPOS, HD]) for d in range(2)]
        self.KRAW = self.scratch("kraw_d", [NPOS, D])
        self.VA = [self.scratch("va%d" % a, [NPOS, D]) for a in range(2)]
        self.GT = [self.scratch("gt%d" % d, [NPOS, D]) for d in range(2)]
        self.YT = [self.scratch("yt%d" % d, [NPOS, D]) for d in range(2)]
        dbg = cfg.get("dbg", [])
        self.DBG = {n: nc.dram_tensor("dbg_" + n, [NTOK, D], F32, kind="ExternalOutput").ap() for n in dbg}

        self.setup_consts()
        self.mod_phase()

        def src0(tt_):
            if tt_ < 32:
                return self.I["x"][tt_ * 128:(tt_ + 1) * 128, :]
            return self.I["ctx"][(tt_ - 32) * 128:(tt_ - 31) * 128, :]

        def xs(tt_):
            return self.XS[tt_ * 128:(tt_ + 1) * 128, :]

        def mk(name):
            d = self.DBG[name]
            return lambda tt_: d[tt_ * 128:(tt_ + 1) * 128, :]

        def yout(tt_):
            return self.Y[tt_ * 128:(tt_ + 1) * 128, :]

        l0, l1 = cfg.get("l0", 0), cfg.get("l1", DEPTH)
        ft = cfg.get("ffn_tiles")
        for i in range(l0, l1):
            last = i == DEPTH - 1
            kind, j = i % 3, i // 3
            alltiles = list(range(NT))
            s_in = src0 if i == l0 else xs
            if not cfg.get("skip_f1"):
                self.ffn_phase(i, 0, s_in, xs, alltiles if ft is None else ft)
                s_in = xs
            if cfg.get("stop") == "f1":
                break
            mt = list(range(32)) if last else alltiles
            if kind == 0:
                self.rwkv_norm_pass(i, s_in)
                self.rwkv_feat1(j)
                self.rwkv_feat2(j)
                self.rwkv_scan(j, cfg.get("nblocks"))
                self.rwkv_readout(i, j, s_in, xs, mt if cfg.get("mix_tiles") is None else cfg["mix_tiles"])
            elif kind == 1:
                self.gattn_phase(i, j, s_in, xs, not last, cfg.get("qgroups"))
            else:
                self.wattn_phase(i, j, s_in, xs, not last, cfg.get("qtiles"))
            if cfg.get("stop") == "mix":
                break
            self.ffn_phase(i, 1, xs, yout if last else xs, mt if ft is None else ft)
        if dbg:
            with ExitStack() as st:
                buf = self.sb(st, "dbgbuf", [128, D], F32)
                for n in dbg:
                    srcarr = {"xs": self.XS}.get(n)
                    if srcarr is None:
                        continue
                    for tt_ in cfg.get("dbg_tiles", range(NT)):
                        self.load(buf[:], srcarr[tt_ * 128:(tt_ + 1) * 128, :])
                        self.store(self.DBG[n][tt_ * 128:(tt_ + 1) * 128, :], buf[:])
            self.S.barrier()
        self.S.barrier()
        self.gs.close()
        return nc


def build_nc(cfg):
    nc = bass.Bass("TRN2", target_bir_lowering=False)
    kb = KB(nc, cfg)
    kb.build()
    print("instructions:", kb.S.nins, "waits:", kb.S.nwaits, "dma sems:", len(kb.S.dsem))
    return nc, list(kb.I.keys())


def host_consts():
    ident = np.eye(128, dtype=np.float32)
    pos = np.arange(SEQ)
    row = (pos // 64).astype(np.float32)
    col = (pos % 64).astype(np.float32)
    inv_freq = (np.float32(10000.0) ** (-np.arange(16, dtype=np.float32) / np.float32(16))).astype(np.float32)
    ang = np.stack([row, col], axis=-1)[:, :, None] * inv_freq
    rc = np.cos(ang).astype(np.float32).reshape(SEQ, 32)
    rs = np.sin(ang).astype(np.float32).reshape(SEQ, 32)
    qq = np.arange(128)[:, None]
    kk = np.arange(128)[None, :]
    maskL = np.where(kk >= qq, 0.0, MASKV).astype(np.float32)
    maskR = np.where(kk <= qq, 0.0, MASKV).astype(np.float32)
    wmask = np.concatenate([maskL, np.zeros((128, 128), np.float32), maskR], axis=1)
    si = np.arange(128)[:, None]
    ti = np.arange(128)[None, :]
    same = (si // 16) == (ti // 16)
    tri = np.stack([(same & (si <= ti)), (same & (si >= ti))]).astype(np.float32)
    return dict(ident=ident, rope_cos=rc, rope_sin=rs, wmask=wmask, tri=tri)


def make_in_maps(inputs, names, ncores=8):
    consts = host_consts()
    maps = []
    for b in range(ncores):
        m = {}
        for k, v in inputs.items():
            if k not in names:
                continue
            v = np.ascontiguousarray(v, dtype=np.float32)
            if k in ("x", "c", "ctx"):
                m[k] = np.ascontiguousarray(v[b])
            else:
                m[k] = v
        for k, v in consts.items():
            if k in names:
                m[k] = v
        maps.append(m)
    return maps


def kernel(**inputs):
    nc, names = build_nc({})
    maps = make_in_maps(inputs, names, 8)
    res = run_bass_kernel_spmd(nc, maps, core_ids=list(range(8)))
    return np.stack([r["y"] for r in res.results], axis=0).astype(np.float32)
```

```python
import numpy as np
from contextlib import ExitStack
import concourse.bass as bass
import concourse.mybir as mybir
from concourse.bass_utils import run_bass_kernel_spmd

F32 = mybir.dt.float32
BF16 = mybir.dt.bfloat16
AF = mybir.ActivationFunctionType
ALU = mybir.AluOpType
AX = mybir.AxisListType

D = 1024
SEQ = 4096
CTXL = 256
NTOK = SEQ + CTXL
NT = NTOK // 128
DEPTH = 4
DFF = 2816
NFC = DFF // 128
NMOD = 9
EPS = 1e-6


class Sched:
    def __init__(self, nc):
        self.nc = nc
        self.stack = ExitStack()
        self.eng = {}
        for name, e in (("pe", nc.tensor), ("dve", nc.vector), ("act", nc.scalar),
                        ("pool", nc.gpsimd), ("sp", nc.sync)):
            sem = self.stack.enter_context(nc.semaphore("s_" + name))
            self.eng[name] = dict(e=e, sem=sem, count=0, waited={}, pending=False)
        self.lastw = {}
        self.readers = {}
        self.dsem = {}
        self.nwaits = 0
        self.nins = 0

    def _wait(self, E, need):
        for sid, (sem, val) in need.items():
            if E["waited"].get(sid, 0) < val:
                E["e"].wait_ge(sem, val)
                E["waited"][sid] = val
                self.nwaits += 1

    @staticmethod
    def _merge(need, toks, skip=None):
        for sid, (sem, val) in toks.items():
            if sid == skip:
                continue
            if sid not in need or need[sid][1] < val:
                need[sid] = (sem, val)

    def op(self, eng, fn, reads=(), writes=(), inc=True, skip_self=False):
        E = self.eng[eng]
        need = {}
        skip = eng if (eng == "pe" or skip_self) else None
        for k in reads:
            self._merge(need, self.lastw.get(k, {}), skip)
        for k in writes:
            self._merge(need, self.lastw.get(k, {}), skip)
            self._merge(need, self.readers.get(k, {}), skip)
        self._wait(E, need)
        ins = fn(E["e"])
        self.nins += 1
        if inc:
            E["count"] += 1
            ins.then_inc(E["sem"], 1)
            val = E["count"]
            E["pending"] = False
        else:
            val = E["count"] + 1
            E["pending"] = True
        tok = (E["sem"], val)
        for k in reads:
            self.readers.setdefault(k, {})[eng] = tok
        for k in writes:
            self.lastw[k] = {eng: tok}
            self.readers[k] = {}
        return ins

    def dma(self, q, out, in_, key, load, part=False, **kw):
        E = self.eng[q]
        sid = ("dma", key)
        need = {}
        self._merge(need, self.lastw.get(key, {}), sid if (load and part) else None)
        if load:
            self._merge(need, self.readers.get(key, {}), None)
        self._wait(E, need)
        if key not in self.dsem:
            sem = self.stack.enter_context(self.nc.semaphore("d_%d" % len(self.dsem)))
            self.dsem[key] = [sem, 0]
        ds = self.dsem[key]
        ins = E["e"].dma_start(out=out, in_=in_, **kw)
        self.nins += 1
        ds[1] += 16
        ins.then_inc(ds[0], 16)
        tok = (ds[0], ds[1])
        if load:
            self.lastw[key] = {sid: tok}
            self.readers[key] = {}
        else:
            self.readers.setdefault(key, {})[sid] = tok
        return ins

    def barrier(self):
        toks = {}
        for name, E in self.eng.items():
            assert not E["pending"], name
            if E["count"] > 0:
                toks[name] = (E["sem"], E["count"])
        for key, ds in self.dsem.items():
            if ds[1] > 0:
                toks[("dma", key)] = (ds[0], ds[1])
        for name, E in self.eng.items():
            self._wait(E, toks)
        self.lastw = {}
        self.readers = {}


HD = 64
NH = 16
NKV = 4
DKV = 256
ATTN_SCALE = HD ** -0.5
NPOS = NTOK
NSROWS = NTOK + 3
GN_EPS = 64e-5
MASKV = -30000.0


class A:
    __slots__ = ("ap", "k")

    def __init__(self, ap, k):
        self.ap = ap
        self.k = k


class T:
    def __init__(self, t, k):
        self.t = t
        self.k = k

    def __getitem__(self, idx):
        return A(self.t[idx], self.k)

    def cust(self, offset, dims):
        return A(bass.AP(tensor=self.t[:].tensor, offset=offset, ap=[list(d) for d in dims]), self.k)


def _ap(x):
    return x.ap if isinstance(x, A) else x


def _keys(*xs):
    return [x.k for x in xs if isinstance(x, A)]


def pos_of_tile(tt):
    return 256 + tt * 128 if tt < 32 else (tt - 32) * 128


def nsrow_of_tile(tt):
    return 1 + tt * 128 if tt < 32 else 4098 + (tt - 32) * 128


class KB:
    def __init__(self, nc, cfg):
        self.nc = nc
        self.cfg = cfg
        self.S = Sched(nc)
        self.gs = self.S.stack
        self.I = {}

    def dram_in(self, name, shape):
        self.I[name] = self.nc.dram_tensor(name, list(shape), F32, kind="ExternalInput").ap()
        return self.I[name]

    def scratch(self, name, shape):
        return self.nc.dram_tensor(name, list(shape), F32, kind="Internal").ap()

    def sb(self, st, name, shape, dt=F32):
        self.uid = getattr(self, "uid", 0) + 1
        return T(st.enter_context(self.nc.sbuf_tensor("%s_%d" % (name, self.uid), list(shape), dt)), name)

    def ps(self, st, name, shape, dt=F32):
        self.uid = getattr(self, "uid", 0) + 1
        return T(st.enter_context(self.nc.psum_tensor("%s_%d" % (name, self.uid), list(shape), dt)), name)

    def tt(self, eng, out, a, b, op):
        self.S.op(eng, lambda e: e.tensor_tensor(out=out.ap, in0=a.ap, in1=b.ap, op=op),
                  reads=_keys(a, b), writes=[out.k])

    def ts(self, eng, out, a, s1, s2, op0, op1=None):
        if op1 is None:
            f = lambda e: e.tensor_scalar(out=out.ap, in0=a.ap, scalar1=_ap(s1), scalar2=None, op0=op0)
        else:
            f = lambda e: e.tensor_scalar(out=out.ap, in0=a.ap, scalar1=_ap(s1), scalar2=_ap(s2), op0=op0, op1=op1)
        self.S.op(eng, f, reads=_keys(a, s1, s2), writes=[out.k])

    def stt(self, out, a, s, b, op0, op1):
        self.S.op("dve", lambda e: e.scalar_tensor_tensor(out=out.ap, in0=a.ap, scalar=_ap(s), in1=b.ap,
                                                          op0=op0, op1=op1),
                  reads=_keys(a, s, b), writes=[out.k])

    def act(self, out, a, func, scale=1.0, bias=None, accum=None, inc=True, skip_self=False):
        kw = {}
        if bias is not None:
            kw["bias"] = _ap(bias)
        if accum is not None:
            kw["accum_out"] = accum.ap
        w = [out.k] + ([accum.k] if accum is not None else [])
        self.S.op("act", lambda e: e.activation(out=out.ap, in_=a.ap, func=func, scale=_ap(scale), **kw),
                  reads=_keys(a, bias, scale), writes=w, inc=inc, skip_self=skip_self)

    def red(self, out, a, op=ALU.add):
        self.S.op("dve", lambda e: e.tensor_reduce(out=out.ap, in_=a.ap, axis=AX.X, op=op),
                  reads=[a.k], writes=[out.k])

    def recip(self, out, a):
        self.S.op("dve", lambda e: e.reciprocal(out=out.ap, in_=a.ap), reads=[a.k], writes=[out.k])

    def cp(self, eng, out, a):
        if eng == "act":
            f = lambda e: e.copy(out=out.ap, in_=a.ap)
        else:
            f = lambda e: e.tensor_copy(out=out.ap, in_=a.ap)
        self.S.op(eng, f, reads=[a.k], writes=[out.k])

    def memset(self, eng, out, val):
        self.S.op(eng, lambda e: e.memset(out.ap, val), writes=[out.k])

    def mm(self, out, lhsT, rhs, start, stop, inc=None):
        self.S.op("pe", lambda e: e.matmul(out.ap, lhsT=lhsT.ap, rhs=rhs.ap, start=start, stop=stop),
                  reads=[lhsT.k, rhs.k], writes=[out.k], inc=(stop if inc is None else inc))

    def tr(self, out, a, ident, inc=True):
        self.S.op("pe", lambda e: e.transpose(out.ap, a.ap, ident.ap), reads=[a.k, ident.k], writes=[out.k], inc=inc)

    def load(self, dst, src_ap, q="sp", part=False, **kw):
        if q == "pool":
            kw.setdefault("max_dma_last_dim", 4096)
        self.S.dma(q, dst.ap, src_ap, dst.k, True, part=part, **kw)

    def store(self, dst_ap, src, q="sp", **kw):
        self.S.dma(q, dst_ap, src.ap, src.k, False, **kw)

    def pow_(self, out, a, expo):
        n = a.ap.shape[-1] if len(a.ap.shape) == 2 else None
        e = self.chalf if expo == 0.5 else self.nhalf
        self.tt("pool", out, a, e[:, 0:n], ALU.pow)

    def setup_consts(self):
        self.ident_f = self.sb(self.gs, "ident_f", [128, 128], F32)
        self.ident_b = self.sb(self.gs, "ident_b", [128, 128], BF16)
        self.nhalf = self.sb(self.gs, "nhalf", [128, 16], F32)
        self.chalf = self.sb(self.gs, "chalf", [128, 16], F32)
        self.ones_f = self.sb(self.gs, "ones_f", [128, 128], F32)
        self.memset("pool", self.nhalf[:], -0.5)
        self.memset("pool", self.chalf[:], 0.5)
        self.memset("pool", self.ones_f[:], 1.0)
        self.load(self.ident_f[:], self.I["ident"][:, :])
        self.cp("dve", self.ident_b[:], self.ident_f[:])

    def mod_phase(self):
        S = self.S
        with ExitStack() as st:
            craw = self.sb(st, "craw", [128, 2, 8], F32)
            sc = self.sb(st, "sc", [128, 8, 2], F32)
            wt = [self.sb(st, "modw%d" % i, [128, 8, 512], F32) for i in range(2)]
            bias = self.sb(st, "modbias", [2, NMOD * D], F32)
            res = self.sb(st, "modres", [2, NMOD * D], F32)
            pss = [self.ps(st, "modps%d" % i, [2, 512], F32) for i in range(2)]
            self.load(craw[:, 0, :], self.I["c"].rearrange("(p k) -> p k", k=8), part=True)
            self.load(craw[:, 1, :], self.I["c_ctx"].rearrange("(p k) -> p k", k=8), part=True)
            for m in range(2):
                self.act(sc[:, :, m], craw[:, m, :], AF.Silu)
            for i in range(self.cfg.get("l0", 0), self.cfg.get("l1", DEPTH)):
                wv = self.I["mod_w"][i].rearrange("(p k) n -> p k n", k=8)
                self.load(bias[:], self.I["mod_b"][i].partition_broadcast(2))
                for cb in range(18):
                    w = wt[cb % 2]
                    p = pss[cb % 2]
                    self.load(w[:], wv[:, :, cb * 512:(cb + 1) * 512])
                    for kc in range(8):
                        self.mm(p[:], sc[:, kc, :], w[:, kc, :], kc == 0, kc == 7)
                    self.tt("dve", res[:, cb * 512:(cb + 1) * 512], p[:], bias[:, cb * 512:(cb + 1) * 512], ALU.add)
                self.store(self.MOD[i], res[:])
        S.barrier()

    def bload(self, dst, row_ap):
        self.load(dst[:], row_ap.partition_broadcast(128))

    def alloc_common(self, st, nx=2):
        self.tmpa = self.sb(st, "tmpa", [128, D], F32)
        self.tmpb = self.sb(st, "tmpb", [128, D], F32)
        self.mvA = self.sb(st, "mvA", [128, D], F32)
        self.mvS = self.sb(st, "mvS", [128, D], F32)
        self.mvG = self.sb(st, "mvG", [128, D], F32)
        self.xt = [self.sb(st, "xt%d" % j, [128, D], F32) for j in range(nx)]
        self.nb = self.sb(st, "nb", [128, D], BF16)
        self.stat = self.sb(st, "stat", [128, 8], F32)
        self.pT = self.ps(st, "pT", [128, 8, 128], BF16)
        self.py = self.ps(st, "py", [128, D], F32)

    def set_mod(self, i, m, mA, gA, mS, mG, gG, cG):
        def mod(idx):
            return self.MOD[i, m, idx * D:(idx + 1) * D]
        self.bload(self.tmpa, mod(mA))
        self.bload(self.tmpb, self.I["norm_g"][i, gA])
        self.stt(self.mvA[:], self.tmpa[:], 1.0, self.tmpb[:], ALU.add, ALU.mult)
        self.bload(self.mvS, mod(mS))
        self.bload(self.tmpa, mod(mG))
        self.bload(self.tmpb, self.I["norm_g"][i, gG])
        self.stt(self.mvG[:], self.tmpa[:], float(cG), self.tmpb[:], ALU.mult, ALU.mult)

    def rms_rstd(self, src, junk, ss, rstd, n=D):
        self.act(junk, src, AF.Square, accum=ss)
        self.ts("dve", rstd, ss, 1.0 / n, EPS, ALU.mult, ALU.add)
        self.tt("pool", rstd, rstd, self.nhalf[:, 0:1], ALU.pow)

    def norm_tile(self, x, dst):
        st = self.stat
        self.rms_rstd(x[:], self.nb[:], st[:, 0:1], st[:, 1:2])
        self.stt(self.tmpa[:], x[:], st[:, 1:2], self.mvA[:], ALU.mult, ALU.mult)
        self.tt("pool", dst, self.tmpa[:], self.mvS[:], ALU.add)

    def transpose_cols(self, dstT, col0, src, nchunk=8):
        for kc in range(nchunk):
            self.tr(self.pT[:, kc, :], src[:, kc * 128:(kc + 1) * 128], self.ident_b[:], inc=(kc == nchunk - 1))
        self.cp("act", dstT[:, 0:nchunk, col0:col0 + 128], self.pT[:, 0:nchunk, :])

    def resid_tile(self, x, dst_ap):
        st = self.stat
        self.rms_rstd(self.py[:], self.tmpb[:], st[:, 2:3], st[:, 3:4])
        self.stt(self.tmpb[:], self.py[:], st[:, 3:4], self.mvG[:], ALU.mult, ALU.mult)
        self.tt("pool", x[:], self.tmpb[:], x[:], ALU.add)
        self.store(dst_ap, x[:])

    def load_w_bf16(self, dst, src, nk):
        sv = src.rearrange("(k p) n -> p k n", p=128)
        for kc in range(nk):
            self.load(dst[:, kc, :], sv[:, kc, :], q="pool", part=True)

    def ffn_phase(self, i, h, src, dst, tiles):
        S = self.S
        mb = 0 if h == 0 else 6
        gb = 0 if h == 0 else 4
        with ExitStack() as st:
            W1 = self.sb(st, "W1", [128, 8, DFF], BF16)
            W3 = self.sb(st, "W3", [128, 8, DFF], BF16)
            W2 = self.sb(st, "W2", [128, NFC, D], BF16)
            self.load_w_bf16(W1, self.I["ffn_w1"][i, h], 8)
            self.load_w_bf16(W3, self.I["ffn_w3"][i, h], 8)
            self.load_w_bf16(W2, self.I["ffn_w2"][i, h], NFC)
            self.alloc_common(st, nx=4)
            nT = [self.sb(st, "nT%d" % j, [128, 8, 256], BF16) for j in range(2)]
            gT = self.sb(st, "gT", [128, NFC, 256], BF16)
            sl = [self.sb(st, "sl%d" % j, [128, 256], BF16) for j in range(2)]
            ph = [self.ps(st, "ph%d" % j, [128, 256], F32) for j in range(4)]
            groups = []
            lat = [t for t in tiles if t < 32]
            ctx = [t for t in tiles if t >= 32]
            for lst in (lat, ctx):
                groups += [lst[a:a + 2] for a in range(0, len(lst), 2)]
            cur_m = None
            for gi_, grp in enumerate(groups):
                m = 0 if grp[0] < 32 else 1
                if m != cur_m:
                    self.set_mod(i, m, mb + 1, gb + 0, mb + 0, mb + 2, gb + 1, 0.5)
                    cur_m = m
                ntk = 128 * len(grp)
                nTg = nT[gi_ % 2]
                for j, tt_ in enumerate(grp):
                    x = self.xt[(gi_ % 2) * 2 + j]
                    self.load(x[:], src(tt_))
                    self.norm_tile(x, self.nb[:])
                    self.transpose_cols(nTg, j * 128, self.nb)
                for fc in range(NFC):
                    p1 = ph[(fc % 2) * 2]
                    p3 = ph[(fc % 2) * 2 + 1]
                    for (W, p) in ((W1, p1), (W3, p3)):
                        for kc in range(8):
                            self.mm(p[:, :ntk], W[:, kc, fc * 128:(fc + 1) * 128], nTg[:, kc, :ntk], kc == 0, kc == 7)
                    s = sl[fc % 2]
                    self.act(s[:, :ntk], p1[:, :ntk], AF.Silu)
                    self.tt("dve", gT[:, fc, :ntk], s[:, :ntk], p3[:, :ntk], ALU.mult)
                for j, tt_ in enumerate(grp):
                    x = self.xt[(gi_ % 2) * 2 + j]
                    for hf in range(2):
                        for fc in range(NFC):
                            self.mm(self.py[:, hf * 512:(hf + 1) * 512], gT[:, fc, j * 128:(j + 1) * 128],
                                    W2[:, fc, hf * 512:(hf + 1) * 512], fc == 0, fc == NFC - 1)
                    self.resid_tile(x, dst(tt_))
        S.barrier()

    def rope(self, dst, src, H, tt_):
        cs, sn = self.rcos, self.rsin
        self.load(cs[:], self.I["rope_cos"][tt_ * 128:(tt_ + 1) * 128, :])
        self.load(sn[:], self.I["rope_sin"][tt_ * 128:(tt_ + 1) * 128, :])

        def hv(t, w, dt_cols=None):
            v = t.t[:, 0:H * 64].rearrange("p (h a w f) -> p h a w f", h=H, a=2, w=2)
            return A(v[:, :, :, w, :], t.k)

        def bc(t):
            v = t.t[:, :].rearrange("p (a f) -> p a f", a=2).unsqueeze(1).to_broadcast([128, H, 2, 16])
            return A(v, t.k)

        def tv(t):
            return A(t.t[:, 0:H * 32].rearrange("p (h a f) -> p h a f", h=H, a=2), t.k)
        t1, t2 = hv(src, 0), hv(src, 1)
        c, s = bc(cs), bc(sn)
        u1, u2, u3, u4 = (tv(u) for u in self.ru)
        self.tt("dve", u1, t1, c, ALU.mult)
        self.tt("pool", u2, t2, s, ALU.mult)
        self.tt("dve", hv(dst, 0), u1, u2, ALU.subtract)
        self.tt("pool", u3, t2, c, ALU.mult)
        self.tt("dve", u4, t1, s, ALU.mult)
        self.tt("pool", hv(dst, 1), u3, u4, ALU.add)

    def alloc_rope(self, st):
        self.rcos = self.sb(st, "rcos", [128, 32], F32)
        self.rsin = self.sb(st, "rsin", [128, 32], F32)
        self.ru = [self.sb(st, "ru%d" % j, [128, 512], F32) for j in range(4)]

    def head_rms(self, dst, src, H, gvec):
        sq = self.tmpb
        self.tt("dve", sq[:, 0:H * 64], src, src, ALU.mult)
        ss = self.hstat
        self.red(ss[:, 0:H], A(sq.t[:, 0:H * 64].rearrange("p (h k) -> p h k", h=H), sq.k))
        self.ts("dve", ss[:, 0:H], ss[:, 0:H], 1.0 / HD, EPS, ALU.mult, ALU.add)
        self.tt("pool", ss[:, 0:H], ss[:, 0:H], self.nhalf[:, 0:H], ALU.pow)
        s3 = A(src.ap.rearrange("p (h k) -> p h k", h=H), src.k)
        d3 = A(dst.ap.rearrange("p (h k) -> p h k", h=H), dst.k)
        rb = A(ss.t[:, 0:H].unsqueeze(2).to_broadcast([128, H, HD]), ss.k)
        gb = A(gvec.t[:, :].unsqueeze(1).to_broadcast([128, H, HD]), gvec.k)
        self.tt("dve", d3, s3, rb, ALU.mult)
        self.tt("pool", d3, d3, gb, ALU.mult)

    def gattn_phase(self, i, j, src, dst, with_ctx, qtiles=None):
        S = self.S
        I = self.I
        with ExitStack() as st:
            Wq = self.sb(st, "Wq", [128, 8, D], BF16)
            Wkv = self.sb(st, "Wkv", [128, 8, 2 * DKV], BF16)
            Wo = self.sb(st, "Wo", [HD, NH, D], BF16)
            self.load_w_bf16(Wq, I["gattn_wq"][j], 8)
            wk = I["gattn_wk"][j].rearrange("(k p) n -> p k n", p=128)
            wv = I["gattn_wv"][j].rearrange("(k p) n -> p k n", p=128)
            for kc in range(8):
                self.load(Wkv[:, kc, 0:DKV], wk[:, kc, :], q="pool", part=True)
                self.load(Wkv[:, kc, DKV:2 * DKV], wv[:, kc, :], q="pool", part=True)
            wo = I["gattn_wo"][j].rearrange("(h p) n -> p h n", p=HD)
            for h in range(NH):
                self.load(Wo[:, h, :], wo[:, h, :], q="pool", part=True)
            self.alloc_common(st, nx=2)
            self.alloc_rope(st)
            self.hstat = self.sb(st, "hstat", [128, 16], F32)
            gq = self.sb(st, "gq", [128, HD], F32)
            gk = self.sb(st, "gk", [128, HD], F32)
            negm = self.sb(st, "negm", [128, 4], F32)
            self.bload(gq, I["gattn_q_norm"][j])
            self.bload(gk, I["gattn_k_norm"][j])
            S.op("dve", lambda e: e.tensor_reduce(out=negm.t[:, 0:1], in_=gq.t[:, :], axis=AX.X, op=ALU.max,
                                                  apply_absolute_value=True), reads=["gq"], writes=["negm"])
            S.op("dve", lambda e: e.tensor_reduce(out=negm.t[:, 1:2], in_=gk.t[:, :], axis=AX.X, op=ALU.max,
                                                  apply_absolute_value=True), reads=["gk"], writes=["negm"])
            self.tt("dve", negm[:, 2:3], negm[:, 0:1], negm[:, 1:2], ALU.mult)
            self.ts("dve", negm[:, 3:4], negm[:, 2:3], -8.0, None, ALU.mult)
            KT = self.sb(st, "KT", [HD, NKV, NTOK], BF16)
            V = self.sb(st, "V", [128, NT, NKV, HD + 1], BF16)
            self.memset("pool", V[:], 1.0)
            nT = self.sb(st, "nT", [128, 8, 128], BF16)
            qf = self.sb(st, "qf", [128, D], F32)
            qb = self.sb(st, "qb", [128, D], BF16)
            QT = self.sb(st, "QT", [HD, NH, 512], BF16)
            OT = self.sb(st, "OT", [HD, NH, 512], BF16)
            pexp = [self.sb(st, "pexp%d" % a, [128, 512], BF16) for a in range(3)]
            rd = self.sb(st, "rd", [HD + 1, 512], F32)
            bcs = self.sb(st, "bcs", [HD, 512], F32)
            pss = [self.ps(st, "pss%d" % a, [128, 512], F32) for a in range(2)]
            po = [self.ps(st, "po%d" % a, [HD + 1, 512], F32) for a in range(2)]
            pbc = self.ps(st, "pbc", [HD, 512], F32)
            x = self.xt[0]

            def prologue(tt_):
                self.load(x[:], src(tt_))
                self.norm_tile(x, self.nb[:])
                self.transpose_cols(nT, 0, self.nb)

            cur_m = None
            for tt_ in range(NT):
                m = 0 if tt_ < 32 else 1
                if m != cur_m:
                    self.set_mod(i, m, 4, 2, 3, 5, 3, 1.0)
                    cur_m = m
                prologue(tt_)
                pkv = self.py
                for kc in range(8):
                    self.mm(pkv[:, 0:512], nT[:, kc, :], Wkv[:, kc, :], kc == 0, kc == 7)
                self.cp("act", qf[:, 0:DKV], pkv[:, 0:DKV])
                self.head_rms(qf[:, 0:DKV], qf[:, 0:DKV], NKV, gk)
                if m == 0:
                    self.rope(qb, qf, NKV, tt_)
                else:
                    self.cp("act", qb[:, 0:DKV], qf[:, 0:DKV])
                for h in range(NKV):
                    self.tr(self.pT[0:HD, h, :], qb[:, h * HD:(h + 1) * HD], self.ident_b[:], inc=(h == NKV - 1))
                self.cp("act", KT[:, :, tt_ * 128:(tt_ + 1) * 128], self.pT[0:HD, 0:NKV, :])
                self.cp("act", V[:, tt_, :, 0:HD],
                        A(pkv.t[:, DKV:2 * DKV].rearrange("p (h k) -> p h k", h=NKV), pkv.k))
            groups = [list(range(a, a + 4)) for a in range(0, 32, 4)]
            if with_ctx:
                groups.append([32, 33])
            if qtiles is not None:
                groups = qtiles
            cur_m = 1
            pi = 0
            for grp in groups:
                m = 0 if grp[0] < 32 else 1
                if m != cur_m:
                    self.set_mod(i, m, 4, 2, 3, 5, 3, 1.0)
                    cur_m = m
                nq = 128 * len(grp)
                for jj, tt_ in enumerate(grp):
                    prologue(tt_)
                    for hf in range(2):
                        for kc in range(8):
                            self.mm(self.py[:, hf * 512:(hf + 1) * 512], nT[:, kc, :], Wq[:, kc, hf * 512:(hf + 1) * 512],
                                    kc == 0, kc == 7)
                    self.cp("act", qf[:], self.py[:])
                    self.head_rms(qf[:], qf[:], NH, gq)
                    if m == 0:
                        self.rope(qb, qf, NH, tt_)
                    else:
                        self.cp("act", qb[:], qf[:])
                    for hb in range(2):
                        for h8 in range(8):
                            h = hb * 8 + h8
                            self.tr(self.pT[0:HD, h8, :], qb[:, h * HD:(h + 1) * HD], self.ident_b[:], inc=(h8 == 7))
                        self.cp("act", QT[:, hb * 8:(hb + 1) * 8, jj * 128:(jj + 1) * 128], self.pT[0:HD, :, :])
                kts = list(range(NT)) if m == 0 else [32, 33]
                for h in range(NH):
                    kvh = h // (NH // NKV)
                    pacc = po[h % 2]
                    nk_ = len(kts)
                    slots = []
                    for ki in range(nk_ + 1):
                        if ki < nk_:
                            kt = kts[ki]
                            ps_ = pss[pi % 2]
                            pe_ = pexp[pi % 3]
                            pi += 1
                            self.mm(ps_[:, :nq], KT[:, kvh, kt * 128:(kt + 1) * 128], QT[:, h, :nq], True, True)
                            self.act(pe_[:, :nq], ps_[:, :nq], AF.Exp, scale=ATTN_SCALE, bias=negm[:, 3:4])
                            slots.append((kt, pe_))
                        if ki >= 1:
                            kt0, pe0 = slots[ki - 1]
                            self.mm(pacc[:, :nq], V[:, kt0, kvh, :], pe0[:, :nq], ki - 1 == 0, ki - 1 == nk_ - 1)
                    self.recip(rd[HD:HD + 1, :nq], pacc[HD:HD + 1, :nq])
                    self.mm(pbc[:, :nq], self.ones_f[HD:HD + 1, 0:HD], rd[HD:HD + 1, :nq], True, True)
                    self.cp("act", bcs[:, :nq], pbc[:, :nq])
                    self.tt("dve", OT[:, h, :nq], pacc[0:HD, :nq], bcs[:, :nq], ALU.mult)
                for jj, tt_ in enumerate(grp):
                    for hf in range(2):
                        for h in range(NH):
                            self.mm(self.py[:, hf * 512:(hf + 1) * 512], OT[:, h, jj * 128:(jj + 1) * 128],
                                    Wo[:, h, hf * 512:(hf + 1) * 512], h == 0, h == NH - 1)
                    x2 = self.xt[1]
                    self.load(x2[:], src(tt_))
                    self.resid_tile(x2, dst(tt_))
        S.barrier()

    def wattn_phase(self, i, j, src, dst, with_ctx, qtiles=None):
        S = self.S
        I = self.I
        with ExitStack() as st:
            Wq = self.sb(st, "Wq", [128, 8, D], BF16)
            Wkv = self.sb(st, "Wkv", [128, 8, 2 * DKV], BF16)
            Wo = self.sb(st, "Wo", [128, 8, D], BF16)
            self.load_w_bf16(Wq, I["wattn_wq"][j], 8)
            self.load_w_bf16(Wo, I["wattn_wo"][j], 8)
            wk = I["wattn_wk"][j].rearrange("(k p) n -> p k n", p=128)
            wv = I["wattn_wv"][j].rearrange("(k p) n -> p k n", p=128)
            for kc in range(8):
                self.load(Wkv[:, kc, 0:DKV], wk[:, kc, :], q="pool", part=True)
                self.load(Wkv[:, kc, DKV:2 * DKV], wv[:, kc, :], q="pool", part=True)
            self.alloc_common(st, nx=2)
            self.alloc_rope(st)
            sinkb = self.sb(st, "sinkb", [128, NH], F32)
            self.bload(sinkb, I["wattn_sink"][j])
            mask = self.sb(st, "mask", [128, 384], F32)
            self.load(mask[:], I["wmask"][:, :])
            KT = self.sb(st, "KT", [HD, NKV, NTOK], BF16)
            V = self.sb(st, "V", [128, NT, NKV, HD], BF16)
            nT = self.sb(st, "nT", [128, 8, 128], BF16)
            qf = self.sb(st, "qf", [128, D], F32)
            qb = self.sb(st, "qb", [128, D], BF16)
            QT = self.sb(st, "QT", [HD, NH, 128], BF16)
            sc = self.sb(st, "sc", [128, 640], F32)
            P = self.sb(st, "P", [128, 640], BF16)
            PT = self.sb(st, "PT", [128, 5, 128], BF16)
            O = self.sb(st, "O", [128, D], BF16)
            OT = self.sb(st, "OT", [128, 8, 128], BF16)
            sm = self.sb(st, "sm", [128, 8], F32)
            pl = self.ps(st, "pl", [128, 512], F32)
            pc = self.ps(st, "pc", [128, 512], F32)
            pPT = self.ps(st, "pPT", [128, 8, 128], BF16)
            pov = self.ps(st, "pov", [128, 512], F32)
            x = self.xt[0]

            def prologue(tt_):
                self.load(x[:], src(tt_))
                self.norm_tile(x, self.nb[:])
                self.transpose_cols(nT, 0, self.nb)

            cur_m = None
            for tt_ in range(NT):
                m = 0 if tt_ < 32 else 1
                if m != cur_m:
                    self.set_mod(i, m, 4, 2, 3, 5, 3, 1.0)
                    cur_m = m
                prologue(tt_)
                pkv = self.py
                for kc in range(8):
                    self.mm(pkv[:, 0:512], nT[:, kc, :], Wkv[:, kc, :], kc == 0, kc == 7)
                if m == 0:
                    self.cp("act", qf[:, 0:DKV], pkv[:, 0:DKV])
                    self.rope(qb, qf, NKV, tt_)
                else:
                    self.cp("act", qb[:, 0:DKV], pkv[:, 0:DKV])
                for h in range(NKV):
                    self.tr(self.pT[0:HD, h, :], qb[:, h * HD:(h + 1) * HD], self.ident_b[:], inc=(h == NKV - 1))
                self.cp("act", KT[:, :, tt_ * 128:(tt_ + 1) * 128], self.pT[0:HD, 0:NKV, :])
                self.cp("act", V[:, tt_, :, :],
                        A(pkv.t[:, DKV:2 * DKV].rearrange("p (h k) -> p h k", h=NKV), pkv.k))
            qt = list(range(32)) + ([32, 33] if with_ctx else [])
            if qtiles is not None:
                qt = qtiles
            cur_m = 1
            for tt_ in qt:
                m = 0 if tt_ < 32 else 1
                if m != cur_m:
                    self.set_mod(i, m, 4, 2, 3, 5, 3, 1.0)
                    cur_m = m
                prologue(tt_)
                for hf in range(2):
                    for kc in range(8):
                        self.mm(self.py[:, hf * 512:(hf + 1) * 512], nT[:, kc, :], Wq[:, kc, hf * 512:(hf + 1) * 512],
                                kc == 0, kc == 7)
                if m == 0:
                    self.cp("act", qf[:], self.py[:])
                    self.rope(qb, qf, NH, tt_)
                else:
                    self.cp("act", qb[:], self.py[:])
                for hb in range(2):
                    for h8 in range(8):
                        h = hb * 8 + h8
                        self.tr(self.pT[0:HD, h8, :], qb[:, h * HD:(h + 1) * HD], self.ident_b[:], inc=(h8 == 7))
                    self.cp("act", QT[:, hb * 8:(hb + 1) * 8, :], self.pT[0:HD, :, :])
                if m == 0:
                    b0 = max(tt_ - 1, 0)
                    b1 = min(tt_ + 1, 31)
                    nloc = (b1 - b0 + 1) * 128
                    moff = 0 if tt_ > 0 else 128
                    ktiles = list(range(b0, b1 + 1)) + [32, 33]
                else:
                    nloc = 0
                    ktiles = [32, 33]
                nk = nloc + 256
                for h in range(NH):
                    kvh = h // (NH // NKV)
                    if nloc:
                        self.mm(pl[:, :nloc], QT[:, h, :], KT[:, kvh, b0 * 128:(b1 + 1) * 128], True, True)
                        self.stt(sc[:, :nloc], pl[:, :nloc], ATTN_SCALE, mask[:, moff:moff + nloc], ALU.mult, ALU.add)
                    self.mm(pc[:, 0:256], QT[:, h, :], KT[:, kvh, SEQ:NTOK], True, True)
                    self.act(sc[:, nloc:nk], pc[:, 0:256], AF.Copy, scale=ATTN_SCALE)
                    self.red(sm[:, 0:1], sc[:, :nk], ALU.max)
                    self.tt("dve", sm[:, 0:1], sm[:, 0:1], sinkb[:, h:h + 1], ALU.max)
                    self.ts("dve", sm[:, 1:2], sm[:, 0:1], -1.0, None, ALU.mult)
                    self.act(P[:, :nk], sc[:, :nk], AF.Exp, bias=sm[:, 1:2], accum=sm[:, 2:3])
                    self.act(sm[:, 3:4], sinkb[:, h:h + 1], AF.Exp, bias=sm[:, 1:2])
                    self.tt("dve", sm[:, 4:5], sm[:, 2:3], sm[:, 3:4], ALU.add)
                    self.recip(sm[:, 5:6], sm[:, 4:5])
                    nkt = nk // 128
                    for kt in range(nkt):
                        self.tr(pPT[:, kt, :], P[:, kt * 128:(kt + 1) * 128], self.ident_b[:], inc=(kt == nkt - 1))
                    self.cp("act", PT[:, 0:nkt, :], pPT[:, 0:nkt, :])
                    for kt in range(nkt):
                        self.mm(pov[:, 0:HD], PT[:, kt, :], V[:, ktiles[kt], kvh, :], kt == 0, kt == nkt - 1)
                    self.ts("dve", O[:, h * HD:(h + 1) * HD], pov[:, 0:HD], sm[:, 5:6], None, ALU.mult)
                self.transpose_cols(OT, 0, O)
                for hf in range(2):
                    for kc in range(8):
                        self.mm(self.py[:, hf * 512:(hf + 1) * 512], OT[:, kc, :], Wo[:, kc, hf * 512:(hf + 1) * 512],
                                kc == 0, kc == 7)
                x2 = self.xt[1]
                self.load(x2[:], src(tt_))
                self.resid_tile(x2, dst(tt_))
        S.barrier()

    def hm_tile(self, arr, pos0):
        return arr[:, pos0:pos0 + 128, :].rearrange("h t k -> t h k")

    @staticmethod
    def v3(a, H=NH):
        return A(a.ap.rearrange("p (h k) -> p h k", h=H), a.k)

    def rwkv_norm_pass(self, i, src):
        S = self.S
        with ExitStack() as st:
            self.alloc_common(st, nx=2)
            nf = [self.sb(st, "nf%d" % a, [128, D], F32) for a in range(2)]
            z = self.sb(st, "zrow", [1, D], F32)
            self.memset("pool", z[:], 0.0)
            for r in (0, SEQ + 1, NSROWS - 1):
                self.store(self.NS[r:r + 1, :], z[:])
            cur_m = None
            for tt_ in range(NT):
                m = 0 if tt_ < 32 else 1
                if m != cur_m:
                    self.set_mod(i, m, 4, 2, 3, 5, 3, 1.0)
                    cur_m = m
                x = self.xt[tt_ % 2]
                self.load(x[:], src(tt_))
                self.norm_tile(x, nf[tt_ % 2][:])
                r0 = nsrow_of_tile(tt_)
                self.store(self.NS[r0:r0 + 128, :], nf[tt_ % 2][:])
        S.barrier()

    def alloc_shift(self, st):
        self.ncur = self.sb(st, "ncur", [128, D], F32)
        self.nprev = self.sb(st, "nprev", [128, D], F32)
        self.nnext = self.sb(st, "nnext", [128, D], F32)
        self.nbb = self.sb(st, "nbb", [128, D], BF16)
        self.xxb = self.sb(st, "xxb", [128, D], BF16)
        self.nTx = self.sb(st, "nTx", [128, 16, 128], BF16)
        self.tmpa = self.sb(st, "tmpa", [128, D], F32)
        self.tmpb = self.sb(st, "tmpb", [128, D], F32)
        self.pT = self.ps(st, "pT", [128, 8, 128], BF16)
        self.hstat = self.sb(st, "hstat", [128, 16], F32)
        self.muT = self.sb(st, "muT", [128, 8, 6], F32)

    def load_mu(self, j):
        for m in range(6):
            for kc in range(8):
                self.load(self.muT[:, kc, m:m + 1],
                          self.I["rwkv_mu"][j, m, kc * 128:(kc + 1) * 128].rearrange("(p o) -> p o", o=1), part=True)

    def shift_tile(self, tt_):
        r0 = nsrow_of_tile(tt_)
        self.load(self.ncur[:], self.NS[r0:r0 + 128, :])
        self.load(self.nprev[:], self.NS[r0 - 1:r0 + 127, :])
        self.load(self.nnext[:], self.NS[r0 + 1:r0 + 129, :])
        self.tt("pool", self.tmpa[:], self.nprev[:], self.nnext[:], ALU.add)
        self.stt(self.xxb[:], self.tmpa[:], 0.5, self.ncur[:], ALU.mult, ALU.subtract)
        self.cp("act", self.nbb[:], self.ncur[:])
        for half, srcb in ((0, self.nbb), (1, self.xxb)):
            for kc in range(8):
                self.tr(self.pT[:, kc, :], srcb[:, kc * 128:(kc + 1) * 128], self.ident_b[:], inc=(kc == 7))
            self.cp("act", self.nTx[:, half * 8:(half + 1) * 8, :], self.pT[:, :, :])

    def load_mixed_w(self, dst, col0, ncol, src, m):
        sv = src.rearrange("(k p) n -> p k n", p=128)
        for kc in range(8):
            self.load(dst[:, kc, col0:col0 + ncol], sv[:, kc, :], q="pool", part=True)
            self.load(dst[:, 8 + kc, col0:col0 + ncol], sv[:, kc, :], q="pool", part=True)
        for kc in range(8):
            self.ts("dve" if kc % 2 else "pool", dst[:, 8 + kc, col0:col0 + ncol], dst[:, 8 + kc, col0:col0 + ncol],
                    self.muT[:, kc, m:m + 1], None, ALU.mult)

    def rwkv_feat1(self, j):
        S, I = self.S, self.I
        with ExitStack() as st:
            self.alloc_shift(st)
            self.load_mu(j)
            Wbig = self.sb(st, "Wbig", [128, 16, 3 * D], BF16)
            self.load_mixed_w(Wbig, 0, D, I["rwkv_wr"][j], 0)
            self.load_mixed_w(Wbig, D, D, I["rwkv_wk"][j], 2)
            self.load_mixed_w(Wbig, 2 * D, D, I["rwkv_wv"][j], 3)
            kk_b = self.sb(st, "kk_b", [128, D], F32)
            self.bload(kk_b, I["rwkv_k_k"][j])
            if j > 0:
                WLv = self.sb(st, "WLv", [128, 16, 32], BF16)
                self.load_mixed_w(WLv, 0, 32, I["rwkv_v1"][j - 1], 3)
                v2s = self.sb(st, "v2s", [32, D], BF16)
                self.load(v2s[:], I["rwkv_v2"][j - 1], q="pool", part=True)
                v0b = self.sb(st, "v0b", [128, D], F32)
                self.bload(v0b, I["rwkv_v0"][j - 1])
                hvb = self.sb(st, "hvb", [32, 128], BF16)
                ph = self.ps(st, "ph", [128, 512], F32)
            ob = [self.sb(st, "ob%d" % a, [128, D], F32) for a in range(4)]
            pys = [self.ps(st, "pya", [128, D], F32), self.ps(st, "pyb", [128, D], F32)]
            VAj = self.VA[j]
            for tt_ in range(NT):
                pos0 = pos_of_tile(tt_)
                self.shift_tile(tt_)
                for qi in range(3):
                    py = pys[qi % 2]
                    for hf in range(2):
                        for c in range(16):
                            self.mm(py[:, hf * 512:(hf + 1) * 512], self.nTx[:, c, :],
                                    Wbig[:, c, qi * D + hf * 512:qi * D + (hf + 1) * 512], c == 0, c == 15)
                    if qi == 0:
                        self.cp("act", ob[0][:], py[:])
                        self.store(self.hm_tile(self.RH, pos0), self.v3(ob[0][:]))
                    elif qi == 1:
                        kraw = ob[1]
                        self.cp("act", kraw[:], py[:])
                        self.store(self.KRAW[pos0:pos0 + 128, :], kraw[:])
                        self.tt("dve", self.tmpb[:], kraw[:], kk_b[:], ALU.mult)
                        self.tt("pool", self.tmpa[:], self.tmpb[:], self.tmpb[:], ALU.mult)
                        hs = self.hstat
                        self.red(hs[:, 0:16], self.v3(self.tmpa[:]))
                        self.tt("pool", hs[:, 0:16], hs[:, 0:16], self.chalf[:, 0:16], ALU.pow)
                        self.ts("dve", hs[:, 0:16], hs[:, 0:16], 1e-12, None, ALU.max)
                        self.recip(hs[:, 0:16], hs[:, 0:16])
                        self.ts("dve", hs[:, 0:16], hs[:, 0:16], -1.0, None, ALU.mult)
                        rb = A(hs.t[:, 0:16].unsqueeze(2).to_broadcast([128, NH, HD]), hs.k)
                        self.tt("dve", self.v3(ob[2][:]), self.v3(self.tmpb[:]), rb, ALU.mult)
                        self.store(self.hm_tile(self.AVH, pos0), self.v3(ob[2][:]))
                    else:
                        vf = ob[3]
                        self.cp("act", vf[:], py[:])
                        if j > 0:
                            for c in range(16):
                                self.mm(ph[0:32, 0:128], WLv[:, c, :], self.nTx[:, c, :], c == 0, c == 15)
                            self.cp("act", hvb[:], ph[0:32, 0:128])
                            py2 = pys[0]
                            for hf in range(2):
                                self.mm(py2[:, hf * 512:(hf + 1) * 512], hvb[:], v2s[:, hf * 512:(hf + 1) * 512], True, True)
                            self.tt("dve", self.tmpa[:], py2[:], v0b[:], ALU.add)
                            self.act(self.tmpa[:], self.tmpa[:], AF.Sigmoid)
                            self.load(self.tmpb[:], self.VA[0][pos0:pos0 + 128, :])
                            self.tt("pool", self.tmpb[:], self.tmpb[:], vf[:], ALU.subtract)
                            self.tt("dve", self.tmpb[:], self.tmpb[:], self.tmpa[:], ALU.mult)
                            self.tt("pool", vf[:], vf[:], self.tmpb[:], ALU.add)
                        self.store(VAj[pos0:pos0 + 128, :], vf[:])
        S.barrier()

    def rwkv_feat2(self, j):
        S, I = self.S, self.I
        with ExitStack() as st:
            self.alloc_shift(st)
            self.load_mu(j)
            WL1 = self.sb(st, "WL1", [128, 16, 576], BF16)
            for d in range(2):
                self.load_mixed_w(WL1, d * 64, 64, I["rwkv_w1"][j, d], 1)
                self.load_mixed_w(WL1, 128 + d * 64, 64, I["rwkv_a1"][j, d], 4)
                self.load_mixed_w(WL1, 256 + d * 160, 160, I["rwkv_g1"][j, d], 5)
            w2s = self.sb(st, "w2s", [128, D], BF16)
            a2s = self.sb(st, "a2s", [128, D], BF16)
            g2s = self.sb(st, "g2s", [128, 2, 2, D], BF16)
            self.load(w2s[:], I["rwkv_w2"][j].rearrange("d r n -> (d r) n"), q="pool", part=True)
            self.load(a2s[:], I["rwkv_a2"][j].rearrange("d r n -> (d r) n"), q="pool", part=True)
            for d in range(2):
                self.load(g2s[:, d, 0, :], I["rwkv_g2"][j, d, 0:128, :], q="pool", part=True)
                self.load(g2s[0:32, d, 1, :], I["rwkv_g2"][j, d, 128:160, :], q="pool", part=True)
            w0b = self.sb(st, "w0b", [128, 2, D], F32)
            a0b = self.sb(st, "a0b", [128, 2, D], F32)
            kab = self.sb(st, "kab", [128, D], F32)
            for d in range(2):
                self.load(w0b[:, d, :], I["rwkv_w0"][j, d].partition_broadcast(128), part=True)
                self.load(a0b[:, d, :], I["rwkv_a0"][j, d].partition_broadcast(128), part=True)
            self.bload(kab, I["rwkv_k_a"][j])
            hwT = self.sb(st, "hwT", [128, 128], BF16)
            haT = self.sb(st, "haT", [128, 128], BF16)
            hgT = [self.sb(st, "hgT%d" % d, [128, 2, 128], BF16) for d in range(2)]
            kraw = self.sb(st, "kraw", [128, D], F32)
            av = self.sb(st, "av", [128, D], F32)
            ob = [self.sb(st, "ob%d" % a, [128, D], F32) for a in range(6)]
            lwt = self.sb(st, "lwt", [128, D], F32)
            gG = self.sb(st, "gG", [128, D], F32)
            gI = self.sb(st, "gI", [128, D], F32)
            gX = self.sb(st, "gX", [128, D], F32)
            rt2 = self.sb(st, "rt2", [128, D], F32)
            tri = self.sb(st, "tri", [128, 2, 128], F32)
            self.load(tri[:], I["tri"].rearrange("d s t -> s d t"))
            pys = [self.ps(st, "pya", [128, D], F32), self.ps(st, "pyb", [128, D], F32)]
            phs = [self.ps(st, "ph%d" % a, [128, 512], F32) for a in range(2)]
            for tt_ in range(NT):
                pos0 = pos_of_tile(tt_)
                self.shift_tile(tt_)
                self.load(kraw[:], self.KRAW[pos0:pos0 + 128, :])
                self.load(self.v3(av[:]), self.hm_tile(self.AVH, pos0))
                self.load(self.v3(rt2[:]), self.hm_tile(self.RH, pos0))
                for c in range(16):
                    self.mm(phs[0][:, 0:128], WL1[:, c, 0:128], self.nTx[:, c, :], c == 0, c == 15)
                self.act(hwT[:], phs[0][:, 0:128], AF.Tanh)
                for c in range(16):
                    self.mm(phs[1][:, 0:128], WL1[:, c, 128:256], self.nTx[:, c, :], c == 0, c == 15)
                self.cp("act", haT[:], phs[1][:, 0:128])
                for d in range(2):
                    g0 = 256 + d * 160
                    for c in range(16):
                        self.mm(phs[0][:, 0:128], WL1[:, c, g0:g0 + 128], self.nTx[:, c, :], c == 0, c == 15)
                    self.act(hgT[d][:, 0, :], phs[0][:, 0:128], AF.Sigmoid)
                    for c in range(16):
                        self.mm(phs[1][0:32, 0:128], WL1[:, c, g0 + 128:g0 + 160], self.nTx[:, c, :], c == 0, c == 15)
                    self.act(hgT[d][0:32, 1, :], phs[1][0:32, 0:128], AF.Sigmoid)
                for d in range(2):
                    ps_ = slice(d * 64, (d + 1) * 64)
                    py = pys[0]
                    for hf in range(2):
                        self.mm(py[:, hf * 512:(hf + 1) * 512], hwT[ps_, :], w2s[ps_, hf * 512:(hf + 1) * 512], True, True)
                    self.tt("dve", self.tmpa[:], py[:], w0b[:, d, :], ALU.add)
                    self.act(self.tmpa[:], self.tmpa[:], AF.Sigmoid)
                    self.ts("dve", lwt[:], self.tmpa[:], -0.6065306597126334, None, ALU.mult)
                    for hf in range(2):
                        cs = slice(hf * 512, (hf + 1) * 512)
                        self.mm(py[:, cs], tri[:, d, :], lwt[:, cs], True, True)
                    self.act(gG[:], py[:], AF.Exp)
                    self.act(gI[:], py[:], AF.Exp, scale=-1.0)
                    self.tt("dve", self.tmpa[:], py[:], lwt[:], ALU.subtract)
                    self.act(gX[:], self.tmpa[:], AF.Exp)
                    self.store(self.hm_tile(self.WH[d], pos0), self.v3(gG[:]))
                    self.tt("pool", ob[0][:], av[:], gX[:], ALU.mult)
                    self.store(self.hm_tile(self.AH[d], pos0), self.v3(ob[0][:]))
                    self.tt("dve", ob[4][:], rt2[:], gG[:], ALU.mult)
                    self.store(self.hm_tile(self.RS[d], pos0), self.v3(ob[4][:]))
                    py = pys[1]
                    for hf in range(2):
                        self.mm(py[:, hf * 512:(hf + 1) * 512], haT[ps_, :], a2s[ps_, hf * 512:(hf + 1) * 512], True, True)
                    self.tt("dve", self.tmpb[:], py[:], a0b[:, d, :], ALU.add)
                    self.act(self.tmpb[:], self.tmpb[:], AF.Sigmoid)
                    self.stt(ob[1][:], av[:], -1.0, self.tmpb[:], ALU.mult, ALU.mult)
                    self.tt("pool", ob[1][:], ob[1][:], gI[:], ALU.mult)
                    self.store(self.hm_tile(self.BH[d], pos0), self.v3(ob[1][:]))
                    self.stt(self.tmpb[:], self.tmpb[:], -1.0, kab[:], ALU.add, ALU.mult)
                    self.tt("pool", self.tmpb[:], self.tmpb[:], kraw[:], ALU.mult)
                    self.tt("dve", ob[2][:], self.tmpb[:], kraw[:], ALU.add)
                    self.store(self.hm_tile(self.KH[d], pos0), self.v3(ob[2][:]))
                    self.tt("pool", ob[5][:], ob[2][:], gI[:], ALU.mult)
                    self.store(self.hm_tile(self.KS[d], pos0), self.v3(ob[5][:]))
                    py = pys[0]
                    for hf in range(2):
                        cs = slice(hf * 512, (hf + 1) * 512)
                        self.mm(py[:, cs], hgT[d][:, 0, :], g2s[:, d, 0, cs], True, False)
                        self.mm(py[:, cs], hgT[d][0:32, 1, :], g2s[0:32, d, 1, cs], False, True)
                    self.cp("act", ob[3][:], py[:])
                    self.store(self.GT[d][pos0:pos0 + 128, :], ob[3][:])
        S.barrier()

    def rwkv_scan(self, j, nblocks=None):
        S = self.S
        TB = 16
        with ExitStack() as st:
            St = self.sb(st, "St", [128, 2, 8, 64], F32)
            t1 = self.sb(st, "sc_t1", [128, 2, 8, 64], F32)
            t2 = self.sb(st, "sc_t2", [128, 2, 8, 64], F32)
            kv = [self.sb(st, "sc_kv%d" % a, [128, 2, 8, 64], F32) for a in range(2)]
            sa = self.sb(st, "sc_sa", [128, 16], F32)
            qs = ("b", "k", "a", "r")
            bufs = [{q: self.sb(st, "sc_%s%d" % (q, bi), [128, 2, TB, 64], F32) for q in qs} for bi in range(2)]
            gend = [self.sb(st, "sc_ge%d" % bi, [128, 2, 64], F32) for bi in range(2)]
            vbuf = [self.sb(st, "sc_v%d" % bi, [128, 2, TB, 8], F32) for bi in range(2)]
            ybuf = [self.sb(st, "sc_y%d" % bi, [128, 2, TB, 8], F32) for bi in range(2)]
            self.memset("dve", St[:], 0.0)
            VAj = self.VA[j]
            NB = NPOS // TB if nblocks is None else nblocks

            def lo_of(b):
                return (TB * b, (240 - TB * b) if b < 16 else (4592 - TB * b))

            def loads(b):
                bi = b % 2
                for d, lo in enumerate(lo_of(b)):
                    arrs = {"b": self.BH[d], "k": self.KS[d], "a": self.AH[d], "r": self.RS[d]}
                    pe_ = lo + TB - 1 if d == 0 else lo
                    gsrc = bass.AP(tensor=self.WH[d].tensor, offset=self.WH[d].offset + pe_ * 64,
                                   ap=[[NPOS * 64, 16], [0, 8], [1, 64]])
                    self.load(gend[bi][:, d, :], gsrc, part=True)
                    for q in qs:
                        arr = arrs[q]
                        src = bass.AP(tensor=arr.tensor, offset=arr.offset + lo * 64,
                                      ap=[[NPOS * 64, 16], [0, 8], [1, TB * 64]])
                        dstb = bufs[bi][q]
                        self.load(A(dstb.t[:, d, :, :].rearrange("p t k -> p (t k)"), dstb.k), src, part=True)
                    self.load(vbuf[bi][:, d, :, :], VAj[lo:lo + TB, :].rearrange("t (p l) -> p t l", l=8), part=True)

            loads(0)
            for b in range(NB):
                bi = b % 2
                if b + 1 < NB:
                    loads(b + 1)
                for s in range(TB):
                    c0, c1 = s, TB - 1 - s

                    def opnd(q):
                        return bufs[bi][q].cust(c0 * 64, [[2 * TB * 64, 128], [TB * 64 + (c1 - c0) * 64, 2], [0, 8], [1, 64]])
                    vb = vbuf[bi].cust(c0 * 8, [[2 * TB * 8, 128], [TB * 8 + (c1 - c0) * 8, 2], [1, 8], [0, 64]])
                    yo = ybuf[bi].cust(c0 * 8, [[2 * TB * 8, 128], [TB * 8 + (c1 - c0) * 8, 2], [1, 8]])
                    kvt = kv[s % 2]
                    for d_, c_ in enumerate((c0, c1)):
                        for vl in range(8):
                            self.act(kvt[:, d_, vl, :], bufs[bi]["k"][:, d_, c_, :], AF.Copy,
                                     scale=vbuf[bi][:, d_, c_, vl:vl + 1],
                                     inc=(d_ == 1 and vl == 7), skip_self=True)
                    self.tt("dve", t1[:], St[:], opnd("a"), ALU.mult)
                    self.red(A(sa.t[:, :].rearrange("p (d l) -> p d l", d=2), sa.k), t1[:])
                    sab = A(sa.t[:, :].rearrange("p (d l) -> p d l", d=2).unsqueeze(3).to_broadcast([128, 2, 8, 64]), sa.k)
                    self.tt("dve", t2[:], sab, opnd("b"), ALU.mult)
                    self.tt("dve", St[:], St[:], t2[:], ALU.add)
                    self.tt("dve", St[:], St[:], kvt[:], ALU.add)
                    self.tt("dve", t1[:], St[:], opnd("r"), ALU.mult)
                    self.red(yo, t1[:])
                geb = A(gend[bi].t[:, :, :].unsqueeze(2).to_broadcast([128, 2, 8, 64]), gend[bi].k)
                self.tt("dve", St[:], St[:], geb, ALU.mult)
                for d, lo in enumerate(lo_of(b)):
                    self.store(self.YT[d][lo:lo + TB, :].rearrange("t (p l) -> p t l", l=8), ybuf[bi][:, d, :, :])
        S.barrier()

    def rwkv_readout(self, i, j, src, dst, tiles):
        S, I = self.S, self.I
        with ExitStack() as st:
            self.alloc_common(st, nx=2)
            self.hstat = self.sb(st, "hstat", [128, 32], F32)
            Wo = self.sb(st, "Wo", [128, 8, D], BF16)
            self.load_w_bf16(Wo, I["rwkv_wo"][j], 8)
            lnw = self.sb(st, "lnw", [128, 2, D], F32)
            lnb = self.sb(st, "lnb", [128, 2, D], F32)
            rkb = self.sb(st, "rkb", [128, D], F32)
            for d in range(2):
                self.load(lnw[:, d, :], I["rwkv_ln_w"][j, d].partition_broadcast(128), part=True)
                self.load(lnb[:, d, :], I["rwkv_ln_b"][j, d].partition_broadcast(128), part=True)
            self.bload(rkb, I["rwkv_r_k"][j].rearrange("h k -> (h k)"))
            rt = self.sb(st, "rt", [128, D], F32)
            vt = self.sb(st, "vt", [128, D], F32)
            yt = self.sb(st, "yt", [128, D], F32)
            kt_ = self.sb(st, "kt", [128, D], F32)
            gt = self.sb(st, "gt", [128, D], F32)
            oacc = self.sb(st, "oacc", [128, D], F32)
            obf = self.sb(st, "obf", [128, D], BF16)
            OT = self.sb(st, "OT", [128, 8, 128], BF16)
            hs = self.hstat
            VAj = self.VA[j]
            cur_m = None
            for tt_ in tiles:
                m = 0 if tt_ < 32 else 1
                if m != cur_m:
                    self.set_mod(i, m, 4, 2, 3, 5, 3, 1.0)
                    cur_m = m
                pos0 = pos_of_tile(tt_)
                self.load(self.v3(rt[:]), self.hm_tile(self.RH, pos0))
                self.load(vt[:], VAj[pos0:pos0 + 128, :])
                for d in range(2):
                    self.load(yt[:], self.YT[d][pos0:pos0 + 128, :])
                    self.load(self.v3(kt_[:]), self.hm_tile(self.KH[d], pos0))
                    self.load(gt[:], self.GT[d][pos0:pos0 + 128, :])
                    ta, tb = self.tmpa, self.tmpb
                    self.red(hs[:, 0:16], self.v3(yt[:]))
                    self.ts("dve", hs[:, 0:16], hs[:, 0:16], 1.0 / HD, None, ALU.mult)
                    mb_ = A(hs.t[:, 0:16].unsqueeze(2).to_broadcast([128, NH, HD]), hs.k)
                    self.tt("dve", self.v3(ta[:]), self.v3(yt[:]), mb_, ALU.subtract)
                    self.tt("pool", tb[:], ta[:], ta[:], ALU.mult)
                    self.red(hs[:, 16:32], self.v3(tb[:]))
                    self.ts("dve", hs[:, 16:32], hs[:, 16:32], 1.0 / HD, GN_EPS, ALU.mult, ALU.add)
                    self.tt("pool", hs[:, 16:32], hs[:, 16:32], self.nhalf[:, 0:16], ALU.pow)
                    rb_ = A(hs.t[:, 16:32].unsqueeze(2).to_broadcast([128, NH, HD]), hs.k)
                    self.tt("dve", self.v3(ta[:]), self.v3(ta[:]), rb_, ALU.mult)
                    self.tt("pool", ta[:], ta[:], lnw[:, d, :], ALU.mult)
                    self.tt("dve", ta[:], ta[:], lnb[:, d, :], ALU.add)
                    self.tt("pool", tb[:], rt[:], kt_[:], ALU.mult)
                    self.tt("dve", tb[:], tb[:], rkb[:], ALU.mult)
                    self.red(hs[:, 0:16], self.v3(tb[:]))
                    bb_ = A(hs.t[:, 0:16].unsqueeze(2).to_broadcast([128, NH, HD]), hs.k)
                    self.tt("dve", self.v3(tb[:]), self.v3(vt[:]), bb_, ALU.mult)
                    self.tt("pool", ta[:], ta[:], tb[:], ALU.add)
                    if d == 0:
                        self.tt("dve", oacc[:], ta[:], gt[:], ALU.mult)
                    else:
                        self.tt("dve", ta[:], ta[:], gt[:], ALU.mult)
                        self.tt("pool", obf[:], ta[:], oacc[:], ALU.add)
                self.transpose_cols(OT, 0, obf)
                for hf in range(2):
                    for kc in range(8):
                        self.mm(self.py[:, hf * 512:(hf + 1) * 512], OT[:, kc, :], Wo[:, kc, hf * 512:(hf + 1) * 512],
                                kc == 0, kc == 7)
                x = self.xt[0]
                self.load(x[:], src(tt_))
                self.resid_tile(x, dst(tt_))
        S.barrier()

    def build(self):
        nc = self.nc
        cfg = self.cfg
        shapes = dict(
            x=[SEQ, D], ctx=[CTXL, D], c=[D], c_ctx=[D], ident=[128, 128], wmask=[128, 384], tri=[2, 128, 128],
            rope_cos=[SEQ, 32], rope_sin=[SEQ, 32],
            mod_w=[DEPTH, D, NMOD * D], mod_b=[DEPTH, NMOD * D], norm_g=[DEPTH, 6, D],
            ffn_w1=[DEPTH, 2, D, DFF], ffn_w3=[DEPTH, 2, D, DFF], ffn_w2=[DEPTH, 2, DFF, D],
            rwkv_mu=[2, 6, D], rwkv_wr=[2, D, D], rwkv_wk=[2, D, D], rwkv_wv=[2, D, D], rwkv_wo=[2, D, D],
            rwkv_k_k=[2, D], rwkv_k_a=[2, D], rwkv_r_k=[2, 16, 64], rwkv_w0=[2, 2, D], rwkv_w1=[2, 2, D, 64],
            rwkv_w2=[2, 2, 64, D], rwkv_a0=[2, 2, D], rwkv_a1=[2, 2, D, 64], rwkv_a2=[2, 2, 64, D],
            rwkv_g1=[2, 2, D, 160], rwkv_g2=[2, 2, 160, D], rwkv_ln_w=[2, 2, D], rwkv_ln_b=[2, 2, D],
            rwkv_v0=[1, D], rwkv_v1=[1, D, 32], rwkv_v2=[1, 32, D],
            gattn_wq=[1, D, D], gattn_wk=[1, D, DKV], gattn_wv=[1, D, DKV], gattn_wo=[1, D, D],
            gattn_q_norm=[1, HD], gattn_k_norm=[1, HD],
            wattn_wq=[1, D, D], wattn_wk=[1, D, DKV], wattn_wv=[1, D, DKV], wattn_wo=[1, D, D], wattn_sink=[1, NH],
        )
        for k, shp in shapes.items():
            self.dram_in(k, shp)
        self.Y = nc.dram_tensor("y", [SEQ, D], F32, kind="ExternalOutput").ap()
        self.XS = self.scratch("xs", [NTOK, D])
        self.MOD = self.scratch("modv", [DEPTH, 2, NMOD * D])
        self.NS = self.scratch("ns", [NSROWS, D])
        self.RH = self.scratch("rh", [NH, NPOS, HD])
        self.AVH = self.scratch("avh", [NH, NPOS, HD])
        self.WH = [self.scratch("wh%d" % d, [NH, NPOS, HD]) for d in range(2)]
        self.BH = [self.scratch("bh%d" % d, [NH, NPOS, HD]) for d in range(2)]
        self.KH = [self.scratch("kh%d" % d, [NH, NPOS, HD]) for d in range(2)]
        self.KS = [self.scratch("ks%d" % d, [NH, NPOS, HD]) for d in range(2)]
        self.AH = [self.scratch("ah%d" % d, [NH, NPOS, HD]) for d in range(2)]
        self.RS = [self.scratch("rs%d" % d, [NH, NPOS, HD]) for d in range(2)]
        self.KRAW = self.scratch("kraw_d", [NPOS, D])
        self.VA = [self.scratch("va%d" % a, [NPOS, D]) for a in range(2)]
        self.GT = [self.scratch("gt%d" % d, [NPOS, D]) for d in range(2)]
        self.YT = [self.scratch("yt%d" % d, [NPOS, D]) for d in range(2)]
        dbg = cfg.get("dbg", [])
        self.DBG = {n: nc.dram_tensor("dbg_" + n, [NTOK, D], F32, kind="ExternalOutput").ap() for n in dbg}

        self.setup_consts()
        self.mod_phase()

        def src0(tt_):
            if tt_ < 32:
                return self.I["x"][tt_ * 128:(tt_ + 1) * 128, :]
            return self.I["ctx"][(tt_ - 32) * 128:(tt_ - 31) * 128, :]

        def xs(tt_):
            return self.XS[tt_ * 128:(tt_ + 1) * 128, :]

        def mk(name):
            d = self.DBG[name]
            return lambda tt_: d[tt_ * 128:(tt_ + 1) * 128, :]

        def yout(tt_):
            return self.Y[tt_ * 128:(tt_ + 1) * 128, :]

        l0, l1 = cfg.get("l0", 0), cfg.get("l1", DEPTH)
        ft = cfg.get("ffn_tiles")
        for i in range(l0, l1):
            last = i == DEPTH - 1
            kind, j = i % 3, i // 3
            alltiles = list(range(NT))
            s_in = src0 if i == l0 else xs
            if not cfg.get("skip_f1"):
                self.ffn_phase(i, 0, s_in, xs, alltiles if ft is None else ft)
                s_in = xs
            if cfg.get("stop") == "f1":
                break
            mt = list(range(32)) if last else alltiles
            if kind == 0:
                self.rwkv_norm_pass(i, s_in)
                self.rwkv_feat1(j)
                self.rwkv_feat2(j)
                self.rwkv_scan(j, cfg.get("nblocks"))
                self.rwkv_readout(i, j, s_in, xs, mt if cfg.get("mix_tiles") is None else cfg["mix_tiles"])
            elif kind == 1:
                self.gattn_phase(i, j, s_in, xs, not last, cfg.get("qgroups"))
            else:
                self.wattn_phase(i, j, s_in, xs, not last, cfg.get("qtiles"))
            if cfg.get("stop") == "mix":
                break
            self.ffn_phase(i, 1, xs, yout if last else xs, mt if ft is None else ft)
        if dbg:
            with ExitStack() as st:
                buf = self.sb(st, "dbgbuf", [128, D], F32)
                for n in dbg:
                    srcarr = {"xs": self.XS}.get(n)
                    if srcarr is None:
                        continue
                    for tt_ in cfg.get("dbg_tiles", range(NT)):
                        self.load(buf[:], srcarr[tt_ * 128:(tt_ + 1) * 128, :])
                        self.store(self.DBG[n][tt_ * 128:(tt_ + 1) * 128, :], buf[:])
            self.S.barrier()
        self.S.barrier()
        self.gs.close()
        return nc


def build_nc(cfg):
    nc = bass.Bass("TRN2", target_bir_lowering=False)
    kb = KB(nc, cfg)
    kb.build()
    print("instructions:", kb.S.nins, "waits:", kb.S.nwaits, "dma sems:", len(kb.S.dsem))
    return nc, list(kb.I.keys())


def host_consts():
    ident = np.eye(128, dtype=np.float32)
    pos = np.arange(SEQ)
    row = (pos // 64).astype(np.float32)
    col = (pos % 64).astype(np.float32)
    inv_freq = (np.float32(10000.0) ** (-np.arange(16, dtype=np.float32) / np.float32(16))).astype(np.float32)
    ang = np.stack([row, col], axis=-1)[:, :, None] * inv_freq
    rc = np.cos(ang).astype(np.float32).reshape(SEQ, 32)
    rs = np.sin(ang).astype(np.float32).reshape(SEQ, 32)
    qq = np.arange(128)[:, None]
    kk = np.arange(128)[None, :]
    maskL = np.where(kk >= qq, 0.0, MASKV).astype(np.float32)
    maskR = np.where(kk <= qq, 0.0, MASKV).astype(np.float32)
    wmask = np.concatenate([maskL, np.zeros((128, 128), np.float32), maskR], axis=1)
    si = np.arange(128)[:, None]
    ti = np.arange(128)[None, :]
    same = (si // 16) == (ti // 16)
    tri = np.stack([(same & (si <= ti)), (same & (si >= ti))]).astype(np.float32)
    return dict(ident=ident, rope_cos=rc, rope_sin=rs, wmask=wmask, tri=tri)


def make_in_maps(inputs, names, ncores=8):
    consts = host_consts()
    maps = []
    for b in range(ncores):
        m = {}
        for k, v in inputs.items():
            if k not in names:
                continue
            v = np.ascontiguousarray(v, dtype=np.float32)
            if k in ("x", "c", "ctx"):
                m[k] = np.ascontiguousarray(v[b])
            else:
                m[k] = v
        for k, v in consts.items():
            if k in names:
                m[k] = v
        maps.append(m)
    return maps


def kernel(**inputs):
    nc, names = build_nc({})
    maps = make_in_maps(inputs, names, 8)
    res = run_bass_kernel_spmd(nc, maps, core_ids=list(range(8)))
    return np.stack([r["y"] for r in res.results], axis=0).astype(np.float32)
```
